# Optimizing a Trainium2 kernel written in Bass

```python
import jax, jax.numpy as jnp
from jax import lax
import numpy as np

D_MODEL = 1024
BATCH = 16
SEQ = 2048
DEPTH = 4
DEC_BATCH = 16
DEC_SEQ = 32
PAST_LEN = 2048

CHUNK = 64
Q_BLOCK = 128
PE_DIM = 256
N_MIXERS = 3
N_RET = (DEPTH + 2) // 3
N_MLA = (DEPTH + 1) // 3
N_FOX = DEPTH // 3
EPS = 1e-6
ROPE_THETA = 10000.0
RET_HEADS = 4
RET_DK = 256
RET_DV = 512
RET_QK = RET_HEADS * RET_DK
RET_VW = RET_HEADS * RET_DV
MLA_HEADS = 16
MLA_NOPE = 128
MLA_ROPE = 64
MLA_V = 128
MLA_Q_LORA = 512
MLA_KV_LORA = 256
FOX_HEADS = 16
FOX_DH = 64
FOX_W = FOX_HEADS * FOX_DH
D_FF = ((8 * D_MODEL + 3 * 256 - 1) // (3 * 256)) * 256

kernel_name = "hybrid_retention_mla_fox_streaming_step"


def rms_norm(x, g):
    xf = x.astype(jnp.float32)
    y = xf * lax.rsqrt(jnp.mean(xf * xf, axis=-1, keepdims=True) + EPS)
    return (y * g.astype(jnp.float32)).astype(x.dtype)


def rope(x, pos):
    half = x.shape[-1] // 2
    inv = ROPE_THETA ** (-jnp.arange(half, dtype=jnp.float32) / half)
    ang = pos.astype(jnp.float32)[:, None] * inv[None, :]
    cos = jnp.cos(ang)[None, :, None, :]
    sin = jnp.sin(ang)[None, :, None, :]
    xf = x.astype(jnp.float32)
    x1, x2 = xf[..., :half], xf[..., half:]
    return jnp.concatenate([x1 * cos - x2 * sin, x2 * cos + x1 * sin], axis=-1).astype(x.dtype)


def block_attention(q, k, v, chunk_causal, q_bias=None, k_bias=None):
    b, tq, h, dk = q.shape
    tk = k.shape[1]
    off = tk - tq
    blk = min(Q_BLOCK, tq)
    scale = dk ** -0.5
    outs = []
    for q0 in range(0, tq, blk):
        last = off + q0 + blk
        kend = min(tk, -(-last // CHUNK) * CHUNK) if chunk_causal else last
        qp = off + q0 + jnp.arange(blk)
        kp = jnp.arange(kend)
        s = jnp.einsum('bqhd,bkhd->bhqk', q[:, q0:q0 + blk], k[:, :kend]).astype(jnp.float32) * scale
        if q_bias is not None:
            s = s + q_bias[:, :, q0:q0 + blk, None] - k_bias[:, :, None, :kend]
        if chunk_causal:
            allowed = (kp[None, :] // CHUNK) <= (qp[:, None] // CHUNK)
        else:
            allowed = kp[None, :] <= qp[:, None]
        s = jnp.where(allowed[None, None], s, -jnp.inf)
        pr = jax.nn.softmax(s, axis=-1).astype(v.dtype)
        outs.append(jnp.einsum('bhqk,bkhd->bqhd', pr, v[:, :kend]))
    return jnp.concatenate(outs, axis=1)


def retention_scan(q, k, v, s0):
    b, t, h, dk = q.shape
    dv = v.shape[-1]
    L = min(CHUNK, t)
    n = t // L
    lg = jnp.log1p(-jnp.exp2(-5.0 - jnp.arange(h, dtype=jnp.float32)))
    idx = jnp.arange(L, dtype=jnp.float32)
    diff = idx[:, None] - idx[None, :]
    dmask = jnp.where(diff >= 0, jnp.exp(lg[:, None, None] * jnp.maximum(diff, 0.0)), 0.0)
    q_dec = jnp.exp(lg[:, None] * (idx + 1.0))[:, :, None]
    k_dec = jnp.exp(lg[:, None] * (L - 1.0 - idx))[:, :, None]
    c_dec = jnp.exp(lg * L)[:, None, None]

    def to_chunks(a):
        return a.astype(jnp.float32).reshape(b, n, L, h, a.shape[-1]).transpose(1, 0, 3, 2, 4)

    def step(s, inp):
        qc, kc, vc = inp
        att = jnp.einsum('bhld,bhmd->bhlm', qc, kc) * dmask
        o = jnp.einsum('bhlm,bhmv->bhlv', att, vc) + jnp.einsum('bhld,bhdv->bhlv', qc * q_dec, s)
        s = s * c_dec + jnp.einsum('bhmd,bhmv->bhdv', kc * k_dec, vc)
        return s, o

    s, o = lax.scan(step, s0.astype(jnp.float32), (to_chunks(q), to_chunks(k), to_chunks(v)))
    o = o.transpose(1, 0, 3, 2, 4).reshape(b, t, h, dv)
    return o, s


def head_group_norm(o, g):
    b, t, h, d = o.shape
    mu = jnp.mean(o, axis=-1, keepdims=True)
    var = jnp.mean(jnp.square(o - mu), axis=-1, keepdims=True)
    y = (o - mu) * lax.rsqrt(var + EPS)
    return y.reshape(b, t, h * d) * g.astype(jnp.float32)


def retention_mixer(xn, pos, s0, w_in, gn, w_out):
    b, t, _ = xn.shape
    proj = xn @ w_in
    q, k, v, g = jnp.split(proj, [RET_QK, 2 * RET_QK, 2 * RET_QK + RET_VW], axis=-1)
    q = rope(q.reshape(b, t, RET_HEADS, RET_DK), pos)
    k = rope(k.reshape(b, t, RET_HEADS, RET_DK), pos) * (RET_DK ** -0.5)
    v = v.reshape(b, t, RET_HEADS, RET_DV)
    o, s = retention_scan(q, k, v, s0)
    y = head_group_norm(o, gn).astype(xn.dtype)
    return (jax.nn.silu(g) * y) @ w_out, s.astype(xn.dtype)


def mla_mixer(xn, pos, lat_past, kr_past, w_in, q_norm, kv_norm, w_qb, w_kvb, gq_nope, gq_rope, gk_nope, gk_rope, w_out):
    b, t, _ = xn.shape
    proj = xn @ w_in
    cq, ckv, kr = jnp.split(proj, [MLA_Q_LORA, MLA_Q_LORA + MLA_KV_LORA], axis=-1)
    q = (rms_norm(cq, q_norm) @ w_qb).reshape(b, t, MLA_HEADS, MLA_NOPE + MLA_ROPE)
    q_nope = rms_norm(q[..., :MLA_NOPE], gq_nope)
    q_rope = rope(rms_norm(q[..., MLA_NOPE:], gq_rope), pos)
    lat = rms_norm(ckv, kv_norm)
    kr = rope(rms_norm(kr, gk_rope)[:, :, None, :], pos)[:, :, 0, :]
    lat_all = jnp.concatenate([lat_past, lat], axis=1)
    kr_all = jnp.concatenate([kr_past, kr], axis=1)
    tk = lat_all.shape[1]
    kv = (lat_all @ w_kvb).reshape(b, tk, MLA_HEADS, MLA_NOPE + MLA_V)
    k_nope = rms_norm(kv[..., :MLA_NOPE], gk_nope)
    v = kv[..., MLA_NOPE:]
    qf = jnp.concatenate([q_nope, q_rope], axis=-1)
    kf = jnp.concatenate([k_nope, jnp.broadcast_to(kr_all[:, :, None, :], (b, tk, MLA_HEADS, MLA_ROPE))], axis=-1)
    o = block_attention(qf, kf, v, chunk_causal=True)
    return o.reshape(b, t, MLA_HEADS * MLA_V) @ w_out, lat, kr


def fox_mixer(xn, k_past, v_past, lf_past, w_in, b_f, gq, gk, w_out):
    b, t, _ = xn.shape
    proj = xn @ w_in
    q, k, v, g, fl = jnp.split(proj, [FOX_W, 2 * FOX_W, 3 * FOX_W, 4 * FOX_W], axis=-1)
    q = rms_norm(q.reshape(b, t, FOX_HEADS, FOX_DH), gq)
    k = rms_norm(k.reshape(b, t, FOX_HEADS, FOX_DH), gk)
    v = v.reshape(b, t, FOX_HEADS, FOX_DH)
    lf = jax.nn.log_sigmoid((fl + b_f).astype(jnp.float32))
    k_all = jnp.concatenate([k_past, k], axis=1)
    v_all = jnp.concatenate([v_past, v], axis=1)
    lf_all = jnp.concatenate([lf_past.astype(jnp.float32), lf], axis=1)
    F = jnp.cumsum(lf_all, axis=1).transpose(0, 2, 1)
    o = block_attention(q, k_all, v_all, chunk_causal=False, q_bias=F[:, :, F.shape[2] - t:], k_bias=F)
    return (jax.nn.sigmoid(g) * o.reshape(b, t, FOX_W)) @ w_out, k, v, lf.astype(xn.dtype)


def swiglu(xn, wg, wu, wd):
    return (jax.nn.silu(xn @ wg) * (xn @ wu)) @ wd


def trunk(x, p, past_len, ret_states, mla_lat_past, mla_kr_past, fox_k_past, fox_v_past, fox_lf_past, w):
    t = x.shape[1]
    pos = past_len + jnp.arange(t)
    h = x
    new_ret, new_lat, new_kr, new_fk, new_fv, new_flf = [], [], [], [], [], []
    for i in range(DEPTH):
        kind, j = i % N_MIXERS, i // N_MIXERS
        xn = rms_norm(h, w['norm_mix'][i])
        if kind == 0:
            y, s = retention_mixer(xn, pos, ret_states[j], w['ret_w_in'][j], w['ret_gn'][j], w['ret_w_out'][j])
            new_ret.append(s)
        elif kind == 1:
            y, lat, kr = mla_mixer(xn, pos, mla_lat_past[j], mla_kr_past[j], w['mla_w_in'][j], w['mla_q_norm'][j],
                                   w['mla_kv_norm'][j], w['mla_w_qb'][j], w['mla_w_kvb'][j], w['mla_gq_nope'][j],
                                   w['mla_gq_rope'][j], w['mla_gk_nope'][j], w['mla_gk_rope'][j], w['mla_w_out'][j])
            new_lat.append(lat)
            new_kr.append(kr)
        else:
            y, fk, fv, flf = fox_mixer(xn, fox_k_past[j], fox_v_past[j], fox_lf_past[j], w['fox_w_in'][j],
                                       w['fox_b_f'][j], w['fox_gq'][j], w['fox_gk'][j], w['fox_w_out'][j])
            new_fk.append(fk)
            new_fv.append(fv)
            new_flf.append(flf)
        h = h + y.astype(h.dtype)
        h = h + swiglu(rms_norm(h, w['norm_ffn'][i]), w['ffn_w_gate'][i], w['ffn_w_up'][i], w['ffn_w_down'][i])
        gate = jax.nn.sigmoid(rms_norm(h, w['norm_pe'][i]) @ w['pe_w_gate'][i])
        h = h + gate * (p[i] @ w['pe_w_proj'][i])
    return (rms_norm(h, w['norm_final']), jnp.stack(new_ret), jnp.stack(new_lat), jnp.stack(new_kr),
            jnp.stack(new_fk), jnp.stack(new_fv), jnp.stack(new_flf))


def setup_inputs(seed: int = 0) -> dict:
    key = jax.random.key(seed)
    ks = iter(jax.random.split(key, 64))

    def nrm(shape, scale=1.0):
        return scale * jax.random.normal(next(ks), shape, jnp.float32)

    def gain(shape):
        return 1.0 + nrm(shape, 0.05)

    def lin(shape):
        return nrm(shape, shape[-2] ** -0.5)

    return {
        'x_prompt': nrm((BATCH, SEQ, D_MODEL)),
        'x_sample': nrm((DEC_BATCH, DEC_SEQ, D_MODEL)),
        'state_ret': nrm((N_RET, DEC_BATCH, RET_HEADS, RET_DK, RET_DV), 0.5),
        'cache_mla_latent': nrm((N_MLA, DEC_BATCH, PAST_LEN, MLA_KV_LORA)),
        'cache_mla_krope': nrm((N_MLA, DEC_BATCH, PAST_LEN, MLA_ROPE)),
        'cache_fox_k': nrm((N_FOX, DEC_BATCH, PAST_LEN, FOX_HEADS, FOX_DH)),
        'cache_fox_v': nrm((N_FOX, DEC_BATCH, PAST_LEN, FOX_HEADS, FOX_DH)),
        'cache_fox_logf': jax.nn.log_sigmoid(nrm((N_FOX, DEC_BATCH, PAST_LEN, FOX_HEADS)) + 3.0),
        'p_prompt': nrm((DEPTH, BATCH, SEQ, PE_DIM)),
        'p_sample': nrm((DEPTH, DEC_BATCH, DEC_SEQ, PE_DIM)),
        'norm_mix': gain((DEPTH, D_MODEL)),
        'norm_ffn': gain((DEPTH, D_MODEL)),
        'norm_pe': gain((DEPTH, D_MODEL)),
        'norm_final': gain((D_MODEL,)),
        'ret_w_in': lin((N_RET, D_MODEL, 2 * RET_QK + 2 * RET_VW)),
        'ret_gn': gain((N_RET, RET_VW)),
        'ret_w_out': lin((N_RET, RET_VW, D_MODEL)),
        'mla_w_in': lin((N_MLA, D_MODEL, MLA_Q_LORA + MLA_KV_LORA + MLA_ROPE)),
        'mla_q_norm': gain((N_MLA, MLA_Q_LORA)),
        'mla_kv_norm': gain((N_MLA, MLA_KV_LORA)),
        'mla_w_qb': lin((N_MLA, MLA_Q_LORA, MLA_HEADS * (MLA_NOPE + MLA_ROPE))),
        'mla_w_kvb': lin((N_MLA, MLA_KV_LORA, MLA_HEADS * (MLA_NOPE + MLA_V))),
        'mla_gq_nope': gain((N_MLA, MLA_NOPE)),
        'mla_gq_rope': gain((N_MLA, MLA_ROPE)),
        'mla_gk_nope': gain((N_MLA, MLA_NOPE)),
        'mla_gk_rope': gain((N_MLA, MLA_ROPE)),
        'mla_w_out': lin((N_MLA, MLA_HEADS * MLA_V, D_MODEL)),
        'fox_w_in': lin((N_FOX, D_MODEL, 4 * FOX_W + FOX_HEADS)),
        'fox_b_f': jnp.linspace(1.0, 6.0, FOX_HEADS, dtype=jnp.float32)[None, :] + nrm((N_FOX, FOX_HEADS), 0.1),
        'fox_gq': gain((N_FOX, FOX_DH)),
        'fox_gk': gain((N_FOX, FOX_DH)),
        'fox_w_out': lin((N_FOX, FOX_W, D_MODEL)),
        'ffn_w_gate': lin((DEPTH, D_MODEL, D_FF)),
        'ffn_w_up': lin((DEPTH, D_MODEL, D_FF)),
        'ffn_w_down': lin((DEPTH, D_FF, D_MODEL)),
        'pe_w_proj': lin((DEPTH, PE_DIM, D_MODEL)),
        'pe_w_gate': lin((DEPTH, D_MODEL, D_MODEL)),
    }


def reference(x_prompt, x_sample, state_ret, cache_mla_latent, cache_mla_krope, cache_fox_k, cache_fox_v,
              cache_fox_logf, p_prompt, p_sample, norm_mix, norm_ffn, norm_pe, norm_final, ret_w_in, ret_gn,
              ret_w_out, mla_w_in, mla_q_norm, mla_kv_norm, mla_w_qb, mla_w_kvb, mla_gq_nope, mla_gq_rope,
              mla_gk_nope, mla_gk_rope, mla_w_out, fox_w_in, fox_b_f, fox_gq, fox_gk, fox_w_out, ffn_w_gate,
              ffn_w_up, ffn_w_down, pe_w_proj, pe_w_gate):
    w = dict(norm_mix=norm_mix, norm_ffn=norm_ffn, norm_pe=norm_pe, norm_final=norm_final,
             ret_w_in=ret_w_in, ret_gn=ret_gn, ret_w_out=ret_w_out,
             mla_w_in=mla_w_in, mla_q_norm=mla_q_norm, mla_kv_norm=mla_kv_norm, mla_w_qb=mla_w_qb,
             mla_w_kvb=mla_w_kvb, mla_gq_nope=mla_gq_nope, mla_gq_rope=mla_gq_rope, mla_gk_nope=mla_gk_nope,
             mla_gk_rope=mla_gk_rope, mla_w_out=mla_w_out,
             fox_w_in=fox_w_in, fox_b_f=fox_b_f, fox_gq=fox_gq, fox_gk=fox_gk, fox_w_out=fox_w_out,
             ffn_w_gate=ffn_w_gate, ffn_w_up=ffn_w_up, ffn_w_down=ffn_w_down,
             pe_w_proj=pe_w_proj, pe_w_gate=pe_w_gate)
    bp = x_prompt.shape[0]
    dt = x_prompt.dtype
    y_prompt, ret_p, lat_p, kr_p, fk_p, fv_p, flf_p = trunk(
        x_prompt, p_prompt, 0,
        jnp.zeros((N_RET, bp, RET_HEADS, RET_DK, RET_DV), dt),
        jnp.zeros((N_MLA, bp, 0, MLA_KV_LORA), dt), jnp.zeros((N_MLA, bp, 0, MLA_ROPE), dt),
        jnp.zeros((N_FOX, bp, 0, FOX_HEADS, FOX_DH), dt), jnp.zeros((N_FOX, bp, 0, FOX_HEADS, FOX_DH), dt),
        jnp.zeros((N_FOX, bp, 0, FOX_HEADS), dt), w)
    y_sample, ret_s, lat_s, kr_s, fk_s, fv_s, flf_s = trunk(
        x_sample, p_sample, cache_mla_latent.shape[2], state_ret, cache_mla_latent, cache_mla_krope,
        cache_fox_k, cache_fox_v, cache_fox_logf, w)
    return (y_prompt, y_sample, ret_p, ret_s, lat_p, kr_p, lat_s, kr_s, fk_p, fv_p, flf_p, fk_s, fv_s, flf_s)
```

```python
import numpy as np
import concourse.bass as bass
import concourse.mybir as mybir
from concourse.bass_utils import run_bass_kernel_spmd
from contextlib import ExitStack

F32 = mybir.dt.float32
BF = mybir.dt.bfloat16
AF = mybir.ActivationFunctionType
ALU = mybir.AluOpType
AX = mybir.AxisListType

D = 1024
EPS = 1e-6
NEG = -30000.0
NWB = 5
LA = 3
import os
DBG_LAYERS = int(os.environ.get("MK_LAYERS", "4"))
DBG_PASSES = os.environ.get("MK_PASSES", "ps")


def _esz(dt):
    return 2 if dt == BF else 4


class Defer:
    def __init__(self, depth=2):
        self.q = []
        self.depth = depth

    def push(self, fn):
        if fn is not None:
            self.q.append(fn)
        while len(self.q) > self.depth:
            self.q.pop(0)()

    def flush(self):
        while self.q:
            self.q.pop(0)()


class Prog:
    def __init__(self, nc, es, dry):
        self.nc = nc
        self.es = es
        self.dry = dry
        self.E = dict(pe=nc.tensor, act=nc.scalar, dve=nc.vector, pool=nc.gpsimd, sp=nc.sync)
        self.semh = {}
        self.cnt = {}
        for k in ("pe", "act", "dve", "pool"):
            self.semh[k] = es.enter_context(nc.semaphore("s_" + k))
            self.cnt[k] = 0
        self.seen = {k: {} for k in self.E}
        self.ent = {}
        self.open = {}
        self.nops = 0

    def dsem(self, key):
        k = ("d", key)
        if k not in self.semh:
            self.semh[k] = self.es.enter_context(self.nc.semaphore("d%d" % len(self.semh)))
            self.cnt[k] = 0
        return k

    @staticmethod
    def box(ap, exact=False):
        t = ap.tensor
        tn = type(t).__name__
        if not (tn.startswith("SB") or tn.startswith("PSum")):
            return None
        if tn.startswith("PSum") and not exact:
            return (t.name, 0, 128, 0, 2048)
        pairs = ap.ap
        pstep, pn = pairs[0]
        sp = ap.start_partition
        if callable(sp):
            sp = sp()
        off = ap.offset - sp * pstep
        lo = hi = off
        for st, c in pairs[1:]:
            if st >= 0:
                hi += st * (c - 1)
            else:
                lo += st * (c - 1)
        e = _esz(ap.dtype)
        return (t.name, sp, sp + pn, lo * e, (hi + 1) * e)

    def op(self, eng, fn, reads=(), writes=(), dkey=None):
        if self.dry:
            return
        self.nops += 1
        need = {}
        rb = [b for b in (self.box(a) for a in reads) if b]
        wb = [b for b in (self.box(a) for a in writes) if b]
        if eng != "pe":
            for b in (x for x in (self.box(a, exact=True) for a in reads) if x):
                if b[0] in self.open:
                    self.open[b[0]] = [ob for ob in self.open[b[0]]
                                       if not (ob[1] < b[2] and b[1] < ob[2] and ob[3] < b[4] and b[3] < ob[4])]
        for b in rb:
            for e in self.ent.get(b[0], ()):
                if e[1] and e[0][1] < b[2] and b[1] < e[0][2] and e[0][3] < b[4] and b[3] < e[0][4]:
                    for k, v in e[2].items():
                        if need.get(k, 0) < v:
                            need[k] = v
        for b in wb:
            for e in self.ent.get(b[0], ()):
                if e[0][1] < b[2] and b[1] < e[0][2] and e[0][3] < b[4] and b[3] < e[0][4]:
                    for k, v in e[2].items():
                        if need.get(k, 0) < v:
                            need[k] = v
        E = self.E[eng]
        seen = self.seen[eng]
        for k, v in need.items():
            if k == eng and eng == "pe":
                continue
            if seen.get(k, 0) >= v:
                continue
            E.wait_ge(self.semh[k], v)
            seen[k] = v
        ins = fn(E)
        if dkey is not None:
            k = self.dsem(dkey)
            self.cnt[k] += 16
            ins.then_inc(self.semh[k], 16)
        else:
            k = eng
            self.cnt[k] += 1
            ins.then_inc(self.semh[k], 1)
        tok = {k: self.cnt[k]}
        for b in wb:
            L = self.ent.setdefault(b[0], [])
            L[:] = [e for e in L if not (b[1] <= e[0][1] and e[0][2] <= b[2] and b[3] <= e[0][3] and e[0][4] <= b[4])]
            L.append((b, True, tok))
        for b in rb:
            L = self.ent.setdefault(b[0], [])
            for e in L:
                if (not e[1]) and e[0] == b:
                    for kk, vv in tok.items():
                        if e[2].get(kk, 0) < vv:
                            e[2][kk] = vv
                    break
            else:
                L.append((b, False, dict(tok)))

    def barrier(self):
        if self.dry:
            return
        for eng, E in self.E.items():
            seen = self.seen[eng]
            for k, v in self.cnt.items():
                if v > seen.get(k, 0):
                    E.wait_ge(self.semh[k], v)
                    seen[k] = v
        self.ent.clear()

    def _pe_open(self, out, start):
        if self.dry:
            return
        b = self.box(out, exact=True)
        L = self.open.setdefault(b[0], [])
        if start:
            for ob in L:
                if ob != b and ob[1] < b[2] and b[1] < ob[2] and ob[3] < b[4] and b[3] < ob[4]:
                    raise RuntimeError("PSUM overwrite of un-evacuated group %s by %s" % (ob, b))
            if b not in L:
                L.append(b)

    def mm(self, out, lhsT, rhs, start, stop):
        self._pe_open(out, start)
        self.op("pe", lambda e: e.matmul(out, lhsT=lhsT, rhs=rhs, start=start, stop=stop), [lhsT, rhs], [out])

    def tr(self, out, in_, ident):
        self._pe_open(out, True)
        self.op("pe", lambda e: e.transpose(out=out, in_=in_, identity=ident), [in_, ident], [out])

    def act(self, out, in_, func, scale=None, bias=None, accum=None, junk=False):
        kw = {}
        rd = [in_]
        if scale is not None:
            kw["scale"] = scale
            if not isinstance(scale, (int, float)):
                rd.append(scale)
        if bias is not None:
            kw["bias"] = bias
            if not isinstance(bias, (int, float)):
                rd.append(bias)
        wr = [out]
        if accum is not None:
            kw["accum_out"] = accum
            wr.append(accum)
        self.op("act", lambda e: e.activation(out=out, in_=in_, func=func, **kw), rd, wr)

    def tt(self, eng, out, in0, in1, op):
        self.op(eng, lambda e: e.tensor_tensor(out=out, in0=in0, in1=in1, op=op), [in0, in1], [out])

    def ts(self, eng, out, in0, s1, op0, s2=None, op1=None):
        rd = [in0] + [s for s in (s1, s2) if s is not None and not isinstance(s, (int, float))]
        if op1 is None:
            self.op(eng, lambda e: e.tensor_scalar(out=out, in0=in0, scalar1=s1, scalar2=None, op0=op0), rd, [out])
        else:
            self.op(eng, lambda e: e.tensor_scalar(out=out, in0=in0, scalar1=s1, scalar2=s2, op0=op0, op1=op1), rd, [out])

    def stt(self, out, in0, sc, in1, op0, op1):
        rd = [in0, in1] + ([] if isinstance(sc, (int, float)) else [sc])
        self.op("dve", lambda e: e.scalar_tensor_tensor(out=out, in0=in0, scalar=sc, in1=in1, op0=op0, op1=op1), rd, [out])

    def copy(self, eng, out, in_):
        if eng == "act":
            self.op("act", lambda e: e.copy(out=out, in_=in_), [in_], [out])
        else:
            self.op(eng, lambda e: e.tensor_copy(out=out, in_=in_), [in_], [out])

    def recip(self, out, in_):
        self.op("dve", lambda e: e.reciprocal(out=out, in_=in_), [in_], [out])

    def memset(self, eng, ap, val):
        self.op(eng, lambda e: e.memset(ap, val), [], [ap])

    def dma(self, out, in_, key=None, q="sp"):
        if self.dry:
            return
        bo, bi = self.box(out), self.box(in_)
        if bo is not None:
            key = ("i", bo[0], bo[3])
        else:
            key = ("o", bi[0], bi[3])
        self.op(q, lambda e: e.dma_start(out=out, in_=in_), [in_], [out], dkey=key)


def _consts():
    c = {}
    i128 = np.arange(128)
    c["ident"] = np.eye(128, dtype=np.float32)
    c["ones"] = np.ones((128, 128), np.float32)
    gam = 1.0 - 2.0 ** (-5.0 - np.arange(4))
    lg = np.log(gam)
    s = i128[:, None].astype(np.float64)
    t = i128[None, :].astype(np.float64)
    mk = np.zeros((128, 4, 128), np.float64)
    for h in range(4):
        mk[:, h, :] = np.where(s <= t, np.exp(lg[h] * (-(s + 1.0))), 0.0) / 16.0
    c["maskY_p"] = mk.astype(np.float32)
    rd = np.zeros((128, 16), np.float64)
    for h in range(4):
        rd[:, h] = np.exp(lg[h] * (i128 + 1.0))
        rd[:, 4 + h] = np.exp(lg[h] * (127.0 - i128)) / 16.0
    c["rdec_p"] = rd.astype(np.float32)
    c["cdec_p"] = [float(np.exp(lg[h] * 128.0)) for h in range(4)]
    i64 = np.arange(64)
    sq = i64 // 32
    sl = (i64 % 32).astype(np.float64)
    same = sq[:, None] == sq[None, :]
    mk = np.zeros((128, 4, 128), np.float64)
    for h in range(4):
        mk[:64, h, :64] = np.where(same & (sl[:, None] <= sl[None, :]), np.exp(lg[h] * (-(sl[:, None] + 1.0))), 0.0) / 16.0
    c["maskY_s"] = mk.astype(np.float32)
    rd = np.zeros((128, 16), np.float64)
    for h in range(4):
        rd[:64, h] = np.exp(lg[h] * (sl + 1.0))
        for i in range(2):
            rd[:64, 4 + 2 * h + i] = np.where(sq == i, np.exp(lg[h] * (31.0 - sl)) / 16.0, 0.0)
    c["rdec_s"] = rd.astype(np.float32)
    c["cdec_s"] = [float(np.exp(lg[h] * 32.0)) for h in range(4)]
    cm = np.zeros((128, 2, 64), np.float32)
    cm[:, 0, :32] = 1.0
    cm[:, 1, 32:] = 1.0
    c["colmask_s"] = cm

    def rope_tab(pos, half):
        inv = (np.float32(10000.0) ** (-np.arange(half, dtype=np.float32) / np.float32(half))).astype(np.float32)
        ang = (pos.astype(np.float32)[:, None] * inv[None, :]).astype(np.float32)
        return np.cos(ang.astype(np.float64)).astype(np.float32), np.sin(ang.astype(np.float64)).astype(np.float32)

    pos_p = np.arange(2048)
    pos_s = np.concatenate([2048 + np.arange(32), 2048 + np.arange(32)])
    cs, sn = rope_tab(pos_p, 128)
    c["rc_p"] = np.ascontiguousarray(cs.T)
    c["rs_p"] = np.ascontiguousarray(sn.T)
    cs, sn = rope_tab(pos_s, 128)
    c["rc_s"] = np.ascontiguousarray(cs.T)
    c["rs_s"] = np.ascontiguousarray(sn.T)
    cs, sn = rope_tab(pos_p, 32)
    c["mc_p"] = cs
    c["ms_p"] = sn
    cs, sn = rope_tab(pos_s, 32)
    c["mc_s"] = cs
    c["ms_s"] = sn
    c["negtri"] = np.where(i128[:, None] <= i128[None, :], 0.0, NEG).astype(np.float32)
    c["negchunk"] = np.where((i128[:, None] // 64) <= (i128[None, :] // 64), 0.0, NEG).astype(np.float32)
    nc_ = np.zeros((128, 2, 64), np.float32)
    nc_[:, 0, 32:] = NEG
    nc_[:, 1, :32] = NEG
    c["negcol"] = nc_
    a = np.full((128, 128), 0.0, np.float32)
    a[:64, :64] = np.where(same & (sl[:, None] <= sl[None, :]), 0.0, NEG)
    c["fox_negnew"] = a
    a = np.full((128, 128), 0.0, np.float32)
    a[:64, :64] = np.where(same, 0.0, NEG)
    c["mla_negnew"] = a
    c["U"] = (i128[:, None] <= i128[None, :]).astype(np.float32)
    a = np.zeros((128, 128), np.float32)
    a[:64, :64] = (same & (sl[:, None] <= sl[None, :])).astype(np.float32)
    c["Ublk"] = a
    sel = np.zeros((128, 3, 128), np.float32)
    sel[127, 0, :] = 1.0
    sel[127, 1, :32] = 1.0
    sel[127, 2, 32:64] = 1.0
    c["SEL"] = sel
    return c


CONST_NAMES = ["ident", "ones", "maskY_p", "rdec_p", "maskY_s", "rdec_s", "colmask_s", "rc_p", "rs_p", "rc_s", "rs_s",
               "mc_p", "ms_p", "mc_s", "ms_s", "negtri", "negchunk", "negcol", "fox_negnew", "mla_negnew", "U", "Ublk", "SEL"]

IN_SHAPES = dict(
    x_prompt=(2, 2048, 1024), x_sample=(2, 32, 1024), state_ret=(2, 2, 4, 256, 512),
    cache_mla_latent=(1, 2, 2048, 256), cache_mla_krope=(1, 2, 2048, 64),
    cache_fox_k=(1, 2, 2048, 16, 64), cache_fox_v=(1, 2, 2048, 16, 64), cache_fox_logf=(1, 2, 2048, 16),
    p_prompt=(4, 2, 2048, 256), p_sample=(4, 2, 32, 256),
    norm_mix=(4, 1024), norm_ffn=(4, 1024), norm_pe=(4, 1024), norm_final=(1024,),
    ret_w_in=(2, 1024, 6144), ret_gn=(2, 2048), ret_w_out=(2, 2048, 1024),
    mla_w_in=(1, 1024, 832), mla_q_norm=(1, 512), mla_kv_norm=(1, 256), mla_w_qb=(1, 512, 3072),
    mla_w_kvb=(1, 256, 4096), mla_gq_nope=(1, 128), mla_gq_rope=(1, 64), mla_gk_nope=(1, 128), mla_gk_rope=(1, 64),
    mla_w_out=(1, 2048, 1024), fox_w_in=(1, 1024, 4112), fox_b_f=(1, 16), fox_gq=(1, 64), fox_gk=(1, 64),
    fox_w_out=(1, 1024, 1024), ffn_w_gate=(4, 1024, 2816), ffn_w_up=(4, 1024, 2816), ffn_w_down=(4, 2816, 1024),
    pe_w_proj=(4, 256, 1024), pe_w_gate=(4, 1024, 1024))
SHARDED = dict(x_prompt=0, x_sample=0, state_ret=1, cache_mla_latent=1, cache_mla_krope=1, cache_fox_k=1,
               cache_fox_v=1, cache_fox_logf=1, p_prompt=1, p_sample=1)
OUT_SHAPES = [
    ("y_prompt", (2, 2048, 1024), 0), ("y_sample", (2, 32, 1024), 0),
    ("ret_p", (2, 2, 4, 256, 512), 1), ("ret_s", (2, 2, 4, 256, 512), 1),
    ("lat_p", (1, 2, 2048, 256), 1), ("kr_p", (1, 2, 2048, 64), 1), ("lat_s", (1, 2, 32, 256), 1), ("kr_s", (1, 2, 32, 64), 1),
    ("fk_p", (1, 2, 2048, 16, 64), 1), ("fv_p", (1, 2, 2048, 16, 64), 1), ("flf_p", (1, 2, 2048, 16), 1),
    ("fk_s", (1, 2, 32, 16, 64), 1), ("fv_s", (1, 2, 32, 16, 64), 1), ("flf_s", (1, 2, 32, 16), 1)]


def emit(nc, dry, specs, CST):
    din = {}
    for n, shp in IN_SHAPES.items():
        din[n] = nc.dram_tensor(n, list(shp), F32, kind="ExternalInput").ap()
    for n in CONST_NAMES:
        din["c_" + n] = nc.dram_tensor("c_" + n, list(CST[n].shape), F32, kind="ExternalInput").ap()
    dout = {}
    for n, shp, _ in OUT_SHAPES:
        dout[n] = nc.dram_tensor(n, list(shp), F32, kind="ExternalOutput").ap()

    gs = ExitStack()
    with gs:
        P = Prog(nc, gs, dry)

        def sb(name, shape, dt, st=gs):
            return st.enter_context(nc.sbuf_tensor(name, list(shape), dt))

        WB = sb("WB", [128, NWB, 2048], BF)
        GB = sb("GB", [128, 1024], F32)
        TF = sb("TF", [128, 3, 512], F32)
        XS = sb("XS", [128, 3, 1024], BF)
        JUNK = sb("JUNK", [128, 1024], BF)
        ST = sb("ST", [128, 4, 8], F32)
        SS = sb("SS", [128, 16], F32)
        RS = sb("RS", [128, 16], F32)
        identf = sb("identf", [128, 128], F32)
        identb = sb("identb", [128, 128], BF)
        onesf = sb("onesf", [128, 128], F32)
        MASKY = sb("MASKY", [128, 4, 128], F32)
        RD = sb("RD", [128, 16], F32)
        NEGA = sb("NEGA", [128, 128], F32)
        NEGB = sb("NEGB", [128, 128], F32)
        NEGC = sb("NEGC", [128, 2, 64], F32)
        UT = sb("UT", [128, 128], F32)
        UBLK = sb("UBLK", [128, 128], F32)
        SEL = sb("SEL", [128, 3, 128], F32)
        CMK = sb("CMK", [128, 2, 64], BF)
        SMALL = sb("SMALL", [128, 512], F32)
        banks = [gs.enter_context(nc.psum_tensor("ps%d" % i, [128, 512], F32)) for i in range(8)]
        bankb = [b[:, :].bitcast(BF) for b in banks]
        psi = [0]

        def ps():
            i = psi[0] % 4
            psi[0] += 1
            return i

        tfi = [0]
        TFS = [TF[:, i, :] for i in range(3)]
        XSS = [XS[:, i, :] for i in range(3)]
        STS = [ST[:, i, :] for i in range(4)]
        DEPTH = [2]

        def tf():
            tfi[0] += 1
            return TFS[tfi[0] % len(TFS)]

        xsi = [0]

        def xs():
            xsi[0] += 1
            return XSS[xsi[0] % len(XSS)]

        sti = [0]

        def stt_slot():
            sti[0] += 1
            return STS[sti[0] % len(STS)]

        class WStream:
            def __init__(self):
                self.i = 0
                self.issued = 0
                self.nst = 0
                self.rec = []
                self.la = LA

            def get(self, name, pre, r0, r1, c0, c1):
                kc = (r1 - r0) // 128
                n = c1 - c0
                assert kc * 128 == r1 - r0 and kc * n <= 2048, (name, r0, r1, c0, c1)
                i = self.i
                self.i += 1
                view = WB[:, i % NWB, 0:kc * n].rearrange("p (k n) -> p k n", n=n)
                spec = (name, pre, r0, r1, c0, c1)
                if dry:
                    self.rec.append(spec)
                    return view
                assert specs[i] == spec, (i, specs[i], spec)
                while self.issued < min(len(specs), i + 1 + self.la):
                    self._issue(self.issued)
                    self.issued += 1
                return view

            def _issue(self, j):
                name, pre, r0, r1, c0, c1 = specs[j]
                kc = (r1 - r0) // 128
                n = c1 - c0
                Wd = din[name]
                for ix in pre:
                    Wd = Wd[ix]
                src = Wd[r0:r1, c0:c1].rearrange("(k p) n -> p k n", p=128)
                dst = WB[:, j % NWB, 0:kc * n].rearrange("p (k n) -> p k n", n=n)
                P.dma(dst, src, q="pool")

        W = WStream()

        def load_const(dst, name, via_bf=False, shape=None):
            src = din["c_" + name]
            if via_bf:
                t = tf()
                v = t[:, 0:int(np.prod(src.shape[1:]))]
                if len(src.shape) == 3:
                    v = v.rearrange("p (a b) -> p a b", b=src.shape[2])
                P.dma(v, src, key="cst")
                P.copy("dve", dst, v)
            else:
                P.dma(dst, src, key="cst")

        load_const(identf[:, :], "ident")
        load_const(identb[:, :], "ident", via_bf=True)
        load_const(onesf[:, :], "ones")
        load_const(UT[:, :], "U")
        load_const(UBLK[:, :], "Ublk")
        load_const(SEL[:, :, :], "SEL")
        load_const(NEGC[:, :, :], "negcol")
        load_const(CMK[:, :, :], "colmask_s", via_bf=True)

        def bcast_row(dst, row_ap, key="bc"):
            P.dma(dst, row_ap.partition_broadcast(128), key=key)

        def run_kind(kind):
            ks = ExitStack()
            with ks:
                if kind == "p":
                    T, NT, R, NKT = 2048, 16, 128, 16
                    blocks = [(b * 512, 512) for b in range(4)]
                else:
                    T, NT, R, NKT = 64, 1, 64, 33
                    blocks = [(0, 64)]
                TK = NKT * 128
                nseq = 1 if kind == "p" else 2

                def kb(name, shape, dt):
                    return sb(name + kind, shape, dt, ks)

                H = kb("H", [128, NT, 1024], F32)
                XT = kb("XT", [128, 8, T], BF)
                if kind == "p":
                    A = kb("A", [128, 8192], BF)
                    B = kb("B", [128, 8192], BF)
                    C8 = kb("C8", [128, 4096], BF)
                    D8 = kb("D8", [128, 4096], BF)
                    E8 = kb("E8", [128, 4096], BF)
                    F8 = kb("F8", [128, 4096], BF)
                else:
                    A = kb("A", [128, 8704], BF)
                    B = kb("B", [128, 8704], BF)
                    C8 = kb("C8", [128, 4096], BF)
                    D8 = kb("D8", [128, 4352], BF)
                    E8 = kb("E8", [128, 4096], BF)
                    F8 = kb("F8", [128, 4096], BF)
                AT = {}
                if kind == "s":
                    AT["FALL"] = kb("FALL", [128, NKT, 16], F32)
                    AT["NEGF"] = kb("NEGF", [128, NKT, 16], F32)
                    AT["LF"] = kb("LF", [128, NKT, 16], F32)
                    AT["MCS"] = kb("MCS", [128, NT, 2, 32], F32)
                    AT["PTB"] = kb("PTB", [128, 2, 512], BF)
                    AT["ZB"] = kb("ZB", [128, 2, 512], F32)
                    AT["FQ"] = kb("FQ", [128, 2, 512], F32)
                    AT["TFX"] = kb("TFX", [128, 4, 512], F32)
                    AT["XSX"] = kb("XSX", [128, 4, 1024], BF)
                    AT["STX"] = kb("STX", [128, 16, 8], F32)
                    AT["QZ"] = kb("QZ", [128, 2, 512], BF)
                pti = [0]

                load_const(MASKY[:, :, :], "maskY_" + kind)
                load_const(RD[:, :], "rdec_" + kind)
                load_const(NEGA[:, :], "negtri" if kind == "p" else "fox_negnew")
                load_const(NEGB[:, :], "negchunk" if kind == "p" else "mla_negnew")
                if kind == "s":
                    P.dma(AT["MCS"][0:64, 0, 0, :], din["c_mc_s"], key="cst")
                    P.dma(AT["MCS"][0:64, 0, 1, :], din["c_ms_s"], key="cst")
                cdec = CST["cdec_" + kind]
                rc_d, rs_d = din["c_rc_" + kind], din["c_rs_" + kind]

                def run_pass(s):
                    def tokv(ap3):
                        if kind == "p":
                            return ap3[s]
                        return ap3.rearrange("b t f -> (b t) f")

                    def tcols(ti):
                        return slice(ti * 128, ti * 128 + R)

                    if kind == "p":
                        for q in range(4):
                            P.dma(H[:, 4 * q:4 * q + 4, :],
                                  din["x_prompt"][s, 512 * q:512 * (q + 1), :].rearrange("(t p) d -> p t d", p=128), key=("h", q))
                    else:
                        P.dma(H[0:64, 0, :], din["x_sample"].rearrange("b t d -> (b t) d"), key=("h", 0))

                    def norm_stats(gain_row):
                        bcast_row(GB[:, :], gain_row, key="gb")
                        for ti in range(NT):
                            P.act(JUNK[:R, :], H[:R, ti, :], AF.Square, accum=SS[:R, ti:ti + 1], junk=True)
                        P.act(RS[:R, :NT], SS[:R, :NT], AF.Sqrt, scale=1.0 / D, bias=EPS)
                        P.recip(RS[:R, :NT], RS[:R, :NT])

                    def norm_to_xt(gain_row):
                        norm_stats(gain_row)
                        dq = Defer(2)
                        for ti in range(NT):
                            x = xs()
                            P.stt(x[:R, :], H[:R, ti, :], RS[:R, ti:ti + 1], GB[:R, :], ALU.mult, ALU.mult)

                            def pe_part(ti=ti, x=x):
                                b = ps()
                                pv = bankb[b][:, 0:1024].rearrange("p (c r) -> p c r", r=128)
                                for c in range(8):
                                    P.tr(pv[:, c, :R], x[:R, c * 128:(c + 1) * 128], identb[:R, :R])
                                P.copy("act", XT[:, :, tcols(ti)], pv[:, :, :R])

                            dq.push(pe_part)
                        dq.flush()

                    def lin_tm(src, wviews, ncols, evac, tiles=None):
                        rh = []
                        for v in wviews:
                            for k in range(v.shape[1]):
                                rh.append(v[:, k, :])
                        dq = Defer(DEPTH[0])
                        for ti in (range(NT) if tiles is None else tiles):
                            b = ps()
                            out = banks[b][:R, :ncols]
                            for i, r_ in enumerate(rh):
                                P.mm(out, src(i, ti), r_, i == 0, i == len(rh) - 1)
                            dq.push(evac(ti, out))
                        dq.flush()

                    def xt_src(i, ti):
                        return XT[:, i, tcols(ti)]

                    def add_to_h(ti, c0, n, psum):
                        P.tt("dve", H[:R, ti, c0:c0 + n], psum, H[:R, ti, c0:c0 + n], ALU.add)

                    def transpose_to(dst_fn, src_tile, nchunks, ti, eng="act"):
                        b = ps()
                        pv = bankb[b][:, 0:1024].rearrange("p (c r) -> p c r", r=128)
                        for c in range(nchunks):
                            P.tr(pv[:, c, :R], src_tile[:R, c * 128:(c + 1) * 128], identb[:R, :R])
                        for c in range(nchunks):
                            P.copy(eng, dst_fn(c), pv[:, c, :R])

                    def retention(j):
                        QT = C8[:, 0:4096].rearrange("p (m t) -> p m t", m=2)
                        KT = D8[:, 0:4096].rearrange("p (m t) -> p m t", m=2)
                        V = A[:, 0:8192].rearrange("p (c v) -> p c v", v=512)
                        YT = A[:, 0:8192].rearrange("p (k t) -> p k t", k=4)
                        Y = B[:, 0:8192].rearrange("p (c v) -> p c v", v=512)
                        RT = B[:, 0:8192].bitcast(F32).rearrange("p (a n) -> p a n", n=512)
                        if nseq == 1:
                            Sf = E8[:, 0:2048].bitcast(F32).rearrange("p (i m v) -> p i m v", i=1, m=2)
                            Sb = E8[:, 2048:3072].rearrange("p (i m v) -> p i m v", i=1, m=2)
                            Sb2 = [Sb, E8[:, 3072:4096].rearrange("p (i m v) -> p i m v", i=1, m=2)]
                            TAB = F8[:, 0:4096].bitcast(F32).rearrange("p (s a n) -> p s a n", s=2, a=2)
                        else:
                            Sf = E8[:, 0:4096].bitcast(F32).rearrange("p (i m v) -> p i m v", i=2, m=2)
                            Sb = F8[:, 0:2048].rearrange("p (i m v) -> p i m v", i=2, m=2)
                            TAB = F8[:, 2048:2560].bitcast(F32).rearrange("p (s a n) -> p s a n", s=1, a=2)
                            QM = F8[:, 2560:2816].rearrange("p (i m t) -> p i m t", i=2, m=2)
                        for h in range(4):
                            wq = W.get("ret_w_in", (j,), 0, 1024, h * 256, h * 256 + 256)
                            wk = W.get("ret_w_in", (j,), 0, 1024, 1024 + h * 256, 1024 + h * 256 + 256)
                            rti = 0
                            for bi, (b0, bn) in enumerate(blocks):
                                sl = bi % TAB.shape[1]
                                P.dma(TAB[:, sl, 0, :bn], rc_d[:, b0:b0 + bn], key=("tab", sl))
                                P.dma(TAB[:, sl, 1, :bn], rs_d[:, b0:b0 + bn], key=("tab", sl))
                                cos, sin = TAB[:, sl, 0, :bn], TAB[:, sl, 1, :bn]
                                for wv, dst in ((wq, QT), (wk, KT)):
                                    bk = []
                                    for m in range(2):
                                        b = ps()
                                        bk.append(banks[b][:, :bn])
                                        for k in range(8):
                                            P.mm(bk[m], wv[:, k, m * 128:(m + 1) * 128], XT[:, k, b0:b0 + bn], k == 0, k == 7)
                                    t = [RT[:, (rti % 2) * 4 + a, :bn] for a in range(4)]
                                    rti += 1
                                    P.tt("dve", t[0], bk[0], cos, ALU.mult)
                                    P.tt("dve", t[1], bk[1], sin, ALU.mult)
                                    P.tt("dve", t[2], bk[1], cos, ALU.mult)
                                    P.tt("dve", t[3], bk[0], sin, ALU.mult)
                                    P.tt("pool", dst[:, 0, b0:b0 + bn], t[0], t[1], ALU.subtract)
                                    P.tt("pool", dst[:, 1, b0:b0 + bn], t[2], t[3], ALU.add)
                            if nseq == 2:
                                for i in range(2):
                                    for m in range(2):
                                        P.tt("pool", QM[:, i, m, :], QT[:, m, 0:64], CMK[:, i, :], ALU.mult)
                            wv0 = W.get("ret_w_in", (j,), 0, 512, 2048 + h * 512, 2048 + h * 512 + 512)
                            wv1 = W.get("ret_w_in", (j,), 512, 1024, 2048 + h * 512, 2048 + h * 512 + 512)
                            lin_tm(xt_src, [wv0, wv1], 512, lambda ti, o: P.copy("act", V[:R, ti, :], o))
                            bcast_row(GB[:, 0:512], din["ret_gn"][j, h * 512:(h + 1) * 512], key="gb")
                            if kind == "p":
                                P.memset("pool", Sf[:, 0, :, :], 0.0)
                                P.memset("pool", Sb[:, 0, :, :], 0.0)
                            else:
                                for i in range(2):
                                    P.dma(Sf[:, i, :, :], din["state_ret"][j, i, h].rearrange("(m p) v -> p m v", p=128), key=("sin", i))
                                    P.copy("pool", Sb[:, i, :, :], Sf[:, i, :, :])
                            for c in range(NT):
                                cols = tcols(c)
                                sb_r = Sb if nseq == 2 else Sb2[c % 2]
                                sb_w = Sb if nseq == 2 else Sb2[(c + 1) % 2]
                                bT = ps()
                                ktv = bankb[bT][:, 0:256]
                                for m in range(2):
                                    P.tr(ktv[:R, m * 128:(m + 1) * 128], KT[:, m, cols], identb[:, :])
                                kds = []
                                for i in range(nseq):
                                    kd = xs()
                                    kcol = (4 + h) if nseq == 1 else (4 + 2 * h + i)
                                    P.act(kd[:R, 0:256], ktv[:R, 0:256], AF.Copy, scale=RD[:R, kcol:kcol + 1])
                                    kds.append(kd)
                                bA = ps()
                                att = banks[bA][:R, :R]
                                for m in range(2):
                                    P.mm(att, KT[:, m, cols], QT[:, m, cols], m == 0, m == 1)
                                at = xs()
                                P.tt("dve", at[:R, :R], att, MASKY[:R, h, :R], ALU.mult)
                                bO = ps()
                                o = banks[bO][:R, :512]
                                P.mm(o, at[:R, :R], V[:R, c, :], True, False)
                                if nseq == 1:
                                    for m in range(2):
                                        P.mm(o, QT[:, m, cols], sb_r[:, 0, m, :], False, m == 1)
                                else:
                                    for i in range(2):
                                        for m in range(2):
                                            P.mm(o, QM[:, i, m, :], sb_r[:, i, m, :], False, i == 1 and m == 1)
                                osb = tf()
                                st = stt_slot()
                                P.act(osb[:R, :], o, AF.Copy, scale=RD[:R, h:h + 1], accum=st[:R, 0:1])
                                for i in range(nseq):
                                    for m in range(2):
                                        sp_ = banks[4 + ((2 * i + m) % 4)][:, :512]
                                        P.mm(sp_, kds[i][:R, m * 128:(m + 1) * 128], V[:R, c, :], True, True)
                                        P.stt(Sf[:, i, m, :], Sf[:, i, m, :], cdec[h], sp_, ALU.mult, ALU.add)
                                for i in range(nseq):
                                    P.copy("act", sb_w[:, i, :, :], Sf[:, i, :, :])
                                P.act(JUNK[:R, 0:512], osb[:R, :], AF.Square, accum=st[:R, 1:2], junk=True)
                                P.ts("dve", st[:R, 2:3], st[:R, 0:1], 1.0 / 512, ALU.mult)
                                P.tt("dve", st[:R, 3:4], st[:R, 2:3], st[:R, 2:3], ALU.mult)
                                P.stt(st[:R, 4:5], st[:R, 1:2], 1.0 / 512, st[:R, 3:4], ALU.mult, ALU.subtract)
                                P.act(st[:R, 5:6], st[:R, 4:5], AF.Sqrt, bias=EPS)
                                P.recip(st[:R, 5:6], st[:R, 5:6])
                                P.stt(st[:R, 6:7], st[:R, 2:3], -1.0, st[:R, 5:6], ALU.mult, ALU.mult)
                                o2 = tf()
                                P.act(o2[:R, :], osb[:R, :], AF.Identity, scale=st[:R, 5:6], bias=st[:R, 6:7])
                                P.tt("dve", Y[:R, c, :], o2[:R, :], GB[:R, 0:512], ALU.mult)
                            od = dout["ret_p"] if kind == "p" else dout["ret_s"]
                            for i in range(nseq):
                                sq_ = s if kind == "p" else i
                                P.dma(od[j, sq_, h].rearrange("(m p) v -> p m v", p=128), Sf[:, i, :, :], key=("sout", i))
                            wg0 = W.get("ret_w_in", (j,), 0, 512, 4096 + h * 512, 4096 + h * 512 + 512)
                            wg1 = W.get("ret_w_in", (j,), 512, 1024, 4096 + h * 512, 4096 + h * 512 + 512)

                            def g_evac(ti, o):
                                g = xs()
                                P.act(g[:R, 0:512], o, AF.Silu)
                                P.tt("pool", Y[:R, ti, :], Y[:R, ti, :], g[:R, 0:512], ALU.mult)
                                return lambda: transpose_to(lambda cc: YT[:, cc, tcols(ti)], Y[:, ti, :], 4, ti, eng="dve")

                            lin_tm(xt_src, [wg0, wg1], 512, g_evac)
                            wo = [W.get("ret_w_out", (j,), h * 512, h * 512 + 512, hf * 512, hf * 512 + 512) for hf in range(2)]
                            for hf in range(2):
                                lin_tm(lambda i, ti: YT[:, i, tcols(ti)], [wo[hf]], 512,
                                       lambda ti, o, hf=hf: add_to_h(ti, hf * 512, 512, o))

                    def ffn(li):
                        norm_to_xt(din["norm_ffn"][li])
                        HT = A[:, 0:8192].rearrange("p (m t) -> p m t", m=4)
                        for g0 in range(0, 2816, 512):
                            g1 = min(2816, g0 + 512)
                            nch = (g1 - g0) // 128
                            for half in range(0, nch, 2):
                                c0 = g0 + half * 128
                                wg = W.get("ffn_w_gate", (li,), 0, 1024, c0, c0 + 256)
                                wu = W.get("ffn_w_up", (li,), 0, 1024, c0, c0 + 256)
                                for mm_ in range(2):
                                    m = half + mm_
                                    for (b0, bn) in blocks:
                                        bg, bu = ps(), ps()
                                        pg, pu = banks[bg][:, :bn], banks[bu][:, :bn]
                                        for k in range(8):
                                            P.mm(pg, wg[:, k, mm_ * 128:(mm_ + 1) * 128], XT[:, k, b0:b0 + bn], k == 0, k == 7)
                                        for k in range(8):
                                            P.mm(pu, wu[:, k, mm_ * 128:(mm_ + 1) * 128], XT[:, k, b0:b0 + bn], k == 0, k == 7)
                                        sg = tf()
                                        P.act(sg[:, :bn], pg, AF.Silu)
                                        P.tt("dve", HT[:, m, b0:b0 + bn], pu, sg[:, :bn], ALU.mult)
                            wd = [W.get("ffn_w_down", (li,), g0, g1, hf * 512, hf * 512 + 512) for hf in range(2)]
                            for hf in range(2):
                                lin_tm(lambda i, ti: HT[:, i, tcols(ti)], [wd[hf]], 512,
                                       lambda ti, o, hf=hf: add_to_h(ti, hf * 512, 512, o))

                    def pegate(li):
                        norm_to_xt(din["norm_pe"][li])
                        PT_ = F8[:, 0:4096].rearrange("p (m t) -> p m t", m=2)
                        pd = tokv(din["p_prompt"][li] if kind == "p" else din["p_sample"][li])
                        dq = Defer(2)
                        for ti in range(NT):
                            st_ = tf()
                            P.dma(st_[:R, 0:256], pd[ti * 128:ti * 128 + R, :], key=("pin", tfi[0] % 3))
                            xb = xs()
                            P.copy("pool", xb[:R, 0:256], st_[:R, 0:256])
                            dq.push(lambda ti=ti, xb=xb: transpose_to(lambda cc: PT_[:, cc, tcols(ti)], xb, 2, ti))
                        dq.flush()
                        for q4 in range(4):
                            wga = W.get("pe_w_gate", (li,), 0, 1024, q4 * 256, q4 * 256 + 256)
                            wp = W.get("pe_w_proj", (li,), 0, 256, q4 * 256, q4 * 256 + 256)
                            for ti in range(NT):
                                bg, bp = ps(), ps()
                                pg, pp = banks[bg][:R, :256], banks[bp][:R, :256]
                                for k in range(8):
                                    P.mm(pg, XT[:, k, tcols(ti)], wga[:, k, :], k == 0, k == 7)
                                for k in range(2):
                                    P.mm(pp, PT_[:, k, tcols(ti)], wp[:, k, :], k == 0, k == 1)
                                sg = tf()
                                P.act(sg[:R, 0:256], pg, AF.Sigmoid)
                                P.tt("dve", sg[:R, 256:512], pp, sg[:R, 0:256], ALU.mult)
                                P.tt("pool", H[:R, ti, q4 * 256:(q4 + 1) * 256], H[:R, ti, q4 * 256:(q4 + 1) * 256], sg[:R, 256:512], ALU.add)

                    def attn(h_q, st_mm, vext, dv, kind_mask, negf_col, fq_build, scale, evac):
                        def blk_post(stp, kt, krows, q0, n, tot_q0, mask, fqs):
                            pt = AT["PTB"][:, pti[0] % 2, :]
                            pti[0] += 1
                            c0 = q0 - tot_q0
                            bias = negf_col(kt, krows)
                            if fqs is not None:
                                if mask is not None:
                                    mw = mask.shape[1]
                                    tm = tf()
                                    P.tt("pool", tm[:krows, :mw], fqs[:krows, c0:c0 + mw], mask, ALU.add)
                                    P.tt("dve", stp[:, 0:mw], stp[:, 0:mw], tm[:krows, :mw], ALU.add)
                                    if n > mw:
                                        P.tt("dve", stp[:, mw:n], stp[:, mw:n], fqs[:krows, c0 + mw:c0 + n], ALU.add)
                                else:
                                    P.tt("dve", stp, stp, fqs[:krows, c0:c0 + n], ALU.add)
                                P.act(pt[:krows, c0:c0 + n], stp, AF.Exp, scale=scale, bias=bias)
                            else:
                                if mask is not None:
                                    mw = mask.shape[1]
                                    P.tt("dve", stp[:, 0:mw], stp[:, 0:mw], mask, ALU.add)
                                P.act(pt[:krows, c0:c0 + n], stp, AF.Exp, scale=scale)
                            return pt, c0

                        def blk_pv(ptc, kt, krows, accs, first, lastmap):
                            pt, c0 = ptc
                            for (jq, acc, qr) in accs:
                                if jq * 128 < c0:
                                    continue
                                P.mm(acc, pt[:krows, jq * 128:jq * 128 + qr], vext(kt, krows), first, lastmap[jq] == kt)

                        def run_blocks(descs, tot_q0, accs, lastmap, fqs, hook=None):
                            LOOK = 2
                            nb = len(descs)
                            sts = {}
                            for i in range(min(LOOK, nb)):
                                d = descs[i]
                                sts[i] = st_mm(d[0], d[1], d[2], d[3])
                            d = descs[0]
                            posts = {0: blk_post(sts.pop(0), d[0], d[1], d[2], d[3], tot_q0, d[4], fqs)}
                            for i in range(nb):
                                if i + LOOK < nb:
                                    d = descs[i + LOOK]
                                    sts[i + LOOK] = st_mm(d[0], d[1], d[2], d[3])
                                if i + 1 < nb:
                                    d = descs[i + 1]
                                    posts[i + 1] = blk_post(sts.pop(i + 1), d[0], d[1], d[2], d[3], tot_q0, d[4], fqs)
                                d = descs[i]
                                blk_pv(posts.pop(i), d[0], d[1], accs, d[5], lastmap)
                                if i == 1 and hook is not None:
                                    hook()

                        if kind == "p":
                            nxt = [fq_build(0) if fq_build else None]
                            for qb in range(4):
                                fqs = nxt[0]

                                def hook(qb=qb):
                                    if fq_build and qb + 1 < 4:
                                        nxt[0] = fq_build(qb + 1)

                                accs = [(jq, banks[4 + jq][:128, :dv + 1], 128) for jq in range(4)]
                                lastmap = {jq: 4 * qb + jq for jq in range(4)}
                                descs = []
                                for kt in range(4 * qb + 4):
                                    jmin = max(0, kt - 4 * qb)
                                    q0 = qb * 512 + jmin * 128
                                    mask = kind_mask if kt >= 4 * qb else None
                                    descs.append((kt, 128, q0, 512 - jmin * 128, mask, kt == 0))
                                run_blocks(descs, qb * 512, accs, lastmap, fqs, hook)
                                dq = Defer(2)
                                for jq in range(4):
                                    dq.push(evac(4 * qb + jq, accs[jq][1]))
                                dq.flush()
                        else:
                            fqs = fq_build(0) if fq_build else None
                            accs = [(0, banks[4][:64, :dv + 1], 64)]
                            lastmap = {0: 32}
                            descs = []
                            for kt in range(33):
                                if kt < 32:
                                    descs.append((kt, 128, 0, 64, NEGC[:, kt // 16, :], kt == 0))
                                else:
                                    descs.append((kt, 64, 0, 64, kind_mask[:64, :64], False))
                            run_blocks(descs, 0, accs, lastmap, fqs)
                            p_ = evac(0, accs[0][1])
                            if p_ is not None:
                                p_()

                    def mla(li):
                        norm_to_xt(din["norm_mix"][li])
                        if kind == "p":
                            AT["MCS"] = E8[:, 0:2048].bitcast(F32).rearrange("p (t a f) -> p t a f", a=2, f=32)
                            AT["PTB"] = E8[:, 2048:3072].rearrange("p (s n) -> p s n", s=2)
                            AT["ZB"] = F8[:, 0:2048].bitcast(F32).rearrange("p (s n) -> p s n", s=2)
                            AT["FQ"] = F8[:, 2048:4096].bitcast(F32).rearrange("p (s n) -> p s n", s=2)
                            P.dma(AT["MCS"][:, :, 0, :], din["c_mc_p"].rearrange("(t p) f -> p t f", p=128))
                            P.dma(AT["MCS"][:, :, 1, :], din["c_ms_p"].rearrange("(t p) f -> p t f", p=128))
                        MCS = AT["MCS"]
                        if kind == "p":
                            ex_tf = [XT[:, 4 + i // 2, (i % 2) * 1024:(i % 2) * 1024 + 1024].bitcast(F32) for i in range(3)]
                            ex_st_base = XT[:, 5, 1024:2048].bitcast(F32).rearrange("p (s e) -> p s e", e=8)
                            ex_st = [ex_st_base[:, i, :] for i in range(16)]
                            ex_xs = [XT[:, 6 + i // 2, (i % 2) * 1024:(i % 2) * 1024 + 1024] for i in range(4)]
                        else:
                            ex_tf = [AT["TFX"][:, i, :] for i in range(4)]
                            ex_st = [AT["STX"][:, i, :] for i in range(16)]
                            ex_xs = [AT["XSX"][:, i, :] for i in range(4)]
                        n_tf0, n_xs0, n_st0 = len(TFS), len(XSS), len(STS)
                        if kind == "p":
                            CQ = A[:, 0:8192].rearrange("p (k t) -> p k t", k=4)
                            LATT = B[:, 0:4096].rearrange("p (k t) -> p k t", k=2)
                            KRT = B[:, 4096:6144]
                            KNT = B[:, 6144:8192]
                            QNT = C8[:, 0:2048]
                            QRT = C8[:, 2048:4096]
                            VE = D8[:, 0:16 * 129].rearrange("p (t v) -> p t v", v=129)
                            OTG = XT[:, 0:4, :]
                        else:
                            CQ = C8[:, 0:256].rearrange("p (k t) -> p k t", k=4)
                            LATT = A[:, 0:2 * TK].rearrange("p (k t) -> p k t", k=2)
                            KRT = B[:, 0:TK]
                            KNT = B[:, TK:2 * TK]
                            QNT = C8[:, 256:320]
                            QRT = C8[:, 320:384]
                            VE = D8[:, 0:33 * 129].rearrange("p (t v) -> p t v", v=129)
                            OTG = C8[:, 512:768].rearrange("p (k t) -> p k t", k=4)
                        bcast_row(SMALL[:, 0:128], din["mla_gq_nope"][0], key="sm")
                        bcast_row(SMALL[:, 128:192], din["mla_gq_rope"][0], key="sm")
                        bcast_row(SMALL[:, 192:320], din["mla_gk_nope"][0], key="sm")
                        bcast_row(SMALL[:, 320:384], din["mla_gk_rope"][0], key="sm")
                        first = [True]

                        def rope_tm(dst_f32, src_f32, ti, t):
                            cs, sn = MCS[:R, ti, 0, :], MCS[:R, ti, 1, :]
                            P.tt("dve", t[:R, 0:32], src_f32[:, 0:32], cs, ALU.mult)
                            P.tt("dve", t[:R, 32:64], src_f32[:, 32:64], sn, ALU.mult)
                            P.tt("dve", t[:R, 64:96], src_f32[:, 32:64], cs, ALU.mult)
                            P.tt("dve", t[:R, 96:128], src_f32[:, 0:32], sn, ALU.mult)
                            P.tt("dve", dst_f32[:, 0:32], t[:R, 0:32], t[:R, 32:64], ALU.subtract)
                            P.tt("dve", dst_f32[:, 32:64], t[:R, 64:96], t[:R, 96:128], ALU.add)

                        def rms_tm(psum, n, gain, out, st, col):
                            P.act(JUNK[:psum.shape[0], 0:n], psum, AF.Square, accum=st[:psum.shape[0], col:col + 1], junk=True)
                            P.act(st[:psum.shape[0], col + 1:col + 2], st[:psum.shape[0], col:col + 1], AF.Sqrt, scale=1.0 / n, bias=EPS)
                            P.recip(st[:psum.shape[0], col + 1:col + 2], st[:psum.shape[0], col + 1:col + 2])
                            P.stt(out, psum, st[:psum.shape[0], col + 1:col + 2], gain, ALU.mult, ALU.mult)

                        key_new0 = (NKT - 1) * 128 if kind == "s" else 0

                        def kcols_new(ti):
                            return slice(key_new0 + ti * 128, key_new0 + ti * 128 + R)

                        bcast_row(GB[:, 0:512], din["mla_q_norm"][0], key="gb")
                        bcast_row(GB[:, 512:768], din["mla_kv_norm"][0], key="gb")

                        def cq_evac(ti, o):
                            st = stt_slot()
                            x = xs()
                            rms_tm(o, 512, GB[:R, 0:512], x[:R, 0:512], st, 0)
                            return lambda: transpose_to(lambda cc: CQ[:, cc, tcols(ti)], x, 4, ti)

                        w0 = W.get("mla_w_in", (0,), 0, 512, 0, 512)
                        w1 = W.get("mla_w_in", (0,), 512, 1024, 0, 512)
                        lin_tm(xt_src, [w0, w1], 512, cq_evac)

                        def kv_evac(ti, o):
                            st = stt_slot()
                            lat = tf()
                            rms_tm(o[:, 0:256], 256, GB[:R, 512:768], lat[:R, 0:256], st, 0)
                            rms_tm(o[:, 256:320], 64, SMALL[:R, 320:384], lat[:R, 320:384], st, 2)
                            rope_tm(lat[:R, 256:320], lat[:R, 320:384], ti, lat[:, 384:512])
                            ld = tokv(dout["lat_p"][0] if kind == "p" else dout["lat_s"][0])
                            kd_ = tokv(dout["kr_p"][0] if kind == "p" else dout["kr_s"][0])
                            P.dma(ld[ti * 128:ti * 128 + R, :], lat[:R, 0:256], key=("mo", tfi[0] % 3))
                            P.dma(kd_[ti * 128:ti * 128 + R, :], lat[:R, 256:320], key=("mo", tfi[0] % 3))
                            x = xs()
                            P.copy("pool", x[:R, 0:320], lat[:R, 0:320])

                            def pe_part():
                                b = ps()
                                pv = bankb[b]
                                for c in range(2):
                                    P.tr(pv[:, c * 128:c * 128 + R], x[:R, c * 128:(c + 1) * 128], identb[:R, :R])
                                P.tr(pv[0:64, 256:256 + R], x[:R, 256:320], identb[:R, :R])
                                for c in range(2):
                                    P.copy("act", LATT[:, c, kcols_new(ti)], pv[:, c * 128:c * 128 + R])
                                P.copy("act", KRT[0:64, kcols_new(ti)], pv[0:64, 256:256 + R])

                            return pe_part

                        w2 = W.get("mla_w_in", (0,), 0, 512, 512, 832)
                        w3 = W.get("mla_w_in", (0,), 512, 1024, 512, 832)
                        lin_tm(xt_src, [w2, w3], 320, kv_evac)
                        if kind == "s":
                            for i in range(2):
                                for q in range(16):
                                    kt = 16 * i + q
                                    st_ = tf()
                                    P.dma(st_[:, 0:256], din["cache_mla_latent"][0, i, q * 128:(q + 1) * 128, :], key=("pin", tfi[0] % 3))
                                    P.dma(st_[:, 256:320], din["cache_mla_krope"][0, i, q * 128:(q + 1) * 128, :], key=("pin", tfi[0] % 3))
                                    x = xs()
                                    P.copy("pool", x[:, 0:320], st_[:, 0:320])
                                    b = ps()
                                    pv = bankb[b]
                                    for c in range(2):
                                        P.tr(pv[:, c * 128:(c + 1) * 128], x[:, c * 128:(c + 1) * 128], identb[:, :])
                                    P.tr(pv[0:64, 256:384], x[:, 256:320], identb[:, :])
                                    for c in range(2):
                                        P.copy("act", LATT[:, c, kt * 128:(kt + 1) * 128], pv[:, c * 128:(c + 1) * 128])
                                    P.copy("act", KRT[0:64, kt * 128:(kt + 1) * 128], pv[0:64, 256:384])
                        P.memset("pool", VE[:, :, 128:129], 1.0)
                        P.memset("pool", KRT[64:128, :], 0.0)
                        P.memset("pool", QRT[64:128, :], 0.0)
                        scale = 192.0 ** -0.5
                        TFS.extend(ex_tf)
                        XSS.extend(ex_xs)
                        STS.extend(ex_st)
                        DEPTH[0] = 4
                        for hg in range(4):
                            W.la = 2
                            wkv = W.get("mla_w_kvb", (0,), 0, 256, hg * 1024, hg * 1024 + 1024)
                            for hh in range(4):
                                h = hg * 4 + hh
                                if hh % 2 == 0:
                                    wq2 = W.get("mla_w_qb", (0,), 0, 512, (hg * 4 + hh) * 192, (hg * 4 + hh) * 192 + 384)
                                wq = wq2[:, :, (hh % 2) * 192:(hh % 2) * 192 + 192]

                                def q_evac(ti, o):
                                    st = stt_slot()
                                    x = xs()
                                    rms_tm(o[:, 0:128], 128, SMALL[:R, 0:128], x[:R, 0:128], st, 0)
                                    qr = tf()
                                    rms_tm(o[:, 128:192], 64, SMALL[:R, 128:192], qr[:R, 128:192], st, 2)
                                    rope_tm(x[:R, 128:192], qr[:R, 128:192], ti, qr[:, 256:384])

                                    def pe_part():
                                        b = ps()
                                        pv = bankb[b]
                                        P.tr(pv[:, 0:R], x[:R, 0:128], identb[:R, :R])
                                        P.tr(pv[0:64, 128:128 + R], x[:R, 128:192], identb[:R, :R])
                                        P.copy("act", QNT[:, tcols(ti)], pv[:, 0:R])
                                        P.copy("act", QRT[0:64, tcols(ti)], pv[0:64, 128:128 + R])

                                    return pe_part

                                lin_tm(lambda i, ti: CQ[:, i, tcols(ti)], [wq], 192, q_evac)
                                dq = Defer(DEPTH[0])
                                for kt in range(NKT):
                                    kr_ = R if (kind == "s" and kt == 32) else 128
                                    b = ps()
                                    o = banks[b][:kr_, :256]
                                    for k in range(2):
                                        P.mm(o, LATT[:, k, kt * 128:kt * 128 + kr_], wkv[:, k, hh * 256:(hh + 1) * 256], k == 0, k == 1)
                                    st = stt_slot()
                                    x = xs()
                                    rms_tm(o[:, 0:128], 128, SMALL[:kr_, 192:320], x[:kr_, 0:128], st, 0)
                                    P.copy("act", VE[:kr_, kt, 0:128], o[:, 128:256])

                                    def pe_part(kt=kt, kr_=kr_, x=x):
                                        b2 = ps()
                                        pv = bankb[b2]
                                        P.tr(pv[:, 0:kr_], x[:kr_, 0:128], identb[:kr_, :kr_])
                                        P.copy("dve", KNT[:, kt * 128:kt * 128 + kr_], pv[:, 0:kr_])

                                    dq.push(pe_part)
                                dq.flush()

                                def st_mm(kt, krows, q0, n):
                                    b = ps()
                                    o = banks[b][:krows, :n]
                                    P.mm(o, KNT[:, kt * 128:kt * 128 + krows], QNT[:, q0:q0 + n], True, False)
                                    P.mm(o, KRT[:, kt * 128:kt * 128 + krows], QRT[:, q0:q0 + n], False, True)
                                    return o

                                def o_evac(ti, acc):
                                    st = stt_slot()
                                    rr = acc.shape[0]
                                    P.recip(st[:rr, 0:1], acc[:, 128:129])
                                    x = xs()
                                    P.ts("dve", x[:rr, 0:128], acc[:, 0:128], st[:rr, 0:1], ALU.mult)

                                    def pe_part():
                                        b = ps()
                                        pv = bankb[b]
                                        P.tr(pv[:, 0:rr], x[:rr, 0:128], identb[:rr, :rr])
                                        P.copy("act", OTG[:, hh, tcols(ti)], pv[:, 0:rr])

                                    return pe_part

                                attn(h, st_mm, lambda kt, kr: VE[:kr, kt, :], 128, NEGB[:, :], lambda kt, kr: None, None, scale, o_evac)
                            wo = [W.get("mla_w_out", (0,), hg * 512, hg * 512 + 512, hf * 512, hf * 512 + 512) for hf in range(2)]
                            W.la = LA
                            for hf in range(2):
                                lin_tm(lambda i, ti: OTG[:, i, tcols(ti)], [wo[hf]], 512,
                                       lambda ti, o, hf=hf: add_to_h(ti, hf * 512, 512, o))
                        del TFS[n_tf0:]
                        del XSS[n_xs0:]
                        del STS[n_st0:]
                        DEPTH[0] = 2

                    def fox(li):
                        norm_to_xt(din["norm_mix"][li])
                        key_new0 = (NKT - 1) * 128 if kind == "s" else 0
                        knew = NKT - 1 if kind == "s" else None
                        if kind == "p":
                            AT["FALL"] = A[:, 4224:4736].bitcast(F32).rearrange("p (t h) -> p t h", h=16)
                            AT["NEGF"] = A[:, 4736:5248].bitcast(F32).rearrange("p (t h) -> p t h", h=16)
                            AT["LF"] = A[:, 5248:5760].bitcast(F32).rearrange("p (t h) -> p t h", h=16)
                            AT["PTB"] = A[:, 6144:7168].rearrange("p (s n) -> p s n", s=2)
                            AT["ZB"] = F8[:, 0:2048].bitcast(F32).rearrange("p (s n) -> p s n", s=2)
                            AT["FQ"] = F8[:, 2048:4096].bitcast(F32).rearrange("p (s n) -> p s n", s=2)
                        if kind == "p":
                            AT["QZ"] = A[:, 7168:8192].rearrange("p (s n) -> p s n", s=2)
                        FALL, NEGF, LF, FQ = AT["FALL"], AT["NEGF"], AT["LF"], AT["FQ"]
                        QZ = AT["QZ"]

                        if kind == "p":
                            VE = A[:, 0:16 * 260].rearrange("p (t h v) -> p t h v", h=4, v=65)
                            KT = B[:, 0:4096].rearrange("p (m t) -> p m t", m=2)
                            OTOK = B[:, 4096:8192].rearrange("p (t v) -> p t v", v=256)
                            QT = C8[:, 0:4096].rearrange("p (m t) -> p m t", m=2)
                            G = D8[:, 0:4096].rearrange("p (t v) -> p t v", v=256)
                            OT = E8[:, 0:4096].rearrange("p (m t) -> p m t", m=2)
                        else:
                            VE = A[:, 0:33 * 260].rearrange("p (t h v) -> p t h v", h=4, v=65)
                            KT = B[:, 0:2 * TK].rearrange("p (m t) -> p m t", m=2)
                            OTOK = C8[:, 0:256].rearrange("p (t v) -> p t v", v=256)
                            QT = C8[:, 256:384].rearrange("p (m t) -> p m t", m=2)
                            G = C8[:, 512:768].rearrange("p (t v) -> p t v", v=256)
                            OT = C8[:, 1024:1152].rearrange("p (m t) -> p m t", m=2)
                        bcast_row(SMALL[:, 0:64], din["fox_gq"][0], key="sm")
                        bcast_row(SMALL[:, 64:128], din["fox_gk"][0], key="sm")
                        bcast_row(SMALL[:, 128:144], din["fox_b_f"][0], key="sm")
                        for a in range(4):
                            P.copy("pool", SMALL[:, 256 + a * 64:256 + (a + 1) * 64], SMALL[:, 0:64])
                        for a in range(4):
                            P.copy("pool", GB[:, a * 64:(a + 1) * 64], SMALL[:, 64:128])
                        GQ4 = SMALL[:, 256:512]
                        GK4 = GB[:, 0:256]
                        wf = W.get("fox_w_in", (0,), 0, 1024, 4096, 4112)
                        lq = NKT - 1 if kind == "s" else 0

                        def f_evac(ti, o):
                            t = tf()
                            P.tt("dve", t[:R, 0:16], o, SMALL[:R, 128:144], ALU.add)
                            P.act(t[:R, 16:32], t[:R, 0:16], AF.Exp, scale=-1.0)
                            P.act(t[:R, 32:48], t[:R, 16:32], AF.Ln, bias=1.0)
                            P.ts("dve", LF[:R, lq + ti, :], t[:R, 32:48], -1.0, ALU.mult)
                            fd = tokv(dout["flf_p"][0] if kind == "p" else dout["flf_s"][0])
                            P.dma(fd[ti * 128:ti * 128 + R, :], LF[:R, lq + ti, :], key="lfo")

                        lin_tm(xt_src, [wf], 16, f_evac)
                        if kind == "s":
                            for i in range(2):
                                P.dma(LF[:, 16 * i:16 * i + 16, :], din["cache_fox_logf"][0, i].rearrange("(t p) h -> p t h", p=128), key=("lfi", i))
                        for kt in range(NKT):
                            b = ps()
                            if kind == "s" and kt == 32:
                                o = banks[b][:64, :16]
                                P.mm(o, UBLK[:64, :64], LF[:64, kt, :], True, False)
                                P.mm(o, SEL[:, 1, 0:64], FALL[:, 15, :], False, False)
                                P.mm(o, SEL[:, 2, 0:64], FALL[:, 31, :], False, True)
                                rr = 64
                            else:
                                o = banks[b][:128, :16]
                                chain = (kt % 16 != 0) if kind == "s" else (kt != 0)
                                P.mm(o, UT[:, :], LF[:, kt, :], True, not chain)
                                if chain:
                                    P.mm(o, SEL[:, 0, :], FALL[:, kt - 1, :], False, True)
                                rr = 128
                            P.copy("dve", FALL[:rr, kt, :], o)
                            P.ts("dve", NEGF[:rr, kt, :], o, -1.0, ALU.mult)
                        for g in range(4):

                            def qk_norm(o, gain4, outf, extra_scale):
                                st = stt_slot()
                                sq = tf()
                                P.act(sq[:R, 0:256], o, AF.Square)
                                P.op("dve", lambda e: e.tensor_reduce(out=st[:R, 0:4], in_=sq[:R, 0:256].rearrange("p (h d) -> p h d", d=64),
                                                                     axis=AX.X, op=ALU.add),
                                     [sq[:R, 0:256]], [st[:R, 0:4]])
                                P.act(st[:R, 4:8], st[:R, 0:4], AF.Sqrt, scale=1.0 / 64, bias=EPS)
                                P.recip(st[:R, 4:8], st[:R, 4:8])
                                if extra_scale != 1.0:
                                    P.ts("dve", st[:R, 4:8], st[:R, 4:8], extra_scale, ALU.mult)
                                for a in range(4):
                                    P.stt(outf[:, a * 64:(a + 1) * 64], o[:, a * 64:(a + 1) * 64], st[:R, 4 + a:5 + a],
                                          gain4[:R, a * 64:(a + 1) * 64], ALU.mult, ALU.mult)

                            def q_evac(ti, o):
                                x = xs()
                                qk_norm(o, GQ4, x[:R, 0:256], 0.125)
                                return lambda: transpose_to(lambda cc: QT[:, cc, tcols(ti)], x, 2, ti)

                            wq = W.get("fox_w_in", (0,), 0, 1024, g * 256, g * 256 + 256)
                            lin_tm(xt_src, [wq], 256, q_evac)

                            def k_evac(ti, o):
                                kf = tf()
                                qk_norm(o, GK4, kf[:R, 0:256], 1.0)
                                kd_ = tokv((dout["fk_p"][0] if kind == "p" else dout["fk_s"][0]).rearrange("b t h d -> b t (h d)"))
                                P.dma(kd_[ti * 128:ti * 128 + R, g * 256:(g + 1) * 256], kf[:R, 0:256], key=("ko", tfi[0] % 3))
                                x = xs()
                                P.copy("pool", x[:R, 0:256], kf[:R, 0:256])
                                return lambda: transpose_to(lambda cc: KT[:, cc, key_new0 + ti * 128:key_new0 + ti * 128 + R], x, 2, ti)

                            wk = W.get("fox_w_in", (0,), 0, 1024, 1024 + g * 256, 1024 + g * 256 + 256)
                            lin_tm(xt_src, [wk], 256, k_evac)
                            P.memset("pool", VE[:, :, :, 64:65], 1.0)

                            def v_evac(ti, o):
                                vf = tf()
                                P.copy("act", vf[:R, 0:256], o)
                                vd = tokv((dout["fv_p"][0] if kind == "p" else dout["fv_s"][0]).rearrange("b t h d -> b t (h d)"))
                                P.dma(vd[ti * 128:ti * 128 + R, g * 256:(g + 1) * 256], vf[:R, 0:256], key=("vo", tfi[0] % 3))
                                P.copy("pool", VE[:R, lq + ti, :, 0:64], vf[:R, 0:256].rearrange("p (h d) -> p h d", d=64))

                            wv = W.get("fox_w_in", (0,), 0, 1024, 2048 + g * 256, 2048 + g * 256 + 256)
                            lin_tm(xt_src, [wv], 256, v_evac)
                            if kind == "s":
                                for i in range(2):
                                    for q in range(16):
                                        kt = 16 * i + q
                                        st_ = tf()
                                        P.dma(st_[:, 0:256], din["cache_fox_k"][0, i, q * 128:(q + 1) * 128, 4 * g:4 * g + 4, :].rearrange("p h d -> p (h d)"),
                                              key=("pin", tfi[0] % 3))
                                        P.dma(st_[:, 256:512], din["cache_fox_v"][0, i, q * 128:(q + 1) * 128, 4 * g:4 * g + 4, :].rearrange("p h d -> p (h d)"),
                                              key=("pin", tfi[0] % 3))
                                        x = xs()
                                        P.copy("pool", x[:, 0:256], st_[:, 0:256])
                                        P.copy("pool", VE[:, kt, :, 0:64], st_[:, 256:512].rearrange("p (h d) -> p h d", d=64))
                                        b = ps()
                                        pv = bankb[b]
                                        for c in range(2):
                                            P.tr(pv[:, c * 128:(c + 1) * 128], x[:, c * 128:(c + 1) * 128], identb[:, :])
                                        for c in range(2):
                                            P.copy("act", KT[:, c, kt * 128:(kt + 1) * 128], pv[:, c * 128:(c + 1) * 128])
                            wg_ = W.get("fox_w_in", (0,), 0, 1024, 3072 + g * 256, 3072 + g * 256 + 256)
                            lin_tm(xt_src, [wg_], 256, lambda ti, o: P.act(G[:R, ti, :], o, AF.Sigmoid))
                            for hh in range(4):
                                h = g * 4 + hh
                                pr = slice((hh % 2) * 64, (hh % 2) * 64 + 64)
                                mc = hh // 2

                                orow = slice(64, 128) if hh % 2 == 0 else slice(0, 64)
                                P.memset("pool", QZ[orow, :, :], 0.0)

                                def fq_build(qb):
                                    qz = QZ[:, qb % 2, :]
                                    wq_ = 512 if kind == "p" else 64
                                    P.copy("act", qz[pr, 0:wq_], QT[pr, mc, qb * 512:qb * 512 + wq_])
                                    fq = FQ[:, qb % 2, :]
                                    b = ps()
                                    for jq in range(len(blocks[0:1]) * (4 if kind == "p" else 1)):
                                        qt = (4 * qb + jq) if kind == "p" else 32
                                        bm = tf()
                                        P.ts("dve", bm[:R, 0:R], identf[:R, :R], FALL[:R, qt, h:h + 1], ALU.mult)
                                        P.mm(banks[b][:, jq * 128:jq * 128 + R], onesf[:R, :], bm[:R, 0:R], True, True)
                                    wdt = 512 if kind == "p" else 64
                                    P.copy("act", fq[:, 0:wdt], banks[b][:, 0:wdt])
                                    return fq

                                def st_mm(kt, krows, q0, n):
                                    b = ps()
                                    o = banks[b][:krows, :n]
                                    P.mm(o, KT[:, mc, kt * 128:kt * 128 + krows], QZ[:, (q0 // 512) % 2, (q0 % 512):(q0 % 512) + n], True, True)
                                    return o

                                def o_evac(ti, acc):
                                    st = stt_slot()
                                    rr = acc.shape[0]
                                    P.recip(st[:rr, 0:1], acc[:, 64:65])
                                    P.stt(OTOK[:rr, ti, hh * 64:(hh + 1) * 64], acc[:, 0:64], st[:rr, 0:1], G[:rr, ti, hh * 64:(hh + 1) * 64],
                                          ALU.mult, ALU.mult)

                                attn(h, st_mm, lambda kt, kr: VE[:kr, kt, hh, :], 64, NEGA[:, :],
                                     lambda kt, kr: NEGF[:kr, kt, h:h + 1], fq_build, 1.0, o_evac)
                            for ti in range(NT):
                                transpose_to(lambda cc: OT[:, cc, tcols(ti)], OTOK[:, ti, :], 2, ti)
                            wo = W.get("fox_w_out", (0,), g * 256, g * 256 + 256, 0, 1024)
                            for hf in range(2):
                                lin_tm(lambda i, ti: OT[:, i, tcols(ti)], [wo[:, :, hf * 512:(hf + 1) * 512]], 512,
                                       lambda ti, o, hf=hf: add_to_h(ti, hf * 512, 512, o))

                    for li in range(DBG_LAYERS):
                        kind_m = li % 3
                        if kind_m == 0:
                            norm_to_xt(din["norm_mix"][li])
                            retention(li // 3)
                        elif kind_m == 1:
                            mla(li)
                        else:
                            fox(li)
                        ffn(li)
                        pegate(li)
                    norm_stats(din["norm_final"])
                    yout = dout["y_prompt"][s] if kind == "p" else dout["y_sample"].rearrange("b t d -> (b t) d")
                    for ti in range(NT):
                        for hf in range(2):
                            t = tf()
                            P.stt(t[:R, :], H[:R, ti, hf * 512:(hf + 1) * 512], RS[:R, ti:ti + 1], GB[:R, hf * 512:(hf + 1) * 512], ALU.mult, ALU.mult)
                            P.dma(yout[ti * 128:ti * 128 + R, hf * 512:(hf + 1) * 512], t[:R, :], key=("yo", tfi[0] % 3))

                if kind == "p":
                    for s in range(int(os.environ.get("MK_PSEQ", "2"))):
                        run_pass(s)
                else:
                    run_pass(0)
                P.barrier()

        if "p" in DBG_PASSES:
            run_kind("p")
        if "s" in DBG_PASSES:
            run_kind("s")
        P.barrier()
        return W.rec, P.nops


_CACHE = {}


def _build():
    if "nc" in _CACHE:
        return _CACHE["nc"], _CACHE["cst"]
    CST = _consts()
    nc0 = bass.Bass("TRN2", target_bir_lowering=False)
    specs, _ = emit(nc0, True, None, CST)
    nc = bass.Bass("TRN2", target_bir_lowering=False)
    _, nops = emit(nc, False, specs, CST)
    _CACHE["nc"] = nc
    _CACHE["cst"] = CST
    _CACHE["nops"] = nops
    return nc, CST


def kernel(**inputs):
    nc, CST = _build()
    in_maps = []
    for c in range(8):
        m = {}
        for n in IN_SHAPES:
            a = np.asarray(inputs[n], dtype=np.float32)
            if n in SHARDED:
                ax = SHARDED[n]
                sl = [slice(None)] * a.ndim
                sl[ax] = slice(2 * c, 2 * c + 2)
                a = a[tuple(sl)]
            m[n] = np.ascontiguousarray(a)
        for n in CONST_NAMES:
            m["c_" + n] = np.ascontiguousarray(CST[n], dtype=np.float32)
        in_maps.append(m)
    res = run_bass_kernel_spmd(nc, in_maps, core_ids=list(range(8)))
    outs = []
    for n, shp, ax in OUT_SHAPES:
        outs.append(np.concatenate([np.asarray(res.results[c][n], dtype=np.float32) for c in range(8)], axis=ax))
    return tuple(outs)
```

```python
import numpy as np
import concourse.bass as bass
import concourse.mybir as mybir
from concourse.bass_utils import run_bass_kernel_spmd
from contextlib import ExitStack

F32 = mybir.dt.float32
BF = mybir.dt.bfloat16
AF = mybir.ActivationFunctionType
ALU = mybir.AluOpType
AX = mybir.AxisListType

D = 1024
EPS = 1e-6
NEG = -30000.0
NWB = 5
LA = 3
import os
DBG_LAYERS = int(os.environ.get("MK_LAYERS", "4"))
DBG_PASSES = os.environ.get("MK_PASSES", "ps")


def _esz(dt):
    return 2 if dt == BF else 4


class Defer:
    def __init__(self, depth=2):
        self.q = []
        self.depth = depth

    def push(self, fn):
        if fn is not None:
            self.q.append(fn)
        while len(self.q) > self.depth:
            self.q.pop(0)()

    def flush(self):
        while self.q:
            self.q.pop(0)()


class Prog:
    def __init__(self, nc, es, dry):
        self.nc = nc
        self.es = es
        self.dry = dry
        self.E = dict(pe=nc.tensor, act=nc.scalar, dve=nc.vector, pool=nc.gpsimd, sp=nc.sync)
        self.semh = {}
        self.cnt = {}
        for k in ("pe", "act", "dve", "pool"):
            self.semh[k] = es.enter_context(nc.semaphore("s_" + k))
            self.cnt[k] = 0
        self.seen = {k: {} for k in self.E}
        self.ent = {}
        self.open = {}
        self.nops = 0

    def dsem(self, key):
        k = ("d", key)
        if k not in self.semh:
            self.semh[k] = self.es.enter_context(self.nc.semaphore("d%d" % len(self.semh)))
            self.cnt[k] = 0
        return k

    @staticmethod
    def box(ap, exact=False):
        t = ap.tensor
        tn = type(t).__name__
        if not (tn.startswith("SB") or tn.startswith("PSum")):
            return None
        if tn.startswith("PSum") and not exact:
            return (t.name, 0, 128, 0, 2048)
        pairs = ap.ap
        pstep, pn = pairs[0]
        sp = ap.start_partition
        if callable(sp):
            sp = sp()
        off = ap.offset - sp * pstep
        lo = hi = off
        for st, c in pairs[1:]:
            if st >= 0:
                hi += st * (c - 1)
            else:
                lo += st * (c - 1)
        e = _esz(ap.dtype)
        return (t.name, sp, sp + pn, lo * e, (hi + 1) * e)

    def op(self, eng, fn, reads=(), writes=(), dkey=None):
        if self.dry:
            return
        self.nops += 1
        need = {}
        rb = [b for b in (self.box(a) for a in reads) if b]
        wb = [b for b in (self.box(a) for a in writes) if b]
        if eng != "pe":
            for b in (x for x in (self.box(a, exact=True) for a in reads) if x):
                if b[0] in self.open:
                    self.open[b[0]] = [ob for ob in self.open[b[0]]
                                       if not (ob[1] < b[2] and b[1] < ob[2] and ob[3] < b[4] and b[3] < ob[4])]
        for b in rb:
            for e in self.ent.get(b[0], ()):
                if e[1] and e[0][1] < b[2] and b[1] < e[0][2] and e[0][3] < b[4] and b[3] < e[0][4]:
                    for k, v in e[2].items():
                        if need.get(k, 0) < v:
                            need[k] = v
        for b in wb:
            for e in self.ent.get(b[0], ()):
                if e[0][1] < b[2] and b[1] < e[0][2] and e[0][3] < b[4] and b[3] < e[0][4]:
                    for k, v in e[2].items():
                        if need.get(k, 0) < v:
                            need[k] = v
        E = self.E[eng]
        seen = self.seen[eng]
        for k, v in need.items():
            if k == eng and eng == "pe":
                continue
            if seen.get(k, 0) >= v:
                continue
            E.wait_ge(self.semh[k], v)
            seen[k] = v
        ins = fn(E)
        if dkey is not None:
            k = self.dsem(dkey)
            self.cnt[k] += 16
            ins.then_inc(self.semh[k], 16)
        else:
            k = eng
            self.cnt[k] += 1
            ins.then_inc(self.semh[k], 1)
        tok = {k: self.cnt[k]}
        for b in wb:
            L = self.ent.setdefault(b[0], [])
            L[:] = [e for e in L if not (b[1] <= e[0][1] and e[0][2] <= b[2] and b[3] <= e[0][3] and e[0][4] <= b[4])]
            L.append((b, True, tok))
        for b in rb:
            L = self.ent.setdefault(b[0], [])
            for e in L:
                if (not e[1]) and e[0] == b:
                    for kk, vv in tok.items():
                        if e[2].get(kk, 0) < vv:
                            e[2][kk] = vv
                    break
            else:
                L.append((b, False, dict(tok)))

    def barrier(self):
        if self.dry:
            return
        for eng, E in self.E.items():
            seen = self.seen[eng]
            for k, v in self.cnt.items():
                if v > seen.get(k, 0):
                    E.wait_ge(self.semh[k], v)
                    seen[k] = v
        self.ent.clear()

    def _pe_open(self, out, start):
        if self.dry:
            return
        b = self.box(out, exact=True)
        L = self.open.setdefault(b[0], [])
        if start:
            for ob in L:
                if ob != b and ob[1] < b[2] and b[1] < ob[2] and ob[3] < b[4] and b[3] < ob[4]:
                    raise RuntimeError("PSUM overwrite of un-evacuated group %s by %s" % (ob, b))
            if b not in L:
                L.append(b)

    def mm(self, out, lhsT, rhs, start, stop):
        self._pe_open(out, start)
        self.op("pe", lambda e: e.matmul(out, lhsT=lhsT, rhs=rhs, start=start, stop=stop), [lhsT, rhs], [out])

    def tr(self, out, in_, ident):
        self._pe_open(out, True)
        self.op("pe", lambda e: e.transpose(out=out, in_=in_, identity=ident), [in_, ident], [out])

    def act(self, out, in_, func, scale=None, bias=None, accum=None, junk=False):
        kw = {}
        rd = [in_]
        if scale is not None:
            kw["scale"] = scale
            if not isinstance(scale, (int, float)):
                rd.append(scale)
        if bias is not None:
            kw["bias"] = bias
            if not isinstance(bias, (int, float)):
                rd.append(bias)
        wr = [out]
        if accum is not None:
            kw["accum_out"] = accum
            wr.append(accum)
        self.op("act", lambda e: e.activation(out=out, in_=in_, func=func, **kw), rd, wr)

    def tt(self, eng, out, in0, in1, op):
        self.op(eng, lambda e: e.tensor_tensor(out=out, in0=in0, in1=in1, op=op), [in0, in1], [out])

    def ts(self, eng, out, in0, s1, op0, s2=None, op1=None):
        rd = [in0] + [s for s in (s1, s2) if s is not None and not isinstance(s, (int, float))]
        if op1 is None:
            self.op(eng, lambda e: e.tensor_scalar(out=out, in0=in0, scalar1=s1, scalar2=None, op0=op0), rd, [out])
        else:
            self.op(eng, lambda e: e.tensor_scalar(out=out, in0=in0, scalar1=s1, scalar2=s2, op0=op0, op1=op1), rd, [out])

    def stt(self, out, in0, sc, in1, op0, op1):
        rd = [in0, in1] + ([] if isinstance(sc, (int, float)) else [sc])
        self.op("dve", lambda e: e.scalar_tensor_tensor(out=out, in0=in0, scalar=sc, in1=in1, op0=op0, op1=op1), rd, [out])

    def copy(self, eng, out, in_):
        if eng == "act":
            self.op("act", lambda e: e.copy(out=out, in_=in_), [in_], [out])
        else:
            self.op(eng, lambda e: e.tensor_copy(out=out, in_=in_), [in_], [out])

    def recip(self, out, in_):
        self.op("dve", lambda e: e.reciprocal(out=out, in_=in_), [in_], [out])

    def memset(self, eng, ap, val):
        self.op(eng, lambda e: e.memset(ap, val), [], [ap])

    def dma(self, out, in_, key=None, q="sp"):
        if self.dry:
            return
        bo, bi = self.box(out), self.box(in_)
        if bo is not None:
            key = ("i", bo[0], bo[3])
        else:
            key = ("o", bi[0], bi[3])
        self.op(q, lambda e: e.dma_start(out=out, in_=in_), [in_], [out], dkey=key)


def _consts():
    c = {}
    i128 = np.arange(128)
    c["ident"] = np.eye(128, dtype=np.float32)
    c["ones"] = np.ones((128, 128), np.float32)
    gam = 1.0 - 2.0 ** (-5.0 - np.arange(4))
    lg = np.log(gam)
    s = i128[:, None].astype(np.float64)
    t = i128[None, :].astype(np.float64)
    mk = np.zeros((128, 4, 128), np.float64)
    for h in range(4):
        mk[:, h, :] = np.where(s <= t, np.exp(lg[h] * (-(s + 1.0))), 0.0) / 16.0
    c["maskY_p"] = mk.astype(np.float32)
    rd = np.zeros((128, 16), np.float64)
    for h in range(4):
        rd[:, h] = np.exp(lg[h] * (i128 + 1.0))
        rd[:, 4 + h] = np.exp(lg[h] * (127.0 - i128)) / 16.0
    c["rdec_p"] = rd.astype(np.float32)
    c["cdec_p"] = [float(np.exp(lg[h] * 128.0)) for h in range(4)]
    i64 = np.arange(64)
    sq = i64 // 32
    sl = (i64 % 32).astype(np.float64)
    same = sq[:, None] == sq[None, :]
    mk = np.zeros((128, 4, 128), np.float64)
    for h in range(4):
        mk[:64, h, :64] = np.where(same & (sl[:, None] <= sl[None, :]), np.exp(lg[h] * (-(sl[:, None] + 1.0))), 0.0) / 16.0
    c["maskY_s"] = mk.astype(np.float32)
    rd = np.zeros((128, 16), np.float64)
    for h in range(4):
        rd[:64, h] = np.exp(lg[h] * (sl + 1.0))
        for i in range(2):
            rd[:64, 4 + 2 * h + i] = np.where(sq == i, np.exp(lg[h] * (31.0 - sl)) / 16.0, 0.0)
    c["rdec_s"] = rd.astype(np.float32)
    c["cdec_s"] = [float(np.exp(lg[h] * 32.0)) for h in range(4)]
    cm = np.zeros((128, 2, 64), np.float32)
    cm[:, 0, :32] = 1.0
    cm[:, 1, 32:] = 1.0
    c["colmask_s"] = cm

    def rope_tab(pos, half):
        inv = (np.float32(10000.0) ** (-np.arange(half, dtype=np.float32) / np.float32(half))).astype(np.float32)
        ang = (pos.astype(np.float32)[:, None] * inv[None, :]).astype(np.float32)
        return np.cos(ang.astype(np.float64)).astype(np.float32), np.sin(ang.astype(np.float64)).astype(np.float32)

    pos_p = np.arange(2048)
    pos_s = np.concatenate([2048 + np.arange(32), 2048 + np.arange(32)])
    cs, sn = rope_tab(pos_p, 128)
    c["rc_p"] = np.ascontiguousarray(cs.T)
    c["rs_p"] = np.ascontiguousarray(sn.T)
    cs, sn = rope_tab(pos_s, 128)
    c["rc_s"] = np.ascontiguousarray(cs.T)
    c["rs_s"] = np.ascontiguousarray(sn.T)
    cs, sn = rope_tab(pos_p, 32)
    c["mc_p"] = cs
    c["ms_p"] = sn
    cs, sn = rope_tab(pos_s, 32)
    c["mc_s"] = cs
    c["ms_s"] = sn
    c["negtri"] = np.where(i128[:, None] <= i128[None, :], 0.0, NEG).astype(np.float32)
    c["negchunk"] = np.where((i128[:, None] // 64) <= (i128[None, :] // 64), 0.0, NEG).astype(np.float32)
    nc_ = np.zeros((128, 2, 64), np.float32)
    nc_[:, 0, 32:] = NEG
    nc_[:, 1, :32] = NEG
    c["negcol"] = nc_
    a = np.full((128, 128), 0.0, np.float32)
    a[:64, :64] = np.where(same & (sl[:, None] <= sl[None, :]), 0.0, NEG)
    c["fox_negnew"] = a
    a = np.full((128, 128), 0.0, np.float32)
    a[:64, :64] = np.where(same, 0.0, NEG)
    c["mla_negnew"] = a
    c["U"] = (i128[:, None] <= i128[None, :]).astype(np.float32)
    a = np.zeros((128, 128), np.float32)
    a[:64, :64] = (same & (sl[:, None] <= sl[None, :])).astype(np.float32)
    c["Ublk"] = a
    sel = np.zeros((128, 3, 128), np.float32)
    sel[127, 0, :] = 1.0
    sel[127, 1, :32] = 1.0
    sel[127, 2, 32:64] = 1.0
    c["SEL"] = sel
    return c


CONST_NAMES = ["ident", "ones", "maskY_p", "rdec_p", "maskY_s", "rdec_s", "colmask_s", "rc_p", "rs_p", "rc_s", "rs_s",
               "mc_p", "ms_p", "mc_s", "ms_s", "negtri", "negchunk", "negcol", "fox_negnew", "mla_negnew", "U", "Ublk", "SEL"]

IN_SHAPES = dict(
    x_prompt=(2, 2048, 1024), x_sample=(2, 32, 1024), state_ret=(2, 2, 4, 256, 512),
    cache_mla_latent=(1, 2, 2048, 256), cache_mla_krope=(1, 2, 2048, 64),
    cache_fox_k=(1, 2, 2048, 16, 64), cache_fox_v=(1, 2, 2048, 16, 64), cache_fox_logf=(1, 2, 2048, 16),
    p_prompt=(4, 2, 2048, 256), p_sample=(4, 2, 32, 256),
    norm_mix=(4, 1024), norm_ffn=(4, 1024), norm_pe=(4, 1024), norm_final=(1024,),
    ret_w_in=(2, 1024, 6144), ret_gn=(2, 2048), ret_w_out=(2, 2048, 1024),
    mla_w_in=(1, 1024, 832), mla_q_norm=(1, 512), mla_kv_norm=(1, 256), mla_w_qb=(1, 512, 3072),
    mla_w_kvb=(1, 256, 4096), mla_gq_nope=(1, 128), mla_gq_rope=(1, 64), mla_gk_nope=(1, 128), mla_gk_rope=(1, 64),
    mla_w_out=(1, 2048, 1024), fox_w_in=(1, 1024, 4112), fox_b_f=(1, 16), fox_gq=(1, 64), fox_gk=(1, 64),
    fox_w_out=(1, 1024, 1024), ffn_w_gate=(4, 1024, 2816), ffn_w_up=(4, 1024, 2816), ffn_w_down=(4, 2816, 1024),
    pe_w_proj=(4, 256, 1024), pe_w_gate=(4, 1024, 1024))
SHARDED = dict(x_prompt=0, x_sample=0, state_ret=1, cache_mla_latent=1, cache_mla_krope=1, cache_fox_k=1,
               cache_fox_v=1, cache_fox_logf=1, p_prompt=1, p_sample=1)
OUT_SHAPES = [
    ("y_prompt", (2, 2048, 1024), 0), ("y_sample", (2, 32, 1024), 0),
    ("ret_p", (2, 2, 4, 256, 512), 1), ("ret_s", (2, 2, 4, 256, 512), 1),
    ("lat_p", (1, 2, 2048, 256), 1), ("kr_p", (1, 2, 2048, 64), 1), ("lat_s", (1, 2, 32, 256), 1), ("kr_s", (1, 2, 32, 64), 1),
    ("fk_p", (1, 2, 2048, 16, 64), 1), ("fv_p", (1, 2, 2048, 16, 64), 1), ("flf_p", (1, 2, 2048, 16), 1),
    ("fk_s", (1, 2, 32, 16, 64), 1), ("fv_s", (1, 2, 32, 16, 64), 1), ("flf_s", (1, 2, 32, 16), 1)]


def emit(nc, dry, specs, CST):
    din = {}
    for n, shp in IN_SHAPES.items():
        din[n] = nc.dram_tensor(n, list(shp), F32, kind="ExternalInput").ap()
    for n in CONST_NAMES:
        din["c_" + n] = nc.dram_tensor("c_" + n, list(CST[n].shape), F32, kind="ExternalInput").ap()
    dout = {}
    for n, shp, _ in OUT_SHAPES:
        dout[n] = nc.dram_tensor(n, list(shp), F32, kind="ExternalOutput").ap()

    gs = ExitStack()
    with gs:
        P = Prog(nc, gs, dry)

        def sb(name, shape, dt, st=gs):
            return st.enter_context(nc.sbuf_tensor(name, list(shape), dt))

        WB = sb("WB", [128, NWB, 2048], BF)
        GB = sb("GB", [128, 1024], F32)
        TF = sb("TF", [128, 3, 512], F32)
        XS = sb("XS", [128, 3, 1024], BF)
        JUNK = sb("JUNK", [128, 1024], BF)
        ST = sb("ST", [128, 4, 8], F32)
        SS = sb("SS", [128, 16], F32)
        RS = sb("RS", [128, 16], F32)
        identf = sb("identf", [128, 128], F32)
        identb = sb("identb", [128, 128], BF)
        onesf = sb("onesf", [128, 128], F32)
        MASKY = sb("MASKY", [128, 4, 128], F32)
        RD = sb("RD", [128, 16], F32)
        NEGA = sb("NEGA", [128, 128], F32)
        NEGB = sb("NEGB", [128, 128], F32)
        NEGC = sb("NEGC", [128, 2, 64], F32)
        UT = sb("UT", [128, 128], F32)
        UBLK = sb("UBLK", [128, 128], F32)
        SEL = sb("SEL", [128, 3, 128], F32)
        CMK = sb("CMK", [128, 2, 64], BF)
        SMALL = sb("SMALL", [128, 512], F32)
        banks = [gs.enter_context(nc.psum_tensor("ps%d" % i, [128, 512], F32)) for i in range(8)]
        bankb = [b[:, :].bitcast(BF) for b in banks]
        psi = [0]

        def ps():
            i = psi[0] % 4
            psi[0] += 1
            return i

        tfi = [0]
        TFS = [TF[:, i, :] for i in range(3)]
        XSS = [XS[:, i, :] for i in range(3)]
        STS = [ST[:, i, :] for i in range(4)]
        DEPTH = [2]

        def tf():
            tfi[0] += 1
            return TFS[tfi[0] % len(TFS)]

        xsi = [0]

        def xs():
            xsi[0] += 1
            return XSS[xsi[0] % len(XSS)]

        sti = [0]

        def stt_slot():
            sti[0] += 1
            return STS[sti[0] % len(STS)]

        class WStream:
            def __init__(self):
                self.i = 0
                self.issued = 0
                self.nst = 0
                self.rec = []
                self.la = LA

            def get(self, name, pre, r0, r1, c0, c1):
                kc = (r1 - r0) // 128
                n = c1 - c0
                assert kc * 128 == r1 - r0 and kc * n <= 2048, (name, r0, r1, c0, c1)
                i = self.i
                self.i += 1
                view = WB[:, i % NWB, 0:kc * n].rearrange("p (k n) -> p k n", n=n)
                spec = (name, pre, r0, r1, c0, c1)
                if dry:
                    self.rec.append(spec)
                    return view
                assert specs[i] == spec, (i, specs[i], spec)
                while self.issued < min(len(specs), i + 1 + self.la):
                    self._issue(self.issued)
                    self.issued += 1
                return view

            def _issue(self, j):
                name, pre, r0, r1, c0, c1 = specs[j]
                kc = (r1 - r0) // 128
                n = c1 - c0
                Wd = din[name]
                for ix in pre:
                    Wd = Wd[ix]
                src = Wd[r0:r1, c0:c1].rearrange("(k p) n -> p k n", p=128)
                dst = WB[:, j % NWB, 0:kc * n].rearrange("p (k n) -> p k n", n=n)
                P.dma(dst, src, q="pool")

        W = WStream()

        def load_const(dst, name, via_bf=False, shape=None):
            src = din["c_" + name]
            if via_bf:
                t = tf()
                v = t[:, 0:int(np.prod(src.shape[1:]))]
                if len(src.shape) == 3:
                    v = v.rearrange("p (a b) -> p a b", b=src.shape[2])
                P.dma(v, src, key="cst")
                P.copy("dve", dst, v)
            else:
                P.dma(dst, src, key="cst")

        load_const(identf[:, :], "ident")
        load_const(identb[:, :], "ident", via_bf=True)
        load_const(onesf[:, :], "ones")
        load_const(UT[:, :], "U")
        load_const(UBLK[:, :], "Ublk")
        load_const(SEL[:, :, :], "SEL")
        load_const(NEGC[:, :, :], "negcol")
        load_const(CMK[:, :, :], "colmask_s", via_bf=True)

        def bcast_row(dst, row_ap, key="bc"):
            P.dma(dst, row_ap.partition_broadcast(128), key=key)

        def run_kind(kind):
            ks = ExitStack()
            with ks:
                if kind == "p":
                    T, NT, R, NKT = 2048, 16, 128, 16
                    blocks = [(b * 512, 512) for b in range(4)]
                else:
                    T, NT, R, NKT = 64, 1, 64, 33
                    blocks = [(0, 64)]
                TK = NKT * 128
                nseq = 1 if kind == "p" else 2

                def kb(name, shape, dt):
                    return sb(name + kind, shape, dt, ks)

                H = kb("H", [128, NT, 1024], F32)
                XT = kb("XT", [128, 8, T], BF)
                if kind == "p":
                    A = kb("A", [128, 8192], BF)
                    B = kb("B", [128, 8192], BF)
                    C8 = kb("C8", [128, 4096], BF)
                    D8 = kb("D8", [128, 4096], BF)
                    E8 = kb("E8", [128, 4096], BF)
                    F8 = kb("F8", [128, 4096], BF)
                else:
                    A = kb("A", [128, 8704], BF)
                    B = kb("B", [128, 8704], BF)
                    C8 = kb("C8", [128, 4096], BF)
                    D8 = kb("D8", [128, 4352], BF)
                    E8 = kb("E8", [128, 4096], BF)
                    F8 = kb("F8", [128, 4096], BF)
                AT = {}
                if kind == "s":
                    AT["FALL"] = kb("FALL", [128, NKT, 16], F32)
                    AT["NEGF"] = kb("NEGF", [128, NKT, 16], F32)
                    AT["LF"] = kb("LF", [128, NKT, 16], F32)
                    AT["MCS"] = kb("MCS", [128, NT, 2, 32], F32)
                    AT["PTB"] = kb("PTB", [128, 2, 512], BF)
                    AT["ZB"] = kb("ZB", [128, 2, 512], F32)
                    AT["FQ"] = kb("FQ", [128, 2, 512], F32)
                    AT["TFX"] = kb("TFX", [128, 4, 512], F32)
                    AT["XSX"] = kb("XSX", [128, 4, 1024], BF)
                    AT["STX"] = kb("STX", [128, 16, 8], F32)
                    AT["QZ"] = kb("QZ", [128, 2, 512], BF)
                pti = [0]

                load_const(MASKY[:, :, :], "maskY_" + kind)
                load_const(RD[:, :], "rdec_" + kind)
                load_const(NEGA[:, :], "negtri" if kind == "p" else "fox_negnew")
                load_const(NEGB[:, :], "negchunk" if kind == "p" else "mla_negnew")
                if kind == "s":
                    P.dma(AT["MCS"][0:64, 0, 0, :], din["c_mc_s"], key="cst")
                    P.dma(AT["MCS"][0:64, 0, 1, :], din["c_ms_s"], key="cst")
                cdec = CST["cdec_" + kind]
                rc_d, rs_d = din["c_rc_" + kind], din["c_rs_" + kind]

                def run_pass(s):
                    def tokv(ap3):
                        if kind == "p":
                            return ap3[s]
                        return ap3.rearrange("b t f -> (b t) f")

                    def tcols(ti):
                        return slice(ti * 128, ti * 128 + R)

                    if kind == "p":
                        for q in range(4):
                            P.dma(H[:, 4 * q:4 * q + 4, :],
                                  din["x_prompt"][s, 512 * q:512 * (q + 1), :].rearrange("(t p) d -> p t d", p=128), key=("h", q))
                    else:
                        P.dma(H[0:64, 0, :], din["x_sample"].rearrange("b t d -> (b t) d"), key=("h", 0))

                    def norm_stats(gain_row):
                        bcast_row(GB[:, :], gain_row, key="gb")
                        for ti in range(NT):
                            P.act(JUNK[:R, :], H[:R, ti, :], AF.Square, accum=SS[:R, ti:ti + 1], junk=True)
                        P.act(RS[:R, :NT], SS[:R, :NT], AF.Sqrt, scale=1.0 / D, bias=EPS)
                        P.recip(RS[:R, :NT], RS[:R, :NT])

                    def norm_to_xt(gain_row):
                        norm_stats(gain_row)
                        dq = Defer(2)
                        for ti in range(NT):
                            x = xs()
                            P.stt(x[:R, :], H[:R, ti, :], RS[:R, ti:ti + 1], GB[:R, :], ALU.mult, ALU.mult)

                            def pe_part(ti=ti, x=x):
                                b = ps()
                                pv = bankb[b][:, 0:1024].rearrange("p (c r) -> p c r", r=128)
                                for c in range(8):
                                    P.tr(pv[:, c, :R], x[:R, c * 128:(c + 1) * 128], identb[:R, :R])
                                P.copy("act", XT[:, :, tcols(ti)], pv[:, :, :R])

                            dq.push(pe_part)
                        dq.flush()

                    def lin_tm(src, wviews, ncols, evac, tiles=None):
                        rh = []
                        for v in wviews:
                            for k in range(v.shape[1]):
                                rh.append(v[:, k, :])
                        dq = Defer(DEPTH[0])
                        for ti in (range(NT) if tiles is None else tiles):
                            b = ps()
                            out = banks[b][:R, :ncols]
                            for i, r_ in enumerate(rh):
                                P.mm(out, src(i, ti), r_, i == 0, i == len(rh) - 1)
                            dq.push(evac(ti, out))
                        dq.flush()

                    def xt_src(i, ti):
                        return XT[:, i, tcols(ti)]

                    def add_to_h(ti, c0, n, psum):
                        P.tt("dve", H[:R, ti, c0:c0 + n], psum, H[:R, ti, c0:c0 + n], ALU.add)

                    def transpose_to(dst_fn, src_tile, nchunks, ti, eng="act"):
                        b = ps()
                        pv = bankb[b][:, 0:1024].rearrange("p (c r) -> p c r", r=128)
                        for c in range(nchunks):
                            P.tr(pv[:, c, :R], src_tile[:R, c * 128:(c + 1) * 128], identb[:R, :R])
                        for c in range(nchunks):
                            P.copy(eng, dst_fn(c), pv[:, c, :R])

                    def retention(j):
                        QT = C8[:, 0:4096].rearrange("p (m t) -> p m t", m=2)
                        KT = D8[:, 0:4096].rearrange("p (m t) -> p m t", m=2)
                        V = A[:, 0:8192].rearrange("p (c v) -> p c v", v=512)
                        YT = A[:, 0:8192].rearrange("p (k t) -> p k t", k=4)
                        Y = B[:, 0:8192].rearrange("p (c v) -> p c v", v=512)
                        RT = B[:, 0:8192].bitcast(F32).rearrange("p (a n) -> p a n", n=512)
                        if nseq == 1:
                            Sf = E8[:, 0:2048].bitcast(F32).rearrange("p (i m v) -> p i m v", i=1, m=2)
                            Sb = E8[:, 2048:3072].rearrange("p (i m v) -> p i m v", i=1, m=2)
                            Sb2 = [Sb, E8[:, 3072:4096].rearrange("p (i m v) -> p i m v", i=1, m=2)]
                            TAB = F8[:, 0:4096].bitcast(F32).rearrange("p (s a n) -> p s a n", s=2, a=2)
                        else:
                            Sf = E8[:, 0:4096].bitcast(F32).rearrange("p (i m v) -> p i m v", i=2, m=2)
                            Sb = F8[:, 0:2048].rearrange("p (i m v) -> p i m v", i=2, m=2)
                            TAB = F8[:, 2048:2560].bitcast(F32).rearrange("p (s a n) -> p s a n", s=1, a=2)
                            QM = F8[:, 2560:2816].rearrange("p (i m t) -> p i m t", i=2, m=2)
                        for h in range(4):
                            wq = W.get("ret_w_in", (j,), 0, 1024, h * 256, h * 256 + 256)
                            wk = W.get("ret_w_in", (j,), 0, 1024, 1024 + h * 256, 1024 + h * 256 + 256)
                            rti = 0
                            for bi, (b0, bn) in enumerate(blocks):
                                sl = bi % TAB.shape[1]
                                P.dma(TAB[:, sl, 0, :bn], rc_d[:, b0:b0 + bn], key=("tab", sl))
                                P.dma(TAB[:, sl, 1, :bn], rs_d[:, b0:b0 + bn], key=("tab", sl))
                                cos, sin = TAB[:, sl, 0, :bn], TAB[:, sl, 1, :bn]
                                for wv, dst in ((wq, QT), (wk, KT)):
                                    bk = []
                                    for m in range(2):
                                        b = ps()
                                        bk.append(banks[b][:, :bn])
                                        for k in range(8):
                                            P.mm(bk[m], wv[:, k, m * 128:(m + 1) * 128], XT[:, k, b0:b0 + bn], k == 0, k == 7)
                                    t = [RT[:, (rti % 2) * 4 + a, :bn] for a in range(4)]
                                    rti += 1
                                    P.tt("dve", t[0], bk[0], cos, ALU.mult)
                                    P.tt("dve", t[1], bk[1], sin, ALU.mult)
                                    P.tt("dve", t[2], bk[1], cos, ALU.mult)
                                    P.tt("dve", t[3], bk[0], sin, ALU.mult)
                                    P.tt("dve", dst[:, 0, b0:b0 + bn], t[0], t[1], ALU.subtract)
                                    P.tt("pool", dst[:, 1, b0:b0 + bn], t[2], t[3], ALU.add)
                            if nseq == 2:
                                for i in range(2):
                                    for m in range(2):
                                        P.tt("pool", QM[:, i, m, :], QT[:, m, 0:64], CMK[:, i, :], ALU.mult)
                            wv0 = W.get("ret_w_in", (j,), 0, 512, 2048 + h * 512, 2048 + h * 512 + 512)
                            wv1 = W.get("ret_w_in", (j,), 512, 1024, 2048 + h * 512, 2048 + h * 512 + 512)
                            lin_tm(xt_src, [wv0, wv1], 512, lambda ti, o: P.copy("act", V[:R, ti, :], o))
                            bcast_row(GB[:, 0:512], din["ret_gn"][j, h * 512:(h + 1) * 512], key="gb")
                            if kind == "p":
                                P.memset("pool", Sf[:, 0, :, :], 0.0)
                                P.memset("pool", Sb[:, 0, :, :], 0.0)
                            else:
                                for i in range(2):
                                    P.dma(Sf[:, i, :, :], din["state_ret"][j, i, h].rearrange("(m p) v -> p m v", p=128), key=("sin", i))
                                    P.copy("pool", Sb[:, i, :, :], Sf[:, i, :, :])
                            for c in range(NT):
                                cols = tcols(c)
                                sb_r = Sb if nseq == 2 else Sb2[c % 2]
                                sb_w = Sb if nseq == 2 else Sb2[(c + 1) % 2]
                                bT = ps()
                                ktv = bankb[bT][:, 0:256]
                                for m in range(2):
                                    P.tr(ktv[:R, m * 128:(m + 1) * 128], KT[:, m, cols], identb[:, :])
                                kds = []
                                for i in range(nseq):
                                    kd = xs()
                                    kcol = (4 + h) if nseq == 1 else (4 + 2 * h + i)
                                    P.act(kd[:R, 0:256], ktv[:R, 0:256], AF.Copy, scale=RD[:R, kcol:kcol + 1])
                                    kds.append(kd)
                                bA = ps()
                                att = banks[bA][:R, :R]
                                for m in range(2):
                                    P.mm(att, KT[:, m, cols], QT[:, m, cols], m == 0, m == 1)
                                at = xs()
                                P.tt("dve", at[:R, :R], att, MASKY[:R, h, :R], ALU.mult)
                                bO = ps()
                                o = banks[bO][:R, :512]
                                P.mm(o, at[:R, :R], V[:R, c, :], True, False)
                                if nseq == 1:
                                    for m in range(2):
                                        P.mm(o, QT[:, m, cols], sb_r[:, 0, m, :], False, m == 1)
                                else:
                                    for i in range(2):
                                        for m in range(2):
                                            P.mm(o, QM[:, i, m, :], sb_r[:, i, m, :], False, i == 1 and m == 1)
                                osb = tf()
                                st = stt_slot()
                                P.act(osb[:R, :], o, AF.Copy, scale=RD[:R, h:h + 1], accum=st[:R, 0:1])
                                for i in range(nseq):
                                    for m in range(2):
                                        sp_ = banks[4 + ((2 * i + m) % 4)][:, :512]
                                        P.mm(sp_, kds[i][:R, m * 128:(m + 1) * 128], V[:R, c, :], True, True)
                                        P.stt(Sf[:, i, m, :], Sf[:, i, m, :], cdec[h], sp_, ALU.mult, ALU.add)
                                for i in range(nseq):
                                    P.copy("act", sb_w[:, i, :, :], Sf[:, i, :, :])
                                P.act(JUNK[:R, 0:512], osb[:R, :], AF.Square, accum=st[:R, 1:2], junk=True)
                                P.ts("dve", st[:R, 2:3], st[:R, 0:1], 1.0 / 512, ALU.mult)
                                P.tt("dve", st[:R, 3:4], st[:R, 2:3], st[:R, 2:3], ALU.mult)
                                P.stt(st[:R, 4:5], st[:R, 1:2], 1.0 / 512, st[:R, 3:4], ALU.mult, ALU.subtract)
                                P.act(st[:R, 5:6], st[:R, 4:5], AF.Sqrt, bias=EPS)
                                P.recip(st[:R, 5:6], st[:R, 5:6])
                                P.stt(st[:R, 6:7], st[:R, 2:3], -1.0, st[:R, 5:6], ALU.mult, ALU.mult)
                                o2 = tf()
                                P.act(o2[:R, :], osb[:R, :], AF.Identity, scale=st[:R, 5:6], bias=st[:R, 6:7])
                                P.tt("dve", Y[:R, c, :], o2[:R, :], GB[:R, 0:512], ALU.mult)
                            od = dout["ret_p"] if kind == "p" else dout["ret_s"]
                            for i in range(nseq):
                                sq_ = s if kind == "p" else i
                                P.dma(od[j, sq_, h].rearrange("(m p) v -> p m v", p=128), Sf[:, i, :, :], key=("sout", i))
                            wg0 = W.get("ret_w_in", (j,), 0, 512, 4096 + h * 512, 4096 + h * 512 + 512)
                            wg1 = W.get("ret_w_in", (j,), 512, 1024, 4096 + h * 512, 4096 + h * 512 + 512)

                            def g_evac(ti, o):
                                g = xs()
                                P.act(g[:R, 0:512], o, AF.Silu)
                                P.tt("dve", Y[:R, ti, :], Y[:R, ti, :], g[:R, 0:512], ALU.mult)
                                return lambda: transpose_to(lambda cc: YT[:, cc, tcols(ti)], Y[:, ti, :], 4, ti, eng="dve")

                            lin_tm(xt_src, [wg0, wg1], 512, g_evac)
                            wo = [W.get("ret_w_out", (j,), h * 512, h * 512 + 512, hf * 512, hf * 512 + 512) for hf in range(2)]
                            for hf in range(2):
                                lin_tm(lambda i, ti: YT[:, i, tcols(ti)], [wo[hf]], 512,
                                       lambda ti, o, hf=hf: add_to_h(ti, hf * 512, 512, o))

                    def ffn(li):
                        norm_to_xt(din["norm_ffn"][li])
                        HT = A[:, 0:8192].rearrange("p (m t) -> p m t", m=4)
                        for g0 in range(0, 2816, 512):
                            g1 = min(2816, g0 + 512)
                            nch = (g1 - g0) // 128
                            for half in range(0, nch, 2):
                                c0 = g0 + half * 128
                                wg = W.get("ffn_w_gate", (li,), 0, 1024, c0, c0 + 256)
                                wu = W.get("ffn_w_up", (li,), 0, 1024, c0, c0 + 256)
                                for mm_ in range(2):
                                    m = half + mm_
                                    for (b0, bn) in blocks:
                                        bg, bu = ps(), ps()
                                        pg, pu = banks[bg][:, :bn], banks[bu][:, :bn]
                                        for k in range(8):
                                            P.mm(pg, wg[:, k, mm_ * 128:(mm_ + 1) * 128], XT[:, k, b0:b0 + bn], k == 0, k == 7)
                                        for k in range(8):
                                            P.mm(pu, wu[:, k, mm_ * 128:(mm_ + 1) * 128], XT[:, k, b0:b0 + bn], k == 0, k == 7)
                                        sg = tf()
                                        P.act(sg[:, :bn], pg, AF.Silu)
                                        P.tt("dve", HT[:, m, b0:b0 + bn], pu, sg[:, :bn], ALU.mult)
                            wd = [W.get("ffn_w_down", (li,), g0, g1, hf * 512, hf * 512 + 512) for hf in range(2)]
                            for hf in range(2):
                                lin_tm(lambda i, ti: HT[:, i, tcols(ti)], [wd[hf]], 512,
                                       lambda ti, o, hf=hf: add_to_h(ti, hf * 512, 512, o))

                    def pegate(li):
                        norm_to_xt(din["norm_pe"][li])
                        PT_ = F8[:, 0:4096].rearrange("p (m t) -> p m t", m=2)
                        pd = tokv(din["p_prompt"][li] if kind == "p" else din["p_sample"][li])
                        dq = Defer(2)
                        for ti in range(NT):
                            st_ = tf()
                            P.dma(st_[:R, 0:256], pd[ti * 128:ti * 128 + R, :], key=("pin", tfi[0] % 3))
                            xb = xs()
                            P.copy("pool", xb[:R, 0:256], st_[:R, 0:256])
                            dq.push(lambda ti=ti, xb=xb: transpose_to(lambda cc: PT_[:, cc, tcols(ti)], xb, 2, ti))
                        dq.flush()
                        for q4 in range(4):
                            wga = W.get("pe_w_gate", (li,), 0, 1024, q4 * 256, q4 * 256 + 256)
                            wp = W.get("pe_w_proj", (li,), 0, 256, q4 * 256, q4 * 256 + 256)
                            for ti in range(NT):
                                bg, bp = ps(), ps()
                                pg, pp = banks[bg][:R, :256], banks[bp][:R, :256]
                                for k in range(8):
                                    P.mm(pg, XT[:, k, tcols(ti)], wga[:, k, :], k == 0, k == 7)
                                for k in range(2):
                                    P.mm(pp, PT_[:, k, tcols(ti)], wp[:, k, :], k == 0, k == 1)
                                sg = tf()
                                P.act(sg[:R, 0:256], pg, AF.Sigmoid)
                                P.tt("dve", sg[:R, 256:512], pp, sg[:R, 0:256], ALU.mult)
                                P.tt("pool", H[:R, ti, q4 * 256:(q4 + 1) * 256], H[:R, ti, q4 * 256:(q4 + 1) * 256], sg[:R, 256:512], ALU.add)

                    def attn(h_q, st_mm, vext, dv, kind_mask, negf_col, fq_build, scale, evac):
                        def blk_post(stp, kt, krows, q0, n, tot_q0, mask, fqs):
                            pt = AT["PTB"][:, pti[0] % 2, :]
                            pti[0] += 1
                            c0 = q0 - tot_q0
                            bias = negf_col(kt, krows)
                            if fqs is not None:
                                if mask is not None:
                                    mw = mask.shape[1]
                                    tm = tf()
                                    P.tt("pool", tm[:krows, :mw], fqs[:krows, c0:c0 + mw], mask, ALU.add)
                                    P.tt("dve", stp[:, 0:mw], stp[:, 0:mw], tm[:krows, :mw], ALU.add)
                                    if n > mw:
                                        P.tt("dve", stp[:, mw:n], stp[:, mw:n], fqs[:krows, c0 + mw:c0 + n], ALU.add)
                                else:
                                    P.tt("dve", stp, stp, fqs[:krows, c0:c0 + n], ALU.add)
                                P.act(pt[:krows, c0:c0 + n], stp, AF.Exp, scale=scale, bias=bias)
                            else:
                                if mask is not None:
                                    mw = mask.shape[1]
                                    P.tt("dve", stp[:, 0:mw], stp[:, 0:mw], mask, ALU.add)
                                P.act(pt[:krows, c0:c0 + n], stp, AF.Exp, scale=scale)
                            return pt, c0

                        def blk_pv(ptc, kt, krows, accs, first, lastmap):
                            pt, c0 = ptc
                            for (jq, acc, qr) in accs:
                                if jq * 128 < c0:
                                    continue
                                P.mm(acc, pt[:krows, jq * 128:jq * 128 + qr], vext(kt, krows), first, lastmap[jq] == kt)

                        def run_blocks(descs, tot_q0, accs, lastmap, fqs, hook=None):
                            LOOK = 2
                            nb = len(descs)
                            sts = {}
                            for i in range(min(LOOK, nb)):
                                d = descs[i]
                                sts[i] = st_mm(d[0], d[1], d[2], d[3])
                            d = descs[0]
                            posts = {0: blk_post(sts.pop(0), d[0], d[1], d[2], d[3], tot_q0, d[4], fqs)}
                            for i in range(nb):
                                if i + LOOK < nb:
                                    d = descs[i + LOOK]
                                    sts[i + LOOK] = st_mm(d[0], d[1], d[2], d[3])
                                if i + 1 < nb:
                                    d = descs[i + 1]
                                    posts[i + 1] = blk_post(sts.pop(i + 1), d[0], d[1], d[2], d[3], tot_q0, d[4], fqs)
                                d = descs[i]
                                blk_pv(posts.pop(i), d[0], d[1], accs, d[5], lastmap)
                                if i == 1 and hook is not None:
                                    hook()

                        if kind == "p":
                            nxt = [fq_build(0) if fq_build else None]
                            for qb in range(4):
                                fqs = nxt[0]

                                def hook(qb=qb):
                                    if fq_build and qb + 1 < 4:
                                        nxt[0] = fq_build(qb + 1)

                                accs = [(jq, banks[4 + jq][:128, :dv + 1], 128) for jq in range(4)]
                                lastmap = {jq: 4 * qb + jq for jq in range(4)}
                                descs = []
                                for kt in range(4 * qb + 4):
                                    jmin = max(0, kt - 4 * qb)
                                    q0 = qb * 512 + jmin * 128
                                    mask = kind_mask if kt >= 4 * qb else None
                                    descs.append((kt, 128, q0, 512 - jmin * 128, mask, kt == 0))
                                run_blocks(descs, qb * 512, accs, lastmap, fqs, hook)
                                dq = Defer(2)
                                for jq in range(4):
                                    dq.push(evac(4 * qb + jq, accs[jq][1]))
                                dq.flush()
                        else:
                            fqs = fq_build(0) if fq_build else None
                            accs = [(0, banks[4][:64, :dv + 1], 64)]
                            lastmap = {0: 32}
                            descs = []
                            for kt in range(33):
                                if kt < 32:
                                    descs.append((kt, 128, 0, 64, NEGC[:, kt // 16, :], kt == 0))
                                else:
                                    descs.append((kt, 64, 0, 64, kind_mask[:64, :64], False))
                            run_blocks(descs, 0, accs, lastmap, fqs)
                            p_ = evac(0, accs[0][1])
                            if p_ is not None:
                                p_()

                    def mla(li):
                        norm_to_xt(din["norm_mix"][li])
                        if kind == "p":
                            AT["MCS"] = E8[:, 0:2048].bitcast(F32).rearrange("p (t a f) -> p t a f", a=2, f=32)
                            AT["PTB"] = E8[:, 2048:3072].rearrange("p (s n) -> p s n", s=2)
                            AT["ZB"] = F8[:, 0:2048].bitcast(F32).rearrange("p (s n) -> p s n", s=2)
                            AT["FQ"] = F8[:, 2048:4096].bitcast(F32).rearrange("p (s n) -> p s n", s=2)
                            P.dma(AT["MCS"][:, :, 0, :], din["c_mc_p"].rearrange("(t p) f -> p t f", p=128))
                            P.dma(AT["MCS"][:, :, 1, :], din["c_ms_p"].rearrange("(t p) f -> p t f", p=128))
                        MCS = AT["MCS"]
                        if kind == "p":
                            ex_tf = [XT[:, 4 + i // 2, (i % 2) * 1024:(i % 2) * 1024 + 1024].bitcast(F32) for i in range(3)]
                            ex_st_base = XT[:, 5, 1024:2048].bitcast(F32).rearrange("p (s e) -> p s e", e=8)
                            ex_st = [ex_st_base[:, i, :] for i in range(16)]
                            ex_xs = [XT[:, 6 + i // 2, (i % 2) * 1024:(i % 2) * 1024 + 1024] for i in range(4)]
                        else:
                            ex_tf = [AT["TFX"][:, i, :] for i in range(4)]
                            ex_st = [AT["STX"][:, i, :] for i in range(16)]
                            ex_xs = [AT["XSX"][:, i, :] for i in range(4)]
                        n_tf0, n_xs0, n_st0 = len(TFS), len(XSS), len(STS)
                        if kind == "p":
                            CQ = A[:, 0:8192].rearrange("p (k t) -> p k t", k=4)
                            LATT = B[:, 0:4096].rearrange("p (k t) -> p k t", k=2)
                            KRT = B[:, 4096:6144]
                            KNT = B[:, 6144:8192]
                            QNT = C8[:, 0:2048]
                            QRT = C8[:, 2048:4096]
                            VE = D8[:, 0:16 * 129].rearrange("p (t v) -> p t v", v=129)
                            OTG = XT[:, 0:4, :]
                        else:
                            CQ = C8[:, 0:256].rearrange("p (k t) -> p k t", k=4)
                            LATT = A[:, 0:2 * TK].rearrange("p (k t) -> p k t", k=2)
                            KRT = B[:, 0:TK]
                            KNT = B[:, TK:2 * TK]
                            QNT = C8[:, 256:320]
                            QRT = C8[:, 320:384]
                            VE = D8[:, 0:33 * 129].rearrange("p (t v) -> p t v", v=129)
                            OTG = C8[:, 512:768].rearrange("p (k t) -> p k t", k=4)
                        bcast_row(SMALL[:, 0:128], din["mla_gq_nope"][0], key="sm")
                        bcast_row(SMALL[:, 128:192], din["mla_gq_rope"][0], key="sm")
                        bcast_row(SMALL[:, 192:320], din["mla_gk_nope"][0], key="sm")
                        bcast_row(SMALL[:, 320:384], din["mla_gk_rope"][0], key="sm")
                        first = [True]

                        def rope_tm(dst_f32, src_f32, ti, t):
                            cs, sn = MCS[:R, ti, 0, :], MCS[:R, ti, 1, :]
                            P.tt("dve", t[:R, 0:32], src_f32[:, 0:32], cs, ALU.mult)
                            P.tt("dve", t[:R, 32:64], src_f32[:, 32:64], sn, ALU.mult)
                            P.tt("dve", t[:R, 64:96], src_f32[:, 32:64], cs, ALU.mult)
                            P.tt("dve", t[:R, 96:128], src_f32[:, 0:32], sn, ALU.mult)
                            P.tt("dve", dst_f32[:, 0:32], t[:R, 0:32], t[:R, 32:64], ALU.subtract)
                            P.tt("dve", dst_f32[:, 32:64], t[:R, 64:96], t[:R, 96:128], ALU.add)

                        def rms_tm(psum, n, gain, out, st, col):
                            P.act(JUNK[:psum.shape[0], 0:n], psum, AF.Square, accum=st[:psum.shape[0], col:col + 1], junk=True)
                            P.act(st[:psum.shape[0], col + 1:col + 2], st[:psum.shape[0], col:col + 1], AF.Sqrt, scale=1.0 / n, bias=EPS)
                            P.recip(st[:psum.shape[0], col + 1:col + 2], st[:psum.shape[0], col + 1:col + 2])
                            P.stt(out, psum, st[:psum.shape[0], col + 1:col + 2], gain, ALU.mult, ALU.mult)

                        key_new0 = (NKT - 1) * 128 if kind == "s" else 0

                        def kcols_new(ti):
                            return slice(key_new0 + ti * 128, key_new0 + ti * 128 + R)

                        bcast_row(GB[:, 0:512], din["mla_q_norm"][0], key="gb")
                        bcast_row(GB[:, 512:768], din["mla_kv_norm"][0], key="gb")

                        def cq_evac(ti, o):
                            st = stt_slot()
                            x = xs()
                            rms_tm(o, 512, GB[:R, 0:512], x[:R, 0:512], st, 0)
                            return lambda: transpose_to(lambda cc: CQ[:, cc, tcols(ti)], x, 4, ti)

                        w0 = W.get("mla_w_in", (0,), 0, 512, 0, 512)
                        w1 = W.get("mla_w_in", (0,), 512, 1024, 0, 512)
                        lin_tm(xt_src, [w0, w1], 512, cq_evac)

                        def kv_evac(ti, o):
                            st = stt_slot()
                            lat = tf()
                            rms_tm(o[:, 0:256], 256, GB[:R, 512:768], lat[:R, 0:256], st, 0)
                            rms_tm(o[:, 256:320], 64, SMALL[:R, 320:384], lat[:R, 320:384], st, 2)
                            rope_tm(lat[:R, 256:320], lat[:R, 320:384], ti, lat[:, 384:512])
                            ld = tokv(dout["lat_p"][0] if kind == "p" else dout["lat_s"][0])
                            kd_ = tokv(dout["kr_p"][0] if kind == "p" else dout["kr_s"][0])
                            P.dma(ld[ti * 128:ti * 128 + R, :], lat[:R, 0:256], key=("mo", tfi[0] % 3))
                            P.dma(kd_[ti * 128:ti * 128 + R, :], lat[:R, 256:320], key=("mo", tfi[0] % 3))
                            x = xs()
                            P.copy("pool", x[:R, 0:320], lat[:R, 0:320])

                            def pe_part():
                                b = ps()
                                pv = bankb[b]
                                for c in range(2):
                                    P.tr(pv[:, c * 128:c * 128 + R], x[:R, c * 128:(c + 1) * 128], identb[:R, :R])
                                P.tr(pv[0:64, 256:256 + R], x[:R, 256:320], identb[:R, :R])
                                for c in range(2):
                                    P.copy("act", LATT[:, c, kcols_new(ti)], pv[:, c * 128:c * 128 + R])
                                P.copy("act", KRT[0:64, kcols_new(ti)], pv[0:64, 256:256 + R])

                            return pe_part

                        w2 = W.get("mla_w_in", (0,), 0, 512, 512, 832)
                        w3 = W.get("mla_w_in", (0,), 512, 1024, 512, 832)
                        lin_tm(xt_src, [w2, w3], 320, kv_evac)
                        if kind == "s":
                            for i in range(2):
                                for q in range(16):
                                    kt = 16 * i + q
                                    st_ = tf()
                                    P.dma(st_[:, 0:256], din["cache_mla_latent"][0, i, q * 128:(q + 1) * 128, :], key=("pin", tfi[0] % 3))
                                    P.dma(st_[:, 256:320], din["cache_mla_krope"][0, i, q * 128:(q + 1) * 128, :], key=("pin", tfi[0] % 3))
                                    x = xs()
                                    P.copy("pool", x[:, 0:320], st_[:, 0:320])
                                    b = ps()
                                    pv = bankb[b]
                                    for c in range(2):
                                        P.tr(pv[:, c * 128:(c + 1) * 128], x[:, c * 128:(c + 1) * 128], identb[:, :])
                                    P.tr(pv[0:64, 256:384], x[:, 256:320], identb[:, :])
                                    for c in range(2):
                                        P.copy("act", LATT[:, c, kt * 128:(kt + 1) * 128], pv[:, c * 128:(c + 1) * 128])
                                    P.copy("act", KRT[0:64, kt * 128:(kt + 1) * 128], pv[0:64, 256:384])
                        P.memset("pool", VE[:, :, 128:129], 1.0)
                        P.memset("pool", KRT[64:128, :], 0.0)
                        P.memset("pool", QRT[64:128, :], 0.0)
                        scale = 192.0 ** -0.5
                        TFS.extend(ex_tf)
                        XSS.extend(ex_xs)
                        STS.extend(ex_st)
                        DEPTH[0] = 4
                        for hg in range(4):
                            W.la = 2
                            wkv = W.get("mla_w_kvb", (0,), 0, 256, hg * 1024, hg * 1024 + 1024)
                            for hh in range(4):
                                h = hg * 4 + hh
                                if hh % 2 == 0:
                                    wq2 = W.get("mla_w_qb", (0,), 0, 512, (hg * 4 + hh) * 192, (hg * 4 + hh) * 192 + 384)
                                wq = wq2[:, :, (hh % 2) * 192:(hh % 2) * 192 + 192]

                                def q_evac(ti, o):
                                    st = stt_slot()
                                    x = xs()
                                    rms_tm(o[:, 0:128], 128, SMALL[:R, 0:128], x[:R, 0:128], st, 0)
                                    qr = tf()
                                    rms_tm(o[:, 128:192], 64, SMALL[:R, 128:192], qr[:R, 128:192], st, 2)
                                    rope_tm(x[:R, 128:192], qr[:R, 128:192], ti, qr[:, 256:384])

                                    def pe_part():
                                        b = ps()
                                        pv = bankb[b]
                                        P.tr(pv[:, 0:R], x[:R, 0:128], identb[:R, :R])
                                        P.tr(pv[0:64, 128:128 + R], x[:R, 128:192], identb[:R, :R])
                                        P.copy("act", QNT[:, tcols(ti)], pv[:, 0:R])
                                        P.copy("act", QRT[0:64, tcols(ti)], pv[0:64, 128:128 + R])

                                    return pe_part

                                lin_tm(lambda i, ti: CQ[:, i, tcols(ti)], [wq], 192, q_evac)
                                dq = Defer(DEPTH[0])
                                for kt in range(NKT):
                                    kr_ = R if (kind == "s" and kt == 32) else 128
                                    b = ps()
                                    o = banks[b][:kr_, :256]
                                    for k in range(2):
                                        P.mm(o, LATT[:, k, kt * 128:kt * 128 + kr_], wkv[:, k, hh * 256:(hh + 1) * 256], k == 0, k == 1)
                                    st = stt_slot()
                                    x = xs()
                                    rms_tm(o[:, 0:128], 128, SMALL[:kr_, 192:320], x[:kr_, 0:128], st, 0)
                                    P.copy("act", VE[:kr_, kt, 0:128], o[:, 128:256])

                                    def pe_part(kt=kt, kr_=kr_, x=x):
                                        b2 = ps()
                                        pv = bankb[b2]
                                        P.tr(pv[:, 0:kr_], x[:kr_, 0:128], identb[:kr_, :kr_])
                                        P.copy("dve", KNT[:, kt * 128:kt * 128 + kr_], pv[:, 0:kr_])

                                    dq.push(pe_part)
                                dq.flush()

                                def st_mm(kt, krows, q0, n):
                                    b = ps()
                                    o = banks[b][:krows, :n]
                                    P.mm(o, KNT[:, kt * 128:kt * 128 + krows], QNT[:, q0:q0 + n], True, False)
                                    P.mm(o, KRT[:, kt * 128:kt * 128 + krows], QRT[:, q0:q0 + n], False, True)
                                    return o

                                def o_evac(ti, acc):
                                    st = stt_slot()
                                    rr = acc.shape[0]
                                    P.recip(st[:rr, 0:1], acc[:, 128:129])
                                    x = xs()
                                    P.ts("dve", x[:rr, 0:128], acc[:, 0:128], st[:rr, 0:1], ALU.mult)

                                    def pe_part():
                                        b = ps()
                                        pv = bankb[b]
                                        P.tr(pv[:, 0:rr], x[:rr, 0:128], identb[:rr, :rr])
                                        P.copy("act", OTG[:, hh, tcols(ti)], pv[:, 0:rr])

                                    return pe_part

                                attn(h, st_mm, lambda kt, kr: VE[:kr, kt, :], 128, NEGB[:, :], lambda kt, kr: None, None, scale, o_evac)
                            wo = [W.get("mla_w_out", (0,), hg * 512, hg * 512 + 512, hf * 512, hf * 512 + 512) for hf in range(2)]
                            W.la = LA
                            for hf in range(2):
                                lin_tm(lambda i, ti: OTG[:, i, tcols(ti)], [wo[hf]], 512,
                                       lambda ti, o, hf=hf: add_to_h(ti, hf * 512, 512, o))
                        del TFS[n_tf0:]
                        del XSS[n_xs0:]
                        del STS[n_st0:]
                        DEPTH[0] = 2

                    def fox(li):
                        norm_to_xt(din["norm_mix"][li])
                        key_new0 = (NKT - 1) * 128 if kind == "s" else 0
                        knew = NKT - 1 if kind == "s" else None
                        if kind == "p":
                            AT["FALL"] = A[:, 4224:4736].bitcast(F32).rearrange("p (t h) -> p t h", h=16)
                            AT["NEGF"] = A[:, 4736:5248].bitcast(F32).rearrange("p (t h) -> p t h", h=16)
                            AT["LF"] = A[:, 5248:5760].bitcast(F32).rearrange("p (t h) -> p t h", h=16)
                            AT["PTB"] = A[:, 6144:7168].rearrange("p (s n) -> p s n", s=2)
                            AT["ZB"] = F8[:, 0:2048].bitcast(F32).rearrange("p (s n) -> p s n", s=2)
                            AT["FQ"] = F8[:, 2048:4096].bitcast(F32).rearrange("p (s n) -> p s n", s=2)
                        if kind == "p":
                            AT["QZ"] = A[:, 7168:8192].rearrange("p (s n) -> p s n", s=2)
                        FALL, NEGF, LF, FQ = AT["FALL"], AT["NEGF"], AT["LF"], AT["FQ"]
                        QZ = AT["QZ"]

                        if kind == "p":
                            VE = A[:, 0:16 * 260].rearrange("p (t h v) -> p t h v", h=4, v=65)
                            KT = B[:, 0:4096].rearrange("p (m t) -> p m t", m=2)
                            OTOK = B[:, 4096:8192].rearrange("p (t v) -> p t v", v=256)
                            QT = C8[:, 0:4096].rearrange("p (m t) -> p m t", m=2)
                            G = D8[:, 0:4096].rearrange("p (t v) -> p t v", v=256)
                            OT = E8[:, 0:4096].rearrange("p (m t) -> p m t", m=2)
                        else:
                            VE = A[:, 0:33 * 260].rearrange("p (t h v) -> p t h v", h=4, v=65)
                            KT = B[:, 0:2 * TK].rearrange("p (m t) -> p m t", m=2)
                            OTOK = C8[:, 0:256].rearrange("p (t v) -> p t v", v=256)
                            QT = C8[:, 256:384].rearrange("p (m t) -> p m t", m=2)
                            G = C8[:, 512:768].rearrange("p (t v) -> p t v", v=256)
                            OT = C8[:, 1024:1152].rearrange("p (m t) -> p m t", m=2)
                        bcast_row(SMALL[:, 0:64], din["fox_gq"][0], key="sm")
                        bcast_row(SMALL[:, 64:128], din["fox_gk"][0], key="sm")
                        bcast_row(SMALL[:, 128:144], din["fox_b_f"][0], key="sm")
                        for a in range(4):
                            P.copy("pool", SMALL[:, 256 + a * 64:256 + (a + 1) * 64], SMALL[:, 0:64])
                        for a in range(4):
                            P.copy("pool", GB[:, a * 64:(a + 1) * 64], SMALL[:, 64:128])
                        GQ4 = SMALL[:, 256:512]
                        GK4 = GB[:, 0:256]
                        wf = W.get("fox_w_in", (0,), 0, 1024, 4096, 4112)
                        lq = NKT - 1 if kind == "s" else 0

                        def f_evac(ti, o):
                            t = tf()
                            P.tt("dve", t[:R, 0:16], o, SMALL[:R, 128:144], ALU.add)
                            P.act(t[:R, 16:32], t[:R, 0:16], AF.Exp, scale=-1.0)
                            P.act(t[:R, 32:48], t[:R, 16:32], AF.Ln, bias=1.0)
                            P.ts("dve", LF[:R, lq + ti, :], t[:R, 32:48], -1.0, ALU.mult)
                            fd = tokv(dout["flf_p"][0] if kind == "p" else dout["flf_s"][0])
                            P.dma(fd[ti * 128:ti * 128 + R, :], LF[:R, lq + ti, :], key="lfo")

                        lin_tm(xt_src, [wf], 16, f_evac)
                        if kind == "s":
                            for i in range(2):
                                P.dma(LF[:, 16 * i:16 * i + 16, :], din["cache_fox_logf"][0, i].rearrange("(t p) h -> p t h", p=128), key=("lfi", i))
                        for kt in range(NKT):
                            b = ps()
                            if kind == "s" and kt == 32:
                                o = banks[b][:64, :16]
                                P.mm(o, UBLK[:64, :64], LF[:64, kt, :], True, False)
                                P.mm(o, SEL[:, 1, 0:64], FALL[:, 15, :], False, False)
                                P.mm(o, SEL[:, 2, 0:64], FALL[:, 31, :], False, True)
                                rr = 64
                            else:
                                o = banks[b][:128, :16]
                                chain = (kt % 16 != 0) if kind == "s" else (kt != 0)
                                P.mm(o, UT[:, :], LF[:, kt, :], True, not chain)
                                if chain:
                                    P.mm(o, SEL[:, 0, :], FALL[:, kt - 1, :], False, True)
                                rr = 128
                            P.copy("dve", FALL[:rr, kt, :], o)
                            P.ts("dve", NEGF[:rr, kt, :], o, -1.0, ALU.mult)
                        for g in range(4):

                            def qk_norm(o, gain4, outf, extra_scale):
                                st = stt_slot()
                                sq = tf()
                                P.act(sq[:R, 0:256], o, AF.Square)
                                P.op("dve", lambda e: e.tensor_reduce(out=st[:R, 0:4], in_=sq[:R, 0:256].rearrange("p (h d) -> p h d", d=64),
                                                                     axis=AX.X, op=ALU.add),
                                     [sq[:R, 0:256]], [st[:R, 0:4]])
                                P.act(st[:R, 4:8], st[:R, 0:4], AF.Sqrt, scale=1.0 / 64, bias=EPS)
                                P.recip(st[:R, 4:8], st[:R, 4:8])
                                if extra_scale != 1.0:
                                    P.ts("dve", st[:R, 4:8], st[:R, 4:8], extra_scale, ALU.mult)
                                for a in range(4):
                                    P.stt(outf[:, a * 64:(a + 1) * 64], o[:, a * 64:(a + 1) * 64], st[:R, 4 + a:5 + a],
                                          gain4[:R, a * 64:(a + 1) * 64], ALU.mult, ALU.mult)

                            def q_evac(ti, o):
                                x = xs()
                                qk_norm(o, GQ4, x[:R, 0:256], 0.125)
                                return lambda: transpose_to(lambda cc: QT[:, cc, tcols(ti)], x, 2, ti)

                            wq = W.get("fox_w_in", (0,), 0, 1024, g * 256, g * 256 + 256)
                            lin_tm(xt_src, [wq], 256, q_evac)

                            def k_evac(ti, o):
                                kf = tf()
                                qk_norm(o, GK4, kf[:R, 0:256], 1.0)
                                kd_ = tokv((dout["fk_p"][0] if kind == "p" else dout["fk_s"][0]).rearrange("b t h d -> b t (h d)"))
                                P.dma(kd_[ti * 128:ti * 128 + R, g * 256:(g + 1) * 256], kf[:R, 0:256], key=("ko", tfi[0] % 3))
                                x = xs()
                                P.copy("pool", x[:R, 0:256], kf[:R, 0:256])
                                return lambda: transpose_to(lambda cc: KT[:, cc, key_new0 + ti * 128:key_new0 + ti * 128 + R], x, 2, ti)

                            wk = W.get("fox_w_in", (0,), 0, 1024, 1024 + g * 256, 1024 + g * 256 + 256)
                            lin_tm(xt_src, [wk], 256, k_evac)
                            P.memset("pool", VE[:, :, :, 64:65], 1.0)

                            def v_evac(ti, o):
                                vf = tf()
                                P.copy("act", vf[:R, 0:256], o)
                                vd = tokv((dout["fv_p"][0] if kind == "p" else dout["fv_s"][0]).rearrange("b t h d -> b t (h d)"))
                                P.dma(vd[ti * 128:ti * 128 + R, g * 256:(g + 1) * 256], vf[:R, 0:256], key=("vo", tfi[0] % 3))
                                P.copy("pool", VE[:R, lq + ti, :, 0:64], vf[:R, 0:256].rearrange("p (h d) -> p h d", d=64))

                            wv = W.get("fox_w_in", (0,), 0, 1024, 2048 + g * 256, 2048 + g * 256 + 256)
                            lin_tm(xt_src, [wv], 256, v_evac)
                            if kind == "s":
                                for i in range(2):
                                    for q in range(16):
                                        kt = 16 * i + q
                                        st_ = tf()
                                        P.dma(st_[:, 0:256], din["cache_fox_k"][0, i, q * 128:(q + 1) * 128, 4 * g:4 * g + 4, :].rearrange("p h d -> p (h d)"),
                                              key=("pin", tfi[0] % 3))
                                        P.dma(st_[:, 256:512], din["cache_fox_v"][0, i, q * 128:(q + 1) * 128, 4 * g:4 * g + 4, :].rearrange("p h d -> p (h d)"),
                                              key=("pin", tfi[0] % 3))
                                        x = xs()
                                        P.copy("pool", x[:, 0:256], st_[:, 0:256])
                                        P.copy("pool", VE[:, kt, :, 0:64], st_[:, 256:512].rearrange("p (h d) -> p h d", d=64))
                                        b = ps()
                                        pv = bankb[b]
                                        for c in range(2):
                                            P.tr(pv[:, c * 128:(c + 1) * 128], x[:, c * 128:(c + 1) * 128], identb[:, :])
                                        for c in range(2):
                                            P.copy("act", KT[:, c, kt * 128:(kt + 1) * 128], pv[:, c * 128:(c + 1) * 128])
                            wg_ = W.get("fox_w_in", (0,), 0, 1024, 3072 + g * 256, 3072 + g * 256 + 256)
                            lin_tm(xt_src, [wg_], 256, lambda ti, o: P.act(G[:R, ti, :], o, AF.Sigmoid))
                            for hh in range(4):
                                h = g * 4 + hh
                                pr = slice((hh % 2) * 64, (hh % 2) * 64 + 64)
                                mc = hh // 2

                                orow = slice(64, 128) if hh % 2 == 0 else slice(0, 64)
                                P.memset("pool", QZ[orow, :, :], 0.0)

                                def fq_build(qb):
                                    qz = QZ[:, qb % 2, :]
                                    wq_ = 512 if kind == "p" else 64
                                    P.copy("act", qz[pr, 0:wq_], QT[pr, mc, qb * 512:qb * 512 + wq_])
                                    fq = FQ[:, qb % 2, :]
                                    b = ps()
                                    for jq in range(len(blocks[0:1]) * (4 if kind == "p" else 1)):
                                        qt = (4 * qb + jq) if kind == "p" else 32
                                        bm = tf()
                                        P.ts("dve", bm[:R, 0:R], identf[:R, :R], FALL[:R, qt, h:h + 1], ALU.mult)
                                        P.mm(banks[b][:, jq * 128:jq * 128 + R], onesf[:R, :], bm[:R, 0:R], True, True)
                                    wdt = 512 if kind == "p" else 64
                                    P.copy("act", fq[:, 0:wdt], banks[b][:, 0:wdt])
                                    return fq

                                def st_mm(kt, krows, q0, n):
                                    b = ps()
                                    o = banks[b][:krows, :n]
                                    P.mm(o, KT[:, mc, kt * 128:kt * 128 + krows], QZ[:, (q0 // 512) % 2, (q0 % 512):(q0 % 512) + n], True, True)
                                    return o

                                def o_evac(ti, acc):
                                    st = stt_slot()
                                    rr = acc.shape[0]
                                    P.recip(st[:rr, 0:1], acc[:, 64:65])
                                    P.stt(OTOK[:rr, ti, hh * 64:(hh + 1) * 64], acc[:, 0:64], st[:rr, 0:1], G[:rr, ti, hh * 64:(hh + 1) * 64],
                                          ALU.mult, ALU.mult)

                                attn(h, st_mm, lambda kt, kr: VE[:kr, kt, hh, :], 64, NEGA[:, :],
                                     lambda kt, kr: NEGF[:kr, kt, h:h + 1], fq_build, 1.0, o_evac)
                            for ti in range(NT):
                                transpose_to(lambda cc: OT[:, cc, tcols(ti)], OTOK[:, ti, :], 2, ti)
                            wo = W.get("fox_w_out", (0,), g * 256, g * 256 + 256, 0, 1024)
                            for hf in range(2):
                                lin_tm(lambda i, ti: OT[:, i, tcols(ti)], [wo[:, :, hf * 512:(hf + 1) * 512]], 512,
                                       lambda ti, o, hf=hf: add_to_h(ti, hf * 512, 512, o))

                    for li in range(DBG_LAYERS):
                        kind_m = li % 3
                        if kind_m == 0:
                            norm_to_xt(din["norm_mix"][li])
                            retention(li // 3)
                        elif kind_m == 1:
                            mla(li)
                        else:
                            fox(li)
                        ffn(li)
                        pegate(li)
                    norm_stats(din["norm_final"])
                    yout = dout["y_prompt"][s] if kind == "p" else dout["y_sample"].rearrange("b t d -> (b t) d")
                    for ti in range(NT):
                        for hf in range(2):
                            t = tf()
                            P.stt(t[:R, :], H[:R, ti, hf * 512:(hf + 1) * 512], RS[:R, ti:ti + 1], GB[:R, hf * 512:(hf + 1) * 512], ALU.mult, ALU.mult)
                            P.dma(yout[ti * 128:ti * 128 + R, hf * 512:(hf + 1) * 512], t[:R, :], key=("yo", tfi[0] % 3))

                if kind == "p":
                    for s in range(int(os.environ.get("MK_PSEQ", "2"))):
                        run_pass(s)
                else:
                    run_pass(0)
                P.barrier()

        if "p" in DBG_PASSES:
            run_kind("p")
        if "s" in DBG_PASSES:
            run_kind("s")
        P.barrier()
        return W.rec, P.nops


_CACHE = {}


def _build():
    if "nc" in _CACHE:
        return _CACHE["nc"], _CACHE["cst"]
    CST = _consts()
    nc0 = bass.Bass("TRN2", target_bir_lowering=False)
    specs, _ = emit(nc0, True, None, CST)
    nc = bass.Bass("TRN2", target_bir_lowering=False)
    _, nops = emit(nc, False, specs, CST)
    _CACHE["nc"] = nc
    _CACHE["cst"] = CST
    _CACHE["nops"] = nops
    return nc, CST


def kernel(**inputs):
    nc, CST = _build()
    in_maps = []
    for c in range(8):
        m = {}
        for n in IN_SHAPES:
            a = np.asarray(inputs[n], dtype=np.float32)
            if n in SHARDED:
                ax = SHARDED[n]
                sl = [slice(None)] * a.ndim
                sl[ax] = slice(2 * c, 2 * c + 2)
                a = a[tuple(sl)]
            m[n] = np.ascontiguousarray(a)
        for n in CONST_NAMES:
            m["c_" + n] = np.ascontiguousarray(CST[n], dtype=np.float32)
        in_maps.append(m)
    res = run_bass_kernel_spmd(nc, in_maps, core_ids=list(range(8)))
    outs = []
    for n, shp, ax in OUT_SHAPES:
        outs.append(np.concatenate([np.asarray(res.results[c][n], dtype=np.float32) for c in range(8)], axis=ax))
    return tuple(outs)
```

```python
import numpy as np
import concourse.bass as bass
import concourse.mybir as mybir
from concourse.bass_utils import run_bass_kernel_spmd
from contextlib import ExitStack

F32 = mybir.dt.float32
BF = mybir.dt.bfloat16
AF = mybir.ActivationFunctionType
ALU = mybir.AluOpType
AX = mybir.AxisListType

D = 1024
EPS = 1e-6
NEG = -30000.0
NWB = 5
LA = 3
import os
DBG_LAYERS = int(os.environ.get("MK_LAYERS", "4"))
DBG_PASSES = os.environ.get("MK_PASSES", "ps")


def _esz(dt):
    return 2 if dt == BF else 4


class Defer:
    def __init__(self, depth=2):
        self.q = []
        self.depth = depth

    def push(self, fn):
        if fn is not None:
            self.q.append(fn)
        while len(self.q) > self.depth:
            self.q.pop(0)()

    def flush(self):
        while self.q:
            self.q.pop(0)()


class Prog:
    def __init__(self, nc, es, dry):
        self.nc = nc
        self.es = es
        self.dry = dry
        self.E = dict(pe=nc.tensor, act=nc.scalar, dve=nc.vector, pool=nc.gpsimd, sp=nc.sync)
        self.semh = {}
        self.cnt = {}
        for k in ("pe", "act", "dve", "pool"):
            self.semh[k] = es.enter_context(nc.semaphore("s_" + k))
            self.cnt[k] = 0
        self.seen = {k: {} for k in self.E}
        self.ent = {}
        self.open = {}
        self.nops = 0

    def dsem(self, key):
        k = ("d", key)
        if k not in self.semh:
            self.semh[k] = self.es.enter_context(self.nc.semaphore("d%d" % len(self.semh)))
            self.cnt[k] = 0
        return k

    @staticmethod
    def box(ap, exact=False):
        t = ap.tensor
        tn = type(t).__name__
        if not (tn.startswith("SB") or tn.startswith("PSum")):
            return None
        if tn.startswith("PSum") and not exact:
            return (t.name, 0, 128, 0, 2048)
        pairs = ap.ap
        pstep, pn = pairs[0]
        sp = ap.start_partition
        if callable(sp):
            sp = sp()
        off = ap.offset - sp * pstep
        lo = hi = off
        for st, c in pairs[1:]:
            if st >= 0:
                hi += st * (c - 1)
            else:
                lo += st * (c - 1)
        e = _esz(ap.dtype)
        return (t.name, sp, sp + pn, lo * e, (hi + 1) * e)

    def op(self, eng, fn, reads=(), writes=(), dkey=None):
        if self.dry:
            return
        self.nops += 1
        need = {}
        rb = [b for b in (self.box(a) for a in reads) if b]
        wb = [b for b in (self.box(a) for a in writes) if b]
        if eng != "pe":
            for b in (x for x in (self.box(a, exact=True) for a in reads) if x):
                if b[0] in self.open:
                    self.open[b[0]] = [ob for ob in self.open[b[0]]
                                       if not (ob[1] < b[2] and b[1] < ob[2] and ob[3] < b[4] and b[3] < ob[4])]
        for b in rb:
            for e in self.ent.get(b[0], ()):
                if e[1] and e[0][1] < b[2] and b[1] < e[0][2] and e[0][3] < b[4] and b[3] < e[0][4]:
                    for k, v in e[2].items():
                        if need.get(k, 0) < v:
                            need[k] = v
        for b in wb:
            for e in self.ent.get(b[0], ()):
                if e[0][1] < b[2] and b[1] < e[0][2] and e[0][3] < b[4] and b[3] < e[0][4]:
                    for k, v in e[2].items():
                        if need.get(k, 0) < v:
                            need[k] = v
        E = self.E[eng]
        seen = self.seen[eng]
        for k, v in need.items():
            if k == eng and eng == "pe":
                continue
            if seen.get(k, 0) >= v:
                continue
            E.wait_ge(self.semh[k], v)
            seen[k] = v
        ins = fn(E)
        if dkey is not None:
            k = self.dsem(dkey)
            self.cnt[k] += 16
            ins.then_inc(self.semh[k], 16)
        else:
            k = eng
            self.cnt[k] += 1
            ins.then_inc(self.semh[k], 1)
        tok = {k: self.cnt[k]}
        for b in wb:
            L = self.ent.setdefault(b[0], [])
            L[:] = [e for e in L if not (b[1] <= e[0][1] and e[0][2] <= b[2] and b[3] <= e[0][3] and e[0][4] <= b[4])]
            L.append((b, True, tok))
        for b in rb:
            L = self.ent.setdefault(b[0], [])
            for e in L:
                if (not e[1]) and e[0] == b:
                    for kk, vv in tok.items():
                        if e[2].get(kk, 0) < vv:
                            e[2][kk] = vv
                    break
            else:
                L.append((b, False, dict(tok)))

    def barrier(self):
        if self.dry:
            return
        for eng, E in self.E.items():
            seen = self.seen[eng]
            for k, v in self.cnt.items():
                if v > seen.get(k, 0):
                    E.wait_ge(self.semh[k], v)
                    seen[k] = v
        self.ent.clear()

    def _pe_open(self, out, start):
        if self.dry:
            return
        b = self.box(out, exact=True)
        L = self.open.setdefault(b[0], [])
        if start:
            for ob in L:
                if ob != b and ob[1] < b[2] and b[1] < ob[2] and ob[3] < b[4] and b[3] < ob[4]:
                    raise RuntimeError("PSUM overwrite of un-evacuated group %s by %s" % (ob, b))
            if b not in L:
                L.append(b)

    def mm(self, out, lhsT, rhs, start, stop):
        self._pe_open(out, start)
        self.op("pe", lambda e: e.matmul(out, lhsT=lhsT, rhs=rhs, start=start, stop=stop), [lhsT, rhs], [out])

    def tr(self, out, in_, ident):
        self._pe_open(out, True)
        self.op("pe", lambda e: e.transpose(out=out, in_=in_, identity=ident), [in_, ident], [out])

    def act(self, out, in_, func, scale=None, bias=None, accum=None, junk=False):
        kw = {}
        rd = [in_]
        if scale is not None:
            kw["scale"] = scale
            if not isinstance(scale, (int, float)):
                rd.append(scale)
        if bias is not None:
            kw["bias"] = bias
            if not isinstance(bias, (int, float)):
                rd.append(bias)
        wr = [out]
        if accum is not None:
            kw["accum_out"] = accum
            wr.append(accum)
        self.op("act", lambda e: e.activation(out=out, in_=in_, func=func, **kw), rd, wr)

    def tt(self, eng, out, in0, in1, op):
        self.op(eng, lambda e: e.tensor_tensor(out=out, in0=in0, in1=in1, op=op), [in0, in1], [out])

    def ts(self, eng, out, in0, s1, op0, s2=None, op1=None):
        rd = [in0] + [s for s in (s1, s2) if s is not None and not isinstance(s, (int, float))]
        if op1 is None:
            self.op(eng, lambda e: e.tensor_scalar(out=out, in0=in0, scalar1=s1, scalar2=None, op0=op0), rd, [out])
        else:
            self.op(eng, lambda e: e.tensor_scalar(out=out, in0=in0, scalar1=s1, scalar2=s2, op0=op0, op1=op1), rd, [out])

    def stt(self, out, in0, sc, in1, op0, op1):
        rd = [in0, in1] + ([] if isinstance(sc, (int, float)) else [sc])
        self.op("dve", lambda e: e.scalar_tensor_tensor(out=out, in0=in0, scalar=sc, in1=in1, op0=op0, op1=op1), rd, [out])

    def copy(self, eng, out, in_):
        if eng == "act":
            self.op("act", lambda e: e.copy(out=out, in_=in_), [in_], [out])
        else:
            self.op(eng, lambda e: e.tensor_copy(out=out, in_=in_), [in_], [out])

    def recip(self, out, in_):
        self.op("dve", lambda e: e.reciprocal(out=out, in_=in_), [in_], [out])

    def memset(self, eng, ap, val):
        self.op(eng, lambda e: e.memset(ap, val), [], [ap])

    def dma(self, out, in_, key=None, q="sp"):
        if self.dry:
            return
        bo, bi = self.box(out), self.box(in_)
        if bo is not None:
            key = ("i", bo[0], bo[3])
        else:
            key = ("o", bi[0], bi[3])
        self.op(q, lambda e: e.dma_start(out=out, in_=in_), [in_], [out], dkey=key)


def _consts():
    c = {}
    i128 = np.arange(128)
    c["ident"] = np.eye(128, dtype=np.float32)
    c["ones"] = np.ones((128, 128), np.float32)
    gam = 1.0 - 2.0 ** (-5.0 - np.arange(4))
    lg = np.log(gam)
    s = i128[:, None].astype(np.float64)
    t = i128[None, :].astype(np.float64)
    mk = np.zeros((128, 4, 128), np.float64)
    for h in range(4):
        mk[:, h, :] = np.where(s <= t, np.exp(lg[h] * (-(s + 1.0))), 0.0) / 16.0
    c["maskY_p"] = mk.astype(np.float32)
    rd = np.zeros((128, 16), np.float64)
    for h in range(4):
        rd[:, h] = np.exp(lg[h] * (i128 + 1.0))
        rd[:, 4 + h] = np.exp(lg[h] * (127.0 - i128)) / 16.0
    c["rdec_p"] = rd.astype(np.float32)
    c["cdec_p"] = [float(np.exp(lg[h] * 128.0)) for h in range(4)]
    i64 = np.arange(64)
    sq = i64 // 32
    sl = (i64 % 32).astype(np.float64)
    same = sq[:, None] == sq[None, :]
    mk = np.zeros((128, 4, 128), np.float64)
    for h in range(4):
        mk[:64, h, :64] = np.where(same & (sl[:, None] <= sl[None, :]), np.exp(lg[h] * (-(sl[:, None] + 1.0))), 0.0) / 16.0
    c["maskY_s"] = mk.astype(np.float32)
    rd = np.zeros((128, 16), np.float64)
    for h in range(4):
        rd[:64, h] = np.exp(lg[h] * (sl + 1.0))
        for i in range(2):
            rd[:64, 4 + 2 * h + i] = np.where(sq == i, np.exp(lg[h] * (31.0 - sl)) / 16.0, 0.0)
    c["rdec_s"] = rd.astype(np.float32)
    c["cdec_s"] = [float(np.exp(lg[h] * 32.0)) for h in range(4)]
    cm = np.zeros((128, 2, 64), np.float32)
    cm[:, 0, :32] = 1.0
    cm[:, 1, 32:] = 1.0
    c["colmask_s"] = cm

    def rope_tab(pos, half):
        inv = (np.float32(10000.0) ** (-np.arange(half, dtype=np.float32) / np.float32(half))).astype(np.float32)
        ang = (pos.astype(np.float32)[:, None] * inv[None, :]).astype(np.float32)
        return np.cos(ang.astype(np.float64)).astype(np.float32), np.sin(ang.astype(np.float64)).astype(np.float32)

    pos_p = np.arange(2048)
    pos_s = np.concatenate([2048 + np.arange(32), 2048 + np.arange(32)])
    cs, sn = rope_tab(pos_p, 128)
    c["rc_p"] = np.ascontiguousarray(cs.T)
    c["rs_p"] = np.ascontiguousarray(sn.T)
    cs, sn = rope_tab(pos_s, 128)
    c["rc_s"] = np.ascontiguousarray(cs.T)
    c["rs_s"] = np.ascontiguousarray(sn.T)
    cs, sn = rope_tab(pos_p, 32)
    c["mc_p"] = cs
    c["ms_p"] = sn
    cs, sn = rope_tab(pos_s, 32)
    c["mc_s"] = cs
    c["ms_s"] = sn
    c["negtri"] = np.where(i128[:, None] <= i128[None, :], 0.0, NEG).astype(np.float32)
    c["negchunk"] = np.where((i128[:, None] // 64) <= (i128[None, :] // 64), 0.0, NEG).astype(np.float32)
    nc_ = np.zeros((128, 2, 64), np.float32)
    nc_[:, 0, 32:] = NEG
    nc_[:, 1, :32] = NEG
    c["negcol"] = nc_
    a = np.full((128, 128), 0.0, np.float32)
    a[:64, :64] = np.where(same & (sl[:, None] <= sl[None, :]), 0.0, NEG)
    c["fox_negnew"] = a
    a = np.full((128, 128), 0.0, np.float32)
    a[:64, :64] = np.where(same, 0.0, NEG)
    c["mla_negnew"] = a
    c["U"] = (i128[:, None] <= i128[None, :]).astype(np.float32)
    a = np.zeros((128, 128), np.float32)
    a[:64, :64] = (same & (sl[:, None] <= sl[None, :])).astype(np.float32)
    c["Ublk"] = a
    sel = np.zeros((128, 3, 128), np.float32)
    sel[127, 0, :] = 1.0
    sel[127, 1, :32] = 1.0
    sel[127, 2, 32:64] = 1.0
    c["SEL"] = sel
    return c


CONST_NAMES = ["ident", "ones", "maskY_p", "rdec_p", "maskY_s", "rdec_s", "colmask_s", "rc_p", "rs_p", "rc_s", "rs_s",
               "mc_p", "ms_p", "mc_s", "ms_s", "negtri", "negchunk", "negcol", "fox_negnew", "mla_negnew", "U", "Ublk", "SEL"]

IN_SHAPES = dict(
    x_prompt=(2, 2048, 1024), x_sample=(2, 32, 1024), state_ret=(2, 2, 4, 256, 512),
    cache_mla_latent=(1, 2, 2048, 256), cache_mla_krope=(1, 2, 2048, 64),
    cache_fox_k=(1, 2, 2048, 16, 64), cache_fox_v=(1, 2, 2048, 16, 64), cache_fox_logf=(1, 2, 2048, 16),
    p_prompt=(4, 2, 2048, 256), p_sample=(4, 2, 32, 256),
    norm_mix=(4, 1024), norm_ffn=(4, 1024), norm_pe=(4, 1024), norm_final=(1024,),
    ret_w_in=(2, 1024, 6144), ret_gn=(2, 2048), ret_w_out=(2, 2048, 1024),
    mla_w_in=(1, 1024, 832), mla_q_norm=(1, 512), mla_kv_norm=(1, 256), mla_w_qb=(1, 512, 3072),
    mla_w_kvb=(1, 256, 4096), mla_gq_nope=(1, 128), mla_gq_rope=(1, 64), mla_gk_nope=(1, 128), mla_gk_rope=(1, 64),
    mla_w_out=(1, 2048, 1024), fox_w_in=(1, 1024, 4112), fox_b_f=(1, 16), fox_gq=(1, 64), fox_gk=(1, 64),
    fox_w_out=(1, 1024, 1024), ffn_w_gate=(4, 1024, 2816), ffn_w_up=(4, 1024, 2816), ffn_w_down=(4, 2816, 1024),
    pe_w_proj=(4, 256, 1024), pe_w_gate=(4, 1024, 1024))
SHARDED = dict(x_prompt=0, x_sample=0, state_ret=1, cache_mla_latent=1, cache_mla_krope=1, cache_fox_k=1,
               cache_fox_v=1, cache_fox_logf=1, p_prompt=1, p_sample=1)
OUT_SHAPES = [
    ("y_prompt", (2, 2048, 1024), 0), ("y_sample", (2, 32, 1024), 0),
    ("ret_p", (2, 2, 4, 256, 512), 1), ("ret_s", (2, 2, 4, 256, 512), 1),
    ("lat_p", (1, 2, 2048, 256), 1), ("kr_p", (1, 2, 2048, 64), 1), ("lat_s", (1, 2, 32, 256), 1), ("kr_s", (1, 2, 32, 64), 1),
    ("fk_p", (1, 2, 2048, 16, 64), 1), ("fv_p", (1, 2, 2048, 16, 64), 1), ("flf_p", (1, 2, 2048, 16), 1),
    ("fk_s", (1, 2, 32, 16, 64), 1), ("fv_s", (1, 2, 32, 16, 64), 1), ("flf_s", (1, 2, 32, 16), 1)]


def emit(nc, dry, specs, CST):
    din = {}
    for n, shp in IN_SHAPES.items():
        din[n] = nc.dram_tensor(n, list(shp), F32, kind="ExternalInput").ap()
    for n in CONST_NAMES:
        din["c_" + n] = nc.dram_tensor("c_" + n, list(CST[n].shape), F32, kind="ExternalInput").ap()
    dout = {}
    for n, shp, _ in OUT_SHAPES:
        dout[n] = nc.dram_tensor(n, list(shp), F32, kind="ExternalOutput").ap()

    gs = ExitStack()
    with gs:
        P = Prog(nc, gs, dry)

        def sb(name, shape, dt, st=gs):
            return st.enter_context(nc.sbuf_tensor(name, list(shape), dt))

        WB = sb("WB", [128, NWB, 2048], BF)
        GB = sb("GB", [128, 1024], F32)
        TF = sb("TF", [128, 3, 512], F32)
        XS = sb("XS", [128, 3, 1024], BF)
        JUNK = sb("JUNK", [128, 1024], BF)
        ST = sb("ST", [128, 4, 8], F32)
        SS = sb("SS", [128, 16], F32)
        RS = sb("RS", [128, 16], F32)
        identf = sb("identf", [128, 128], F32)
        identb = sb("identb", [128, 128], BF)
        onesf = sb("onesf", [128, 128], F32)
        MASKY = sb("MASKY", [128, 4, 128], F32)
        RD = sb("RD", [128, 16], F32)
        NEGA = sb("NEGA", [128, 128], F32)
        NEGB = sb("NEGB", [128, 128], F32)
        NEGC = sb("NEGC", [128, 2, 64], F32)
        UT = sb("UT", [128, 128], F32)
        UBLK = sb("UBLK", [128, 128], F32)
        SEL = sb("SEL", [128, 3, 128], F32)
        CMK = sb("CMK", [128, 2, 64], BF)
        SMALL = sb("SMALL", [128, 512], F32)
        banks = [gs.enter_context(nc.psum_tensor("ps%d" % i, [128, 512], F32)) for i in range(8)]
        bankb = [b[:, :].bitcast(BF) for b in banks]
        psi = [0]

        def ps():
            i = psi[0] % 4
            psi[0] += 1
            return i

        tfi = [0]
        TFS = [TF[:, i, :] for i in range(3)]
        XSS = [XS[:, i, :] for i in range(3)]
        STS = [ST[:, i, :] for i in range(4)]
        DEPTH = [2]

        def tf():
            tfi[0] += 1
            return TFS[tfi[0] % len(TFS)]

        xsi = [0]

        def xs():
            xsi[0] += 1
            return XSS[xsi[0] % len(XSS)]

        sti = [0]

        def stt_slot():
            sti[0] += 1
            return STS[sti[0] % len(STS)]

        class WStream:
            def __init__(self):
                self.i = 0
                self.issued = 0
                self.nst = 0
                self.rec = []
                self.la = LA

            def get(self, name, pre, r0, r1, c0, c1):
                kc = (r1 - r0) // 128
                n = c1 - c0
                assert kc * 128 == r1 - r0 and kc * n <= 2048, (name, r0, r1, c0, c1)
                i = self.i
                self.i += 1
                view = WB[:, i % NWB, 0:kc * n].rearrange("p (k n) -> p k n", n=n)
                spec = (name, pre, r0, r1, c0, c1)
                if dry:
                    self.rec.append(spec)
                    return view
                assert specs[i] == spec, (i, specs[i], spec)
                while self.issued < min(len(specs), i + 1 + self.la):
                    self._issue(self.issued)
                    self.issued += 1
                return view

            def _issue(self, j):
                name, pre, r0, r1, c0, c1 = specs[j]
                kc = (r1 - r0) // 128
                n = c1 - c0
                Wd = din[name]
                for ix in pre:
                    Wd = Wd[ix]
                src = Wd[r0:r1, c0:c1].rearrange("(k p) n -> p k n", p=128)
                dst = WB[:, j % NWB, 0:kc * n].rearrange("p (k n) -> p k n", n=n)
                P.dma(dst, src, q="pool")

        W = WStream()

        def load_const(dst, name, via_bf=False, shape=None):
            src = din["c_" + name]
            if via_bf:
                t = tf()
                v = t[:, 0:int(np.prod(src.shape[1:]))]
                if len(src.shape) == 3:
                    v = v.rearrange("p (a b) -> p a b", b=src.shape[2])
                P.dma(v, src, key="cst")
                P.copy("dve", dst, v)
            else:
                P.dma(dst, src, key="cst")

        load_const(identf[:, :], "ident")
        load_const(identb[:, :], "ident", via_bf=True)
        load_const(onesf[:, :], "ones")
        load_const(UT[:, :], "U")
        load_const(UBLK[:, :], "Ublk")
        load_const(SEL[:, :, :], "SEL")
        load_const(NEGC[:, :, :], "negcol")
        load_const(CMK[:, :, :], "colmask_s", via_bf=True)

        def bcast_row(dst, row_ap, key="bc"):
            P.dma(dst, row_ap.partition_broadcast(128), key=key)

        def run_kind(kind):
            ks = ExitStack()
            with ks:
                if kind == "p":
                    T, NT, R, NKT = 2048, 16, 128, 16
                    blocks = [(b * 512, 512) for b in range(4)]
                else:
                    T, NT, R, NKT = 64, 1, 64, 33
                    blocks = [(0, 64)]
                TK = NKT * 128
                nseq = 1 if kind == "p" else 2

                def kb(name, shape, dt):
                    return sb(name + kind, shape, dt, ks)

                H = kb("H", [128, NT, 1024], F32)
                XT = kb("XT", [128, 8, T], BF)
                if kind == "p":
                    A = kb("A", [128, 8192], BF)
                    B = kb("B", [128, 8192], BF)
                    C8 = kb("C8", [128, 4096], BF)
                    D8 = kb("D8", [128, 4096], BF)
                    E8 = kb("E8", [128, 4096], BF)
                    F8 = kb("F8", [128, 4096], BF)
                else:
                    A = kb("A", [128, 8704], BF)
                    B = kb("B", [128, 8704], BF)
                    C8 = kb("C8", [128, 4096], BF)
                    D8 = kb("D8", [128, 4352], BF)
                    E8 = kb("E8", [128, 4096], BF)
                    F8 = kb("F8", [128, 4096], BF)
                AT = {}
                if kind == "s":
                    AT["FALL"] = kb("FALL", [128, NKT, 16], F32)
                    AT["NEGF"] = kb("NEGF", [128, NKT, 16], F32)
                    AT["LF"] = kb("LF", [128, NKT, 16], F32)
                    AT["MCS"] = kb("MCS", [128, NT, 2, 32], F32)
                    AT["PTB"] = kb("PTB", [128, 2, 512], BF)
                    AT["ZB"] = kb("ZB", [128, 2, 512], F32)
                    AT["FQ"] = kb("FQ", [128, 2, 512], F32)
                    AT["TFX"] = kb("TFX", [128, 4, 512], F32)
                    AT["XSX"] = kb("XSX", [128, 4, 1024], BF)
                    AT["STX"] = kb("STX", [128, 16, 8], F32)
                    AT["QZ"] = kb("QZ", [128, 2, 512], BF)
                pti = [0]

                load_const(MASKY[:, :, :], "maskY_" + kind)
                load_const(RD[:, :], "rdec_" + kind)
                load_const(NEGA[:, :], "negtri" if kind == "p" else "fox_negnew")
                load_const(NEGB[:, :], "negchunk" if kind == "p" else "mla_negnew")
                if kind == "s":
                    P.dma(AT["MCS"][0:64, 0, 0, :], din["c_mc_s"], key="cst")
                    P.dma(AT["MCS"][0:64, 0, 1, :], din["c_ms_s"], key="cst")
                cdec = CST["cdec_" + kind]
                rc_d, rs_d = din["c_rc_" + kind], din["c_rs_" + kind]

                def run_pass(s):
                    def tokv(ap3):
                        if kind == "p":
                            return ap3[s]
                        return ap3.rearrange("b t f -> (b t) f")

                    def tcols(ti):
                        return slice(ti * 128, ti * 128 + R)

                    if kind == "p":
                        for q in range(4):
                            P.dma(H[:, 4 * q:4 * q + 4, :],
                                  din["x_prompt"][s, 512 * q:512 * (q + 1), :].rearrange("(t p) d -> p t d", p=128), key=("h", q))
                    else:
                        P.dma(H[0:64, 0, :], din["x_sample"].rearrange("b t d -> (b t) d"), key=("h", 0))

                    def norm_stats(gain_row):
                        bcast_row(GB[:, :], gain_row, key="gb")
                        for ti in range(NT):
                            P.act(JUNK[:R, :], H[:R, ti, :], AF.Square, accum=SS[:R, ti:ti + 1], junk=True)
                        P.act(RS[:R, :NT], SS[:R, :NT], AF.Sqrt, scale=1.0 / D, bias=EPS)
                        P.recip(RS[:R, :NT], RS[:R, :NT])

                    def norm_to_xt(gain_row):
                        norm_stats(gain_row)
                        dq = Defer(2)
                        for ti in range(NT):
                            x = xs()
                            P.stt(x[:R, :], H[:R, ti, :], RS[:R, ti:ti + 1], GB[:R, :], ALU.mult, ALU.mult)

                            def pe_part(ti=ti, x=x):
                                b = ps()
                                pv = bankb[b][:, 0:1024].rearrange("p (c r) -> p c r", r=128)
                                for c in range(8):
                                    P.tr(pv[:, c, :R], x[:R, c * 128:(c + 1) * 128], identb[:R, :R])
                                P.copy("act", XT[:, :, tcols(ti)], pv[:, :, :R])

                            dq.push(pe_part)
                        dq.flush()

                    def lin_tm(src, wviews, ncols, evac, tiles=None):
                        rh = []
                        for v in wviews:
                            for k in range(v.shape[1]):
                                rh.append(v[:, k, :])
                        dq = Defer(DEPTH[0])
                        for ti in (range(NT) if tiles is None else tiles):
                            b = ps()
                            out = banks[b][:R, :ncols]
                            for i, r_ in enumerate(rh):
                                P.mm(out, src(i, ti), r_, i == 0, i == len(rh) - 1)
                            dq.push(evac(ti, out))
                        dq.flush()

                    def xt_src(i, ti):
                        return XT[:, i, tcols(ti)]

                    def add_to_h(ti, c0, n, psum):
                        P.tt("dve", H[:R, ti, c0:c0 + n], psum, H[:R, ti, c0:c0 + n], ALU.add)

                    def transpose_to(dst_fn, src_tile, nchunks, ti, eng="act"):
                        b = ps()
                        pv = bankb[b][:, 0:1024].rearrange("p (c r) -> p c r", r=128)
                        for c in range(nchunks):
                            P.tr(pv[:, c, :R], src_tile[:R, c * 128:(c + 1) * 128], identb[:R, :R])
                        for c in range(nchunks):
                            P.copy(eng, dst_fn(c), pv[:, c, :R])

                    def retention(j):
                        QT = C8[:, 0:4096].rearrange("p (m t) -> p m t", m=2)
                        KT = D8[:, 0:4096].rearrange("p (m t) -> p m t", m=2)
                        V = A[:, 0:8192].rearrange("p (c v) -> p c v", v=512)
                        YT = A[:, 0:8192].rearrange("p (k t) -> p k t", k=4)
                        Y = B[:, 0:8192].rearrange("p (c v) -> p c v", v=512)
                        RT = B[:, 0:8192].bitcast(F32).rearrange("p (a n) -> p a n", n=512)
                        if nseq == 1:
                            Sf = E8[:, 0:2048].bitcast(F32).rearrange("p (i m v) -> p i m v", i=1, m=2)
                            Sb = E8[:, 2048:3072].rearrange("p (i m v) -> p i m v", i=1, m=2)
                            Sb2 = [Sb, E8[:, 3072:4096].rearrange("p (i m v) -> p i m v", i=1, m=2)]
                            TAB = F8[:, 0:4096].bitcast(F32).rearrange("p (s a n) -> p s a n", s=2, a=2)
                        else:
                            Sf = E8[:, 0:4096].bitcast(F32).rearrange("p (i m v) -> p i m v", i=2, m=2)
                            Sb = F8[:, 0:2048].rearrange("p (i m v) -> p i m v", i=2, m=2)
                            TAB = F8[:, 2048:2560].bitcast(F32).rearrange("p (s a n) -> p s a n", s=1, a=2)
                            QM = F8[:, 2560:2816].rearrange("p (i m t) -> p i m t", i=2, m=2)
                        for h in range(4):
                            wq = W.get("ret_w_in", (j,), 0, 1024, h * 256, h * 256 + 256)
                            wk = W.get("ret_w_in", (j,), 0, 1024, 1024 + h * 256, 1024 + h * 256 + 256)
                            rti = 0
                            for bi, (b0, bn) in enumerate(blocks):
                                sl = bi % TAB.shape[1]
                                P.dma(TAB[:, sl, 0, :bn], rc_d[:, b0:b0 + bn], key=("tab", sl))
                                P.dma(TAB[:, sl, 1, :bn], rs_d[:, b0:b0 + bn], key=("tab", sl))
                                cos, sin = TAB[:, sl, 0, :bn], TAB[:, sl, 1, :bn]
                                for wv, dst in ((wq, QT), (wk, KT)):
                                    bk = []
                                    for m in range(2):
                                        b = ps()
                                        bk.append(banks[b][:, :bn])
                                        for k in range(8):
                                            P.mm(bk[m], wv[:, k, m * 128:(m + 1) * 128], XT[:, k, b0:b0 + bn], k == 0, k == 7)
                                    t = [RT[:, (rti % 2) * 4 + a, :bn] for a in range(4)]
                                    rti += 1
                                    P.tt("dve", t[0], bk[0], cos, ALU.mult)
                                    P.tt("dve", t[1], bk[1], sin, ALU.mult)
                                    P.tt("dve", t[2], bk[1], cos, ALU.mult)
                                    P.tt("dve", t[3], bk[0], sin, ALU.mult)
                                    P.tt("dve", dst[:, 0, b0:b0 + bn], t[0], t[1], ALU.subtract)
                                    P.tt("dve", dst[:, 1, b0:b0 + bn], t[2], t[3], ALU.add)
                            if nseq == 2:
                                for i in range(2):
                                    for m in range(2):
                                        P.tt("pool", QM[:, i, m, :], QT[:, m, 0:64], CMK[:, i, :], ALU.mult)
                            wv0 = W.get("ret_w_in", (j,), 0, 512, 2048 + h * 512, 2048 + h * 512 + 512)
                            wv1 = W.get("ret_w_in", (j,), 512, 1024, 2048 + h * 512, 2048 + h * 512 + 512)
                            lin_tm(xt_src, [wv0, wv1], 512, lambda ti, o: P.copy("act", V[:R, ti, :], o))
                            bcast_row(GB[:, 0:512], din["ret_gn"][j, h * 512:(h + 1) * 512], key="gb")
                            if kind == "p":
                                P.memset("pool", Sf[:, 0, :, :], 0.0)
                                P.memset("pool", Sb[:, 0, :, :], 0.0)
                            else:
                                for i in range(2):
                                    P.dma(Sf[:, i, :, :], din["state_ret"][j, i, h].rearrange("(m p) v -> p m v", p=128), key=("sin", i))
                                    P.copy("pool", Sb[:, i, :, :], Sf[:, i, :, :])
                            for c in range(NT):
                                cols = tcols(c)
                                sb_r = Sb if nseq == 2 else Sb2[c % 2]
                                sb_w = Sb if nseq == 2 else Sb2[(c + 1) % 2]
                                bT = ps()
                                ktv = bankb[bT][:, 0:256]
                                for m in range(2):
                                    P.tr(ktv[:R, m * 128:(m + 1) * 128], KT[:, m, cols], identb[:, :])
                                kds = []
                                for i in range(nseq):
                                    kd = xs()
                                    kcol = (4 + h) if nseq == 1 else (4 + 2 * h + i)
                                    P.act(kd[:R, 0:256], ktv[:R, 0:256], AF.Copy, scale=RD[:R, kcol:kcol + 1])
                                    kds.append(kd)
                                bA = ps()
                                att = banks[bA][:R, :R]
                                for m in range(2):
                                    P.mm(att, KT[:, m, cols], QT[:, m, cols], m == 0, m == 1)
                                at = xs()
                                P.tt("dve", at[:R, :R], att, MASKY[:R, h, :R], ALU.mult)
                                bO = ps()
                                o = banks[bO][:R, :512]
                                P.mm(o, at[:R, :R], V[:R, c, :], True, False)
                                if nseq == 1:
                                    for m in range(2):
                                        P.mm(o, QT[:, m, cols], sb_r[:, 0, m, :], False, m == 1)
                                else:
                                    for i in range(2):
                                        for m in range(2):
                                            P.mm(o, QM[:, i, m, :], sb_r[:, i, m, :], False, i == 1 and m == 1)
                                osb = tf()
                                st = stt_slot()
                                P.act(osb[:R, :], o, AF.Copy, scale=RD[:R, h:h + 1], accum=st[:R, 0:1])
                                for i in range(nseq):
                                    for m in range(2):
                                        sp_ = banks[4 + ((2 * i + m) % 4)][:, :512]
                                        P.mm(sp_, kds[i][:R, m * 128:(m + 1) * 128], V[:R, c, :], True, True)
                                        P.stt(Sf[:, i, m, :], Sf[:, i, m, :], cdec[h], sp_, ALU.mult, ALU.add)
                                for i in range(nseq):
                                    P.copy("act", sb_w[:, i, :, :], Sf[:, i, :, :])
                                P.act(JUNK[:R, 0:512], osb[:R, :], AF.Square, accum=st[:R, 1:2], junk=True)
                                P.ts("dve", st[:R, 2:3], st[:R, 0:1], 1.0 / 512, ALU.mult)
                                P.tt("dve", st[:R, 3:4], st[:R, 2:3], st[:R, 2:3], ALU.mult)
                                P.stt(st[:R, 4:5], st[:R, 1:2], 1.0 / 512, st[:R, 3:4], ALU.mult, ALU.subtract)
                                P.act(st[:R, 5:6], st[:R, 4:5], AF.Sqrt, bias=EPS)
                                P.recip(st[:R, 5:6], st[:R, 5:6])
                                P.stt(st[:R, 6:7], st[:R, 2:3], -1.0, st[:R, 5:6], ALU.mult, ALU.mult)
                                o2 = tf()
                                P.act(o2[:R, :], osb[:R, :], AF.Identity, scale=st[:R, 5:6], bias=st[:R, 6:7])
                                P.tt("dve", Y[:R, c, :], o2[:R, :], GB[:R, 0:512], ALU.mult)
                            od = dout["ret_p"] if kind == "p" else dout["ret_s"]
                            for i in range(nseq):
                                sq_ = s if kind == "p" else i
                                P.dma(od[j, sq_, h].rearrange("(m p) v -> p m v", p=128), Sf[:, i, :, :], key=("sout", i))
                            wg0 = W.get("ret_w_in", (j,), 0, 512, 4096 + h * 512, 4096 + h * 512 + 512)
                            wg1 = W.get("ret_w_in", (j,), 512, 1024, 4096 + h * 512, 4096 + h * 512 + 512)

                            def g_evac(ti, o):
                                g = xs()
                                P.act(g[:R, 0:512], o, AF.Silu)
                                P.tt("dve", Y[:R, ti, :], Y[:R, ti, :], g[:R, 0:512], ALU.mult)
                                return lambda: transpose_to(lambda cc: YT[:, cc, tcols(ti)], Y[:, ti, :], 4, ti, eng="dve")

                            lin_tm(xt_src, [wg0, wg1], 512, g_evac)
                            wo = [W.get("ret_w_out", (j,), h * 512, h * 512 + 512, hf * 512, hf * 512 + 512) for hf in range(2)]
                            for hf in range(2):
                                lin_tm(lambda i, ti: YT[:, i, tcols(ti)], [wo[hf]], 512,
                                       lambda ti, o, hf=hf: add_to_h(ti, hf * 512, 512, o))

                    def ffn(li):
                        norm_to_xt(din["norm_ffn"][li])
                        HT = A[:, 0:8192].rearrange("p (m t) -> p m t", m=4)
                        for g0 in range(0, 2816, 512):
                            g1 = min(2816, g0 + 512)
                            nch = (g1 - g0) // 128
                            for half in range(0, nch, 2):
                                c0 = g0 + half * 128
                                wg = W.get("ffn_w_gate", (li,), 0, 1024, c0, c0 + 256)
                                wu = W.get("ffn_w_up", (li,), 0, 1024, c0, c0 + 256)
                                for mm_ in range(2):
                                    m = half + mm_
                                    for (b0, bn) in blocks:
                                        bg, bu = ps(), ps()
                                        pg, pu = banks[bg][:, :bn], banks[bu][:, :bn]
                                        for k in range(8):
                                            P.mm(pg, wg[:, k, mm_ * 128:(mm_ + 1) * 128], XT[:, k, b0:b0 + bn], k == 0, k == 7)
                                        for k in range(8):
                                            P.mm(pu, wu[:, k, mm_ * 128:(mm_ + 1) * 128], XT[:, k, b0:b0 + bn], k == 0, k == 7)
                                        sg = tf()
                                        P.act(sg[:, :bn], pg, AF.Silu)
                                        P.tt("dve", HT[:, m, b0:b0 + bn], pu, sg[:, :bn], ALU.mult)
                            wd = [W.get("ffn_w_down", (li,), g0, g1, hf * 512, hf * 512 + 512) for hf in range(2)]
                            for hf in range(2):
                                lin_tm(lambda i, ti: HT[:, i, tcols(ti)], [wd[hf]], 512,
                                       lambda ti, o, hf=hf: add_to_h(ti, hf * 512, 512, o))

                    def pegate(li):
                        norm_to_xt(din["norm_pe"][li])
                        PT_ = F8[:, 0:4096].rearrange("p (m t) -> p m t", m=2)
                        pd = tokv(din["p_prompt"][li] if kind == "p" else din["p_sample"][li])
                        dq = Defer(2)
                        for ti in range(NT):
                            st_ = tf()
                            P.dma(st_[:R, 0:256], pd[ti * 128:ti * 128 + R, :], key=("pin", tfi[0] % 3))
                            xb = xs()
                            P.copy("pool", xb[:R, 0:256], st_[:R, 0:256])
                            dq.push(lambda ti=ti, xb=xb: transpose_to(lambda cc: PT_[:, cc, tcols(ti)], xb, 2, ti))
                        dq.flush()
                        for q4 in range(4):
                            wga = W.get("pe_w_gate", (li,), 0, 1024, q4 * 256, q4 * 256 + 256)
                            wp = W.get("pe_w_proj", (li,), 0, 256, q4 * 256, q4 * 256 + 256)
                            for ti in range(NT):
                                bg, bp = ps(), ps()
                                pg, pp = banks[bg][:R, :256], banks[bp][:R, :256]
                                for k in range(8):
                                    P.mm(pg, XT[:, k, tcols(ti)], wga[:, k, :], k == 0, k == 7)
                                for k in range(2):
                                    P.mm(pp, PT_[:, k, tcols(ti)], wp[:, k, :], k == 0, k == 1)
                                sg = tf()
                                P.act(sg[:R, 0:256], pg, AF.Sigmoid)
                                P.tt("dve", sg[:R, 256:512], pp, sg[:R, 0:256], ALU.mult)
                                P.tt("pool", H[:R, ti, q4 * 256:(q4 + 1) * 256], H[:R, ti, q4 * 256:(q4 + 1) * 256], sg[:R, 256:512], ALU.add)

                    def attn(h_q, st_mm, vext, dv, kind_mask, negf_col, fq_build, scale, evac):
                        def blk_post(stp, kt, krows, q0, n, tot_q0, mask, fqs):
                            pt = AT["PTB"][:, pti[0] % 2, :]
                            pti[0] += 1
                            c0 = q0 - tot_q0
                            bias = negf_col(kt, krows)
                            if fqs is not None:
                                if mask is not None:
                                    mw = mask.shape[1]
                                    tm = tf()
                                    P.tt("pool", tm[:krows, :mw], fqs[:krows, c0:c0 + mw], mask, ALU.add)
                                    P.tt("dve", stp[:, 0:mw], stp[:, 0:mw], tm[:krows, :mw], ALU.add)
                                    if n > mw:
                                        P.tt("dve", stp[:, mw:n], stp[:, mw:n], fqs[:krows, c0 + mw:c0 + n], ALU.add)
                                else:
                                    P.tt("dve", stp, stp, fqs[:krows, c0:c0 + n], ALU.add)
                                P.act(pt[:krows, c0:c0 + n], stp, AF.Exp, scale=scale, bias=bias)
                            else:
                                if mask is not None:
                                    mw = mask.shape[1]
                                    P.tt("dve", stp[:, 0:mw], stp[:, 0:mw], mask, ALU.add)
                                P.act(pt[:krows, c0:c0 + n], stp, AF.Exp, scale=scale)
                            return pt, c0

                        def blk_pv(ptc, kt, krows, accs, first, lastmap):
                            pt, c0 = ptc
                            for (jq, acc, qr) in accs:
                                if jq * 128 < c0:
                                    continue
                                P.mm(acc, pt[:krows, jq * 128:jq * 128 + qr], vext(kt, krows), first, lastmap[jq] == kt)

                        def run_blocks(descs, tot_q0, accs, lastmap, fqs, hook=None):
                            LOOK = 2
                            nb = len(descs)
                            sts = {}
                            for i in range(min(LOOK, nb)):
                                d = descs[i]
                                sts[i] = st_mm(d[0], d[1], d[2], d[3])
                            d = descs[0]
                            posts = {0: blk_post(sts.pop(0), d[0], d[1], d[2], d[3], tot_q0, d[4], fqs)}
                            for i in range(nb):
                                if i + LOOK < nb:
                                    d = descs[i + LOOK]
                                    sts[i + LOOK] = st_mm(d[0], d[1], d[2], d[3])
                                if i + 1 < nb:
                                    d = descs[i + 1]
                                    posts[i + 1] = blk_post(sts.pop(i + 1), d[0], d[1], d[2], d[3], tot_q0, d[4], fqs)
                                d = descs[i]
                                blk_pv(posts.pop(i), d[0], d[1], accs, d[5], lastmap)
                                if i == 1 and hook is not None:
                                    hook()

                        if kind == "p":
                            nxt = [fq_build(0) if fq_build else None]
                            for qb in range(4):
                                fqs = nxt[0]

                                def hook(qb=qb):
                                    if fq_build and qb + 1 < 4:
                                        nxt[0] = fq_build(qb + 1)

                                accs = [(jq, banks[4 + jq][:128, :dv + 1], 128) for jq in range(4)]
                                lastmap = {jq: 4 * qb + jq for jq in range(4)}
                                descs = []
                                for kt in range(4 * qb + 4):
                                    jmin = max(0, kt - 4 * qb)
                                    q0 = qb * 512 + jmin * 128
                                    mask = kind_mask if kt >= 4 * qb else None
                                    descs.append((kt, 128, q0, 512 - jmin * 128, mask, kt == 0))
                                run_blocks(descs, qb * 512, accs, lastmap, fqs, hook)
                                dq = Defer(2)
                                for jq in range(4):
                                    dq.push(evac(4 * qb + jq, accs[jq][1]))
                                dq.flush()
                        else:
                            fqs = fq_build(0) if fq_build else None
                            accs = [(0, banks[4][:64, :dv + 1], 64)]
                            lastmap = {0: 32}
                            descs = []
                            for kt in range(33):
                                if kt < 32:
                                    descs.append((kt, 128, 0, 64, NEGC[:, kt // 16, :], kt == 0))
                                else:
                                    descs.append((kt, 64, 0, 64, kind_mask[:64, :64], False))
                            run_blocks(descs, 0, accs, lastmap, fqs)
                            p_ = evac(0, accs[0][1])
                            if p_ is not None:
                                p_()

                    def mla(li):
                        norm_to_xt(din["norm_mix"][li])
                        if kind == "p":
                            AT["MCS"] = E8[:, 0:2048].bitcast(F32).rearrange("p (t a f) -> p t a f", a=2, f=32)
                            AT["PTB"] = E8[:, 2048:3072].rearrange("p (s n) -> p s n", s=2)
                            AT["ZB"] = F8[:, 0:2048].bitcast(F32).rearrange("p (s n) -> p s n", s=2)
                            AT["FQ"] = F8[:, 2048:4096].bitcast(F32).rearrange("p (s n) -> p s n", s=2)
                            P.dma(AT["MCS"][:, :, 0, :], din["c_mc_p"].rearrange("(t p) f -> p t f", p=128))
                            P.dma(AT["MCS"][:, :, 1, :], din["c_ms_p"].rearrange("(t p) f -> p t f", p=128))
                        MCS = AT["MCS"]
                        if kind == "p":
                            ex_tf = [XT[:, 4 + i // 2, (i % 2) * 1024:(i % 2) * 1024 + 1024].bitcast(F32) for i in range(3)]
                            ex_st_base = XT[:, 5, 1024:2048].bitcast(F32).rearrange("p (s e) -> p s e", e=8)
                            ex_st = [ex_st_base[:, i, :] for i in range(16)]
                            ex_xs = [XT[:, 6 + i // 2, (i % 2) * 1024:(i % 2) * 1024 + 1024] for i in range(4)]
                        else:
                            ex_tf = [AT["TFX"][:, i, :] for i in range(4)]
                            ex_st = [AT["STX"][:, i, :] for i in range(16)]
                            ex_xs = [AT["XSX"][:, i, :] for i in range(4)]
                        n_tf0, n_xs0, n_st0 = len(TFS), len(XSS), len(STS)
                        if kind == "p":
                            CQ = A[:, 0:8192].rearrange("p (k t) -> p k t", k=4)
                            LATT = B[:, 0:4096].rearrange("p (k t) -> p k t", k=2)
                            KRT = B[:, 4096:6144]
                            KNT = B[:, 6144:8192]
                            QNT = C8[:, 0:2048]
                            QRT = C8[:, 2048:4096]
                            VE = D8[:, 0:16 * 129].rearrange("p (t v) -> p t v", v=129)
                            OTG = XT[:, 0:4, :]
                        else:
                            CQ = C8[:, 0:256].rearrange("p (k t) -> p k t", k=4)
                            LATT = A[:, 0:2 * TK].rearrange("p (k t) -> p k t", k=2)
                            KRT = B[:, 0:TK]
                            KNT = B[:, TK:2 * TK]
                            QNT = C8[:, 256:320]
                            QRT = C8[:, 320:384]
                            VE = D8[:, 0:33 * 129].rearrange("p (t v) -> p t v", v=129)
                            OTG = C8[:, 512:768].rearrange("p (k t) -> p k t", k=4)
                        bcast_row(SMALL[:, 0:128], din["mla_gq_nope"][0], key="sm")
                        bcast_row(SMALL[:, 128:192], din["mla_gq_rope"][0], key="sm")
                        bcast_row(SMALL[:, 192:320], din["mla_gk_nope"][0], key="sm")
                        bcast_row(SMALL[:, 320:384], din["mla_gk_rope"][0], key="sm")
                        first = [True]

                        def rope_tm(dst_f32, src_f32, ti, t):
                            cs, sn = MCS[:R, ti, 0, :], MCS[:R, ti, 1, :]
                            P.tt("dve", t[:R, 0:32], src_f32[:, 0:32], cs, ALU.mult)
                            P.tt("dve", t[:R, 32:64], src_f32[:, 32:64], sn, ALU.mult)
                            P.tt("dve", t[:R, 64:96], src_f32[:, 32:64], cs, ALU.mult)
                            P.tt("dve", t[:R, 96:128], src_f32[:, 0:32], sn, ALU.mult)
                            P.tt("dve", dst_f32[:, 0:32], t[:R, 0:32], t[:R, 32:64], ALU.subtract)
                            P.tt("dve", dst_f32[:, 32:64], t[:R, 64:96], t[:R, 96:128], ALU.add)

                        def rms_tm(psum, n, gain, out, st, col):
                            P.act(JUNK[:psum.shape[0], 0:n], psum, AF.Square, accum=st[:psum.shape[0], col:col + 1], junk=True)
                            P.act(st[:psum.shape[0], col + 1:col + 2], st[:psum.shape[0], col:col + 1], AF.Sqrt, scale=1.0 / n, bias=EPS)
                            P.recip(st[:psum.shape[0], col + 1:col + 2], st[:psum.shape[0], col + 1:col + 2])
                            P.stt(out, psum, st[:psum.shape[0], col + 1:col + 2], gain, ALU.mult, ALU.mult)

                        key_new0 = (NKT - 1) * 128 if kind == "s" else 0

                        def kcols_new(ti):
                            return slice(key_new0 + ti * 128, key_new0 + ti * 128 + R)

                        bcast_row(GB[:, 0:512], din["mla_q_norm"][0], key="gb")
                        bcast_row(GB[:, 512:768], din["mla_kv_norm"][0], key="gb")

                        def cq_evac(ti, o):
                            st = stt_slot()
                            x = xs()
                            rms_tm(o, 512, GB[:R, 0:512], x[:R, 0:512], st, 0)
                            return lambda: transpose_to(lambda cc: CQ[:, cc, tcols(ti)], x, 4, ti)

                        w0 = W.get("mla_w_in", (0,), 0, 512, 0, 512)
                        w1 = W.get("mla_w_in", (0,), 512, 1024, 0, 512)
                        lin_tm(xt_src, [w0, w1], 512, cq_evac)

                        def kv_evac(ti, o):
                            st = stt_slot()
                            lat = tf()
                            rms_tm(o[:, 0:256], 256, GB[:R, 512:768], lat[:R, 0:256], st, 0)
                            rms_tm(o[:, 256:320], 64, SMALL[:R, 320:384], lat[:R, 320:384], st, 2)
                            rope_tm(lat[:R, 256:320], lat[:R, 320:384], ti, lat[:, 384:512])
                            ld = tokv(dout["lat_p"][0] if kind == "p" else dout["lat_s"][0])
                            kd_ = tokv(dout["kr_p"][0] if kind == "p" else dout["kr_s"][0])
                            P.dma(ld[ti * 128:ti * 128 + R, :], lat[:R, 0:256], key=("mo", tfi[0] % 3))
                            P.dma(kd_[ti * 128:ti * 128 + R, :], lat[:R, 256:320], key=("mo", tfi[0] % 3))
                            x = xs()
                            P.copy("pool", x[:R, 0:320], lat[:R, 0:320])

                            def pe_part():
                                b = ps()
                                pv = bankb[b]
                                for c in range(2):
                                    P.tr(pv[:, c * 128:c * 128 + R], x[:R, c * 128:(c + 1) * 128], identb[:R, :R])
                                P.tr(pv[0:64, 256:256 + R], x[:R, 256:320], identb[:R, :R])
                                for c in range(2):
                                    P.copy("act", LATT[:, c, kcols_new(ti)], pv[:, c * 128:c * 128 + R])
                                P.copy("act", KRT[0:64, kcols_new(ti)], pv[0:64, 256:256 + R])

                            return pe_part

                        w2 = W.get("mla_w_in", (0,), 0, 512, 512, 832)
                        w3 = W.get("mla_w_in", (0,), 512, 1024, 512, 832)
                        lin_tm(xt_src, [w2, w3], 320, kv_evac)
                        if kind == "s":
                            for i in range(2):
                                for q in range(16):
                                    kt = 16 * i + q
                                    st_ = tf()
                                    P.dma(st_[:, 0:256], din["cache_mla_latent"][0, i, q * 128:(q + 1) * 128, :], key=("pin", tfi[0] % 3))
                                    P.dma(st_[:, 256:320], din["cache_mla_krope"][0, i, q * 128:(q + 1) * 128, :], key=("pin", tfi[0] % 3))
                                    x = xs()
                                    P.copy("pool", x[:, 0:320], st_[:, 0:320])
                                    b = ps()
                                    pv = bankb[b]
                                    for c in range(2):
                                        P.tr(pv[:, c * 128:(c + 1) * 128], x[:, c * 128:(c + 1) * 128], identb[:, :])
                                    P.tr(pv[0:64, 256:384], x[:, 256:320], identb[:, :])
                                    for c in range(2):
                                        P.copy("act", LATT[:, c, kt * 128:(kt + 1) * 128], pv[:, c * 128:(c + 1) * 128])
                                    P.copy("act", KRT[0:64, kt * 128:(kt + 1) * 128], pv[0:64, 256:384])
                        P.memset("pool", VE[:, :, 128:129], 1.0)
                        P.memset("pool", KRT[64:128, :], 0.0)
                        P.memset("pool", QRT[64:128, :], 0.0)
                        scale = 192.0 ** -0.5
                        TFS.extend(ex_tf)
                        XSS.extend(ex_xs)
                        STS.extend(ex_st)
                        DEPTH[0] = 4
                        for hg in range(4):
                            W.la = 2
                            wkv = W.get("mla_w_kvb", (0,), 0, 256, hg * 1024, hg * 1024 + 1024)
                            for hh in range(4):
                                h = hg * 4 + hh
                                if hh % 2 == 0:
                                    wq2 = W.get("mla_w_qb", (0,), 0, 512, (hg * 4 + hh) * 192, (hg * 4 + hh) * 192 + 384)
                                wq = wq2[:, :, (hh % 2) * 192:(hh % 2) * 192 + 192]

                                def q_evac(ti, o):
                                    st = stt_slot()
                                    x = xs()
                                    qr = tf()
                                    P.act(JUNK[:R, 0:128], o[:, 0:128], AF.Square, accum=st[:R, 0:1], junk=True)
                                    P.act(JUNK[:R, 128:192], o[:, 128:192], AF.Square, scale=2.0 ** 0.5, accum=st[:R, 1:2], junk=True)
                                    P.act(st[:R, 2:4], st[:R, 0:2], AF.Sqrt, scale=1.0 / 128, bias=EPS)
                                    P.recip(st[:R, 2:4], st[:R, 2:4])
                                    P.stt(x[:R, 0:128], o[:, 0:128], st[:R, 2:3], SMALL[:R, 0:128], ALU.mult, ALU.mult)
                                    P.stt(qr[:R, 128:192], o[:, 128:192], st[:R, 3:4], SMALL[:R, 128:192], ALU.mult, ALU.mult)
                                    rope_tm(x[:R, 128:192], qr[:R, 128:192], ti, qr[:, 256:384])

                                    def pe_part():
                                        b = ps()
                                        pv = bankb[b]
                                        P.tr(pv[:, 0:R], x[:R, 0:128], identb[:R, :R])
                                        P.tr(pv[0:64, 128:128 + R], x[:R, 128:192], identb[:R, :R])
                                        P.copy("act", QNT[:, tcols(ti)], pv[:, 0:R])
                                        P.copy("act", QRT[0:64, tcols(ti)], pv[0:64, 128:128 + R])

                                    return pe_part

                                lin_tm(lambda i, ti: CQ[:, i, tcols(ti)], [wq], 192, q_evac)
                                dq = Defer(DEPTH[0])
                                for kt in range(NKT):
                                    kr_ = R if (kind == "s" and kt == 32) else 128
                                    b = ps()
                                    o = banks[b][:kr_, :256]
                                    for k in range(2):
                                        P.mm(o, LATT[:, k, kt * 128:kt * 128 + kr_], wkv[:, k, hh * 256:(hh + 1) * 256], k == 0, k == 1)
                                    st = stt_slot()
                                    x = xs()
                                    rms_tm(o[:, 0:128], 128, SMALL[:kr_, 192:320], x[:kr_, 0:128], st, 0)
                                    P.copy("act", VE[:kr_, kt, 0:128], o[:, 128:256])

                                    def pe_part(kt=kt, kr_=kr_, x=x):
                                        b2 = ps()
                                        pv = bankb[b2]
                                        P.tr(pv[:, 0:kr_], x[:kr_, 0:128], identb[:kr_, :kr_])
                                        P.copy("dve", KNT[:, kt * 128:kt * 128 + kr_], pv[:, 0:kr_])

                                    dq.push(pe_part)
                                dq.flush()

                                def st_mm(kt, krows, q0, n):
                                    b = ps()
                                    o = banks[b][:krows, :n]
                                    P.mm(o, KNT[:, kt * 128:kt * 128 + krows], QNT[:, q0:q0 + n], True, False)
                                    P.mm(o, KRT[:, kt * 128:kt * 128 + krows], QRT[:, q0:q0 + n], False, True)
                                    return o

                                def o_evac(ti, acc):
                                    st = stt_slot()
                                    rr = acc.shape[0]
                                    P.recip(st[:rr, 0:1], acc[:, 128:129])
                                    x = xs()
                                    P.ts("dve", x[:rr, 0:128], acc[:, 0:128], st[:rr, 0:1], ALU.mult)

                                    def pe_part():
                                        b = ps()
                                        pv = bankb[b]
                                        P.tr(pv[:, 0:rr], x[:rr, 0:128], identb[:rr, :rr])
                                        P.copy("act", OTG[:, hh, tcols(ti)], pv[:, 0:rr])

                                    return pe_part

                                attn(h, st_mm, lambda kt, kr: VE[:kr, kt, :], 128, NEGB[:, :], lambda kt, kr: None, None, scale, o_evac)
                            wo = [W.get("mla_w_out", (0,), hg * 512, hg * 512 + 512, hf * 512, hf * 512 + 512) for hf in range(2)]
                            W.la = LA
                            for hf in range(2):
                                lin_tm(lambda i, ti: OTG[:, i, tcols(ti)], [wo[hf]], 512,
                                       lambda ti, o, hf=hf: add_to_h(ti, hf * 512, 512, o))
                        del TFS[n_tf0:]
                        del XSS[n_xs0:]
                        del STS[n_st0:]
                        DEPTH[0] = 2

                    def fox(li):
                        norm_to_xt(din["norm_mix"][li])
                        key_new0 = (NKT - 1) * 128 if kind == "s" else 0
                        knew = NKT - 1 if kind == "s" else None
                        if kind == "p":
                            AT["FALL"] = A[:, 4224:4736].bitcast(F32).rearrange("p (t h) -> p t h", h=16)
                            AT["NEGF"] = A[:, 4736:5248].bitcast(F32).rearrange("p (t h) -> p t h", h=16)
                            AT["LF"] = A[:, 5248:5760].bitcast(F32).rearrange("p (t h) -> p t h", h=16)
                            AT["PTB"] = A[:, 6144:7168].rearrange("p (s n) -> p s n", s=2)
                            AT["ZB"] = F8[:, 0:2048].bitcast(F32).rearrange("p (s n) -> p s n", s=2)
                            AT["FQ"] = F8[:, 2048:4096].bitcast(F32).rearrange("p (s n) -> p s n", s=2)
                        if kind == "p":
                            AT["QZ"] = A[:, 7168:8192].rearrange("p (s n) -> p s n", s=2)
                        FALL, NEGF, LF, FQ = AT["FALL"], AT["NEGF"], AT["LF"], AT["FQ"]
                        QZ = AT["QZ"]

                        if kind == "p":
                            VE = A[:, 0:16 * 260].rearrange("p (t h v) -> p t h v", h=4, v=65)
                            KT = B[:, 0:4096].rearrange("p (m t) -> p m t", m=2)
                            OTOK = B[:, 4096:8192].rearrange("p (t v) -> p t v", v=256)
                            QT = C8[:, 0:4096].rearrange("p (m t) -> p m t", m=2)
                            G = D8[:, 0:4096].rearrange("p (t v) -> p t v", v=256)
                            OT = E8[:, 0:4096].rearrange("p (m t) -> p m t", m=2)
                        else:
                            VE = A[:, 0:33 * 260].rearrange("p (t h v) -> p t h v", h=4, v=65)
                            KT = B[:, 0:2 * TK].rearrange("p (m t) -> p m t", m=2)
                            OTOK = C8[:, 0:256].rearrange("p (t v) -> p t v", v=256)
                            QT = C8[:, 256:384].rearrange("p (m t) -> p m t", m=2)
                            G = C8[:, 512:768].rearrange("p (t v) -> p t v", v=256)
                            OT = C8[:, 1024:1152].rearrange("p (m t) -> p m t", m=2)
                        bcast_row(SMALL[:, 0:64], din["fox_gq"][0], key="sm")
                        bcast_row(SMALL[:, 64:128], din["fox_gk"][0], key="sm")
                        bcast_row(SMALL[:, 128:144], din["fox_b_f"][0], key="sm")
                        for a in range(4):
                            P.copy("pool", SMALL[:, 256 + a * 64:256 + (a + 1) * 64], SMALL[:, 0:64])
                        for a in range(4):
                            P.copy("pool", GB[:, a * 64:(a + 1) * 64], SMALL[:, 64:128])
                        GQ4 = SMALL[:, 256:512]
                        GK4 = GB[:, 0:256]
                        wf = W.get("fox_w_in", (0,), 0, 1024, 4096, 4112)
                        lq = NKT - 1 if kind == "s" else 0

                        def f_evac(ti, o):
                            t = tf()
                            P.tt("dve", t[:R, 0:16], o, SMALL[:R, 128:144], ALU.add)
                            P.act(t[:R, 16:32], t[:R, 0:16], AF.Exp, scale=-1.0)
                            P.act(t[:R, 32:48], t[:R, 16:32], AF.Ln, bias=1.0)
                            P.ts("dve", LF[:R, lq + ti, :], t[:R, 32:48], -1.0, ALU.mult)
                            fd = tokv(dout["flf_p"][0] if kind == "p" else dout["flf_s"][0])
                            P.dma(fd[ti * 128:ti * 128 + R, :], LF[:R, lq + ti, :], key="lfo")

                        lin_tm(xt_src, [wf], 16, f_evac)
                        if kind == "s":
                            for i in range(2):
                                P.dma(LF[:, 16 * i:16 * i + 16, :], din["cache_fox_logf"][0, i].rearrange("(t p) h -> p t h", p=128), key=("lfi", i))
                        for kt in range(NKT):
                            b = ps()
                            if kind == "s" and kt == 32:
                                o = banks[b][:64, :16]
                                P.mm(o, UBLK[:64, :64], LF[:64, kt, :], True, False)
                                P.mm(o, SEL[:, 1, 0:64], FALL[:, 15, :], False, False)
                                P.mm(o, SEL[:, 2, 0:64], FALL[:, 31, :], False, True)
                                rr = 64
                            else:
                                o = banks[b][:128, :16]
                                chain = (kt % 16 != 0) if kind == "s" else (kt != 0)
                                P.mm(o, UT[:, :], LF[:, kt, :], True, not chain)
                                if chain:
                                    P.mm(o, SEL[:, 0, :], FALL[:, kt - 1, :], False, True)
                                rr = 128
                            P.copy("dve", FALL[:rr, kt, :], o)
                            P.ts("dve", NEGF[:rr, kt, :], o, -1.0, ALU.mult)
                        for g in range(4):

                            def qk_norm(o, gain4, outf, extra_scale):
                                st = stt_slot()
                                sq = tf()
                                P.act(sq[:R, 0:256], o, AF.Square)
                                P.op("dve", lambda e: e.tensor_reduce(out=st[:R, 0:4], in_=sq[:R, 0:256].rearrange("p (h d) -> p h d", d=64),
                                                                     axis=AX.X, op=ALU.add),
                                     [sq[:R, 0:256]], [st[:R, 0:4]])
                                P.act(st[:R, 4:8], st[:R, 0:4], AF.Sqrt, scale=1.0 / 64, bias=EPS)
                                P.recip(st[:R, 4:8], st[:R, 4:8])
                                if extra_scale != 1.0:
                                    P.ts("dve", st[:R, 4:8], st[:R, 4:8], extra_scale, ALU.mult)
                                for a in range(4):
                                    P.stt(outf[:, a * 64:(a + 1) * 64], o[:, a * 64:(a + 1) * 64], st[:R, 4 + a:5 + a],
                                          gain4[:R, a * 64:(a + 1) * 64], ALU.mult, ALU.mult)

                            def q_evac(ti, o):
                                x = xs()
                                qk_norm(o, GQ4, x[:R, 0:256], 0.125)
                                return lambda: transpose_to(lambda cc: QT[:, cc, tcols(ti)], x, 2, ti)

                            wq = W.get("fox_w_in", (0,), 0, 1024, g * 256, g * 256 + 256)
                            lin_tm(xt_src, [wq], 256, q_evac)

                            def k_evac(ti, o):
                                kf = tf()
                                qk_norm(o, GK4, kf[:R, 0:256], 1.0)
                                kd_ = tokv((dout["fk_p"][0] if kind == "p" else dout["fk_s"][0]).rearrange("b t h d -> b t (h d)"))
                                P.dma(kd_[ti * 128:ti * 128 + R, g * 256:(g + 1) * 256], kf[:R, 0:256], key=("ko", tfi[0] % 3))
                                x = xs()
                                P.copy("pool", x[:R, 0:256], kf[:R, 0:256])
                                return lambda: transpose_to(lambda cc: KT[:, cc, key_new0 + ti * 128:key_new0 + ti * 128 + R], x, 2, ti)

                            wk = W.get("fox_w_in", (0,), 0, 1024, 1024 + g * 256, 1024 + g * 256 + 256)
                            lin_tm(xt_src, [wk], 256, k_evac)
                            P.memset("pool", VE[:, :, :, 64:65], 1.0)

                            def v_evac(ti, o):
                                vf = tf()
                                P.copy("act", vf[:R, 0:256], o)
                                vd = tokv((dout["fv_p"][0] if kind == "p" else dout["fv_s"][0]).rearrange("b t h d -> b t (h d)"))
                                P.dma(vd[ti * 128:ti * 128 + R, g * 256:(g + 1) * 256], vf[:R, 0:256], key=("vo", tfi[0] % 3))
                                P.copy("pool", VE[:R, lq + ti, :, 0:64], vf[:R, 0:256].rearrange("p (h d) -> p h d", d=64))

                            wv = W.get("fox_w_in", (0,), 0, 1024, 2048 + g * 256, 2048 + g * 256 + 256)
                            lin_tm(xt_src, [wv], 256, v_evac)
                            if kind == "s":
                                for i in range(2):
                                    for q in range(16):
                                        kt = 16 * i + q
                                        st_ = tf()
                                        P.dma(st_[:, 0:256], din["cache_fox_k"][0, i, q * 128:(q + 1) * 128, 4 * g:4 * g + 4, :].rearrange("p h d -> p (h d)"),
                                              key=("pin", tfi[0] % 3))
                                        P.dma(st_[:, 256:512], din["cache_fox_v"][0, i, q * 128:(q + 1) * 128, 4 * g:4 * g + 4, :].rearrange("p h d -> p (h d)"),
                                              key=("pin", tfi[0] % 3))
                                        x = xs()
                                        P.copy("pool", x[:, 0:256], st_[:, 0:256])
                                        P.copy("pool", VE[:, kt, :, 0:64], st_[:, 256:512].rearrange("p (h d) -> p h d", d=64))
                                        b = ps()
                                        pv = bankb[b]
                                        for c in range(2):
                                            P.tr(pv[:, c * 128:(c + 1) * 128], x[:, c * 128:(c + 1) * 128], identb[:, :])
                                        for c in range(2):
                                            P.copy("act", KT[:, c, kt * 128:(kt + 1) * 128], pv[:, c * 128:(c + 1) * 128])
                            wg_ = W.get("fox_w_in", (0,), 0, 1024, 3072 + g * 256, 3072 + g * 256 + 256)
                            lin_tm(xt_src, [wg_], 256, lambda ti, o: P.act(G[:R, ti, :], o, AF.Sigmoid))
                            for hh in range(4):
                                h = g * 4 + hh
                                pr = slice((hh % 2) * 64, (hh % 2) * 64 + 64)
                                mc = hh // 2

                                orow = slice(64, 128) if hh % 2 == 0 else slice(0, 64)
                                P.memset("pool", QZ[orow, :, :], 0.0)

                                def fq_build(qb):
                                    qz = QZ[:, qb % 2, :]
                                    wq_ = 512 if kind == "p" else 64
                                    P.copy("act", qz[pr, 0:wq_], QT[pr, mc, qb * 512:qb * 512 + wq_])
                                    fq = FQ[:, qb % 2, :]
                                    b = ps()
                                    for jq in range(len(blocks[0:1]) * (4 if kind == "p" else 1)):
                                        qt = (4 * qb + jq) if kind == "p" else 32
                                        bm = tf()
                                        P.ts("dve", bm[:R, 0:R], identf[:R, :R], FALL[:R, qt, h:h + 1], ALU.mult)
                                        P.mm(banks[b][:, jq * 128:jq * 128 + R], onesf[:R, :], bm[:R, 0:R], True, True)
                                    wdt = 512 if kind == "p" else 64
                                    P.copy("act", fq[:, 0:wdt], banks[b][:, 0:wdt])
                                    return fq

                                def st_mm(kt, krows, q0, n):
                                    b = ps()
                                    o = banks[b][:krows, :n]
                                    P.mm(o, KT[:, mc, kt * 128:kt * 128 + krows], QZ[:, (q0 // 512) % 2, (q0 % 512):(q0 % 512) + n], True, True)
                                    return o

                                def o_evac(ti, acc):
                                    st = stt_slot()
                                    rr = acc.shape[0]
                                    P.recip(st[:rr, 0:1], acc[:, 64:65])
                                    P.stt(OTOK[:rr, ti, hh * 64:(hh + 1) * 64], acc[:, 0:64], st[:rr, 0:1], G[:rr, ti, hh * 64:(hh + 1) * 64],
                                          ALU.mult, ALU.mult)

                                attn(h, st_mm, lambda kt, kr: VE[:kr, kt, hh, :], 64, NEGA[:, :],
                                     lambda kt, kr: NEGF[:kr, kt, h:h + 1], fq_build, 1.0, o_evac)
                            for ti in range(NT):
                                transpose_to(lambda cc: OT[:, cc, tcols(ti)], OTOK[:, ti, :], 2, ti)
                            wo = W.get("fox_w_out", (0,), g * 256, g * 256 + 256, 0, 1024)
                            for hf in range(2):
                                lin_tm(lambda i, ti: OT[:, i, tcols(ti)], [wo[:, :, hf * 512:(hf + 1) * 512]], 512,
                                       lambda ti, o, hf=hf: add_to_h(ti, hf * 512, 512, o))

                    for li in range(DBG_LAYERS):
                        kind_m = li % 3
                        if kind_m == 0:
                            norm_to_xt(din["norm_mix"][li])
                            retention(li // 3)
                        elif kind_m == 1:
                            mla(li)
                        else:
                            fox(li)
                        ffn(li)
                        pegate(li)
                    norm_stats(din["norm_final"])
                    yout = dout["y_prompt"][s] if kind == "p" else dout["y_sample"].rearrange("b t d -> (b t) d")
                    for ti in range(NT):
                        for hf in range(2):
                            t = tf()
                            P.stt(t[:R, :], H[:R, ti, hf * 512:(hf + 1) * 512], RS[:R, ti:ti + 1], GB[:R, hf * 512:(hf + 1) * 512], ALU.mult, ALU.mult)
                            P.dma(yout[ti * 128:ti * 128 + R, hf * 512:(hf + 1) * 512], t[:R, :], key=("yo", tfi[0] % 3))

                if kind == "p":
                    for s in range(int(os.environ.get("MK_PSEQ", "2"))):
                        run_pass(s)
                else:
                    run_pass(0)
                P.barrier()

        if "p" in DBG_PASSES:
            run_kind("p")
        if "s" in DBG_PASSES:
            run_kind("s")
        P.barrier()
        return W.rec, P.nops


_CACHE = {}


def _build():
    if "nc" in _CACHE:
        return _CACHE["nc"], _CACHE["cst"]
    CST = _consts()
    nc0 = bass.Bass("TRN2", target_bir_lowering=False)
    specs, _ = emit(nc0, True, None, CST)
    nc = bass.Bass("TRN2", target_bir_lowering=False)
    _, nops = emit(nc, False, specs, CST)
    _CACHE["nc"] = nc
    _CACHE["cst"] = CST
    _CACHE["nops"] = nops
    return nc, CST


def kernel(**inputs):
    nc, CST = _build()
    in_maps = []
    for c in range(8):
        m = {}
        for n in IN_SHAPES:
            a = np.asarray(inputs[n], dtype=np.float32)
            if n in SHARDED:
                ax = SHARDED[n]
                sl = [slice(None)] * a.ndim
                sl[ax] = slice(2 * c, 2 * c + 2)
                a = a[tuple(sl)]
            m[n] = np.ascontiguousarray(a)
        for n in CONST_NAMES:
            m["c_" + n] = np.ascontiguousarray(CST[n], dtype=np.float32)
        in_maps.append(m)
    res = run_bass_kernel_spmd(nc, in_maps, core_ids=list(range(8)))
    outs = []
    for n, shp, ax in OUT_SHAPES:
        outs.append(np.concatenate([np.asarray(res.results[c][n], dtype=np.float32) for c in range(8)], axis=ax))
    return tuple(outs)
```

```python
import numpy as np
import concourse.bass as bass
import concourse.mybir as mybir
from concourse.bass_utils import run_bass_kernel_spmd
from contextlib import ExitStack

F32 = mybir.dt.float32
BF = mybir.dt.bfloat16
AF = mybir.ActivationFunctionType
ALU = mybir.AluOpType
AX = mybir.AxisListType

D = 1024
EPS = 1e-6
NEG = -30000.0
NWB = 5
LA = 3
import os
DBG_LAYERS = int(os.environ.get("MK_LAYERS", "4"))
DBG_PASSES = os.environ.get("MK_PASSES", "ps")


def _esz(dt):
    return 2 if dt == BF else 4


class Defer:
    def __init__(self, depth=2):
        self.q = []
        self.depth = depth

    def push(self, fn):
        if fn is not None:
            self.q.append(fn)
        while len(self.q) > self.depth:
            self.q.pop(0)()

    def flush(self):
        while self.q:
            self.q.pop(0)()


class Prog:
    def __init__(self, nc, es, dry):
        self.nc = nc
        self.es = es
        self.dry = dry
        self.E = dict(pe=nc.tensor, act=nc.scalar, dve=nc.vector, pool=nc.gpsimd, sp=nc.sync)
        self.semh = {}
        self.cnt = {}
        for k in ("pe", "act", "dve", "pool"):
            self.semh[k] = es.enter_context(nc.semaphore("s_" + k))
            self.cnt[k] = 0
        self.seen = {k: {} for k in self.E}
        self.ent = {}
        self.open = {}
        self.nops = 0

    def dsem(self, key):
        k = ("d", key)
        if k not in self.semh:
            self.semh[k] = self.es.enter_context(self.nc.semaphore("d%d" % len(self.semh)))
            self.cnt[k] = 0
        return k

    @staticmethod
    def box(ap, exact=False):
        t = ap.tensor
        tn = type(t).__name__
        if not (tn.startswith("SB") or tn.startswith("PSum")):
            return None
        if tn.startswith("PSum") and not exact:
            return (t.name, 0, 128, 0, 2048)
        pairs = ap.ap
        pstep, pn = pairs[0]
        sp = ap.start_partition
        if callable(sp):
            sp = sp()
        off = ap.offset - sp * pstep
        lo = hi = off
        for st, c in pairs[1:]:
            if st >= 0:
                hi += st * (c - 1)
            else:
                lo += st * (c - 1)
        e = _esz(ap.dtype)
        return (t.name, sp, sp + pn, lo * e, (hi + 1) * e)

    def op(self, eng, fn, reads=(), writes=(), dkey=None):
        if self.dry:
            return
        self.nops += 1
        need = {}
        rb = [b for b in (self.box(a) for a in reads) if b]
        wb = [b for b in (self.box(a) for a in writes) if b]
        if eng != "pe":
            for b in (x for x in (self.box(a, exact=True) for a in reads) if x):
                if b[0] in self.open:
                    self.open[b[0]] = [ob for ob in self.open[b[0]]
                                       if not (ob[1] < b[2] and b[1] < ob[2] and ob[3] < b[4] and b[3] < ob[4])]
        for b in rb:
            for e in self.ent.get(b[0], ()):
                if e[1] and e[0][1] < b[2] and b[1] < e[0][2] and e[0][3] < b[4] and b[3] < e[0][4]:
                    for k, v in e[2].items():
                        if need.get(k, 0) < v:
                            need[k] = v
        for b in wb:
            for e in self.ent.get(b[0], ()):
                if e[0][1] < b[2] and b[1] < e[0][2] and e[0][3] < b[4] and b[3] < e[0][4]:
                    for k, v in e[2].items():
                        if need.get(k, 0) < v:
                            need[k] = v
        E = self.E[eng]
        seen = self.seen[eng]
        for k, v in need.items():
            if k == eng and eng == "pe":
                continue
            if seen.get(k, 0) >= v:
                continue
            E.wait_ge(self.semh[k], v)
            seen[k] = v
        ins = fn(E)
        if dkey is not None:
            k = self.dsem(dkey)
            self.cnt[k] += 16
            ins.then_inc(self.semh[k], 16)
        else:
            k = eng
            self.cnt[k] += 1
            ins.then_inc(self.semh[k], 1)
        tok = {k: self.cnt[k]}
        for b in wb:
            L = self.ent.setdefault(b[0], [])
            L[:] = [e for e in L if not (b[1] <= e[0][1] and e[0][2] <= b[2] and b[3] <= e[0][3] and e[0][4] <= b[4])]
            L.append((b, True, tok))
        for b in rb:
            L = self.ent.setdefault(b[0], [])
            for e in L:
                if (not e[1]) and e[0] == b:
                    for kk, vv in tok.items():
                        if e[2].get(kk, 0) < vv:
                            e[2][kk] = vv
                    break
            else:
                L.append((b, False, dict(tok)))

    def barrier(self):
        if self.dry:
            return
        for eng, E in self.E.items():
            seen = self.seen[eng]
            for k, v in self.cnt.items():
                if v > seen.get(k, 0):
                    E.wait_ge(self.semh[k], v)
                    seen[k] = v
        self.ent.clear()

    def _pe_open(self, out, start):
        if self.dry:
            return
        b = self.box(out, exact=True)
        L = self.open.setdefault(b[0], [])
        if start:
            for ob in L:
                if ob != b and ob[1] < b[2] and b[1] < ob[2] and ob[3] < b[4] and b[3] < ob[4]:
                    raise RuntimeError("PSUM overwrite of un-evacuated group %s by %s" % (ob, b))
            if b not in L:
                L.append(b)

    def mm(self, out, lhsT, rhs, start, stop):
        self._pe_open(out, start)
        self.op("pe", lambda e: e.matmul(out, lhsT=lhsT, rhs=rhs, start=start, stop=stop), [lhsT, rhs], [out])

    def tr(self, out, in_, ident):
        self._pe_open(out, True)
        self.op("pe", lambda e: e.transpose(out=out, in_=in_, identity=ident), [in_, ident], [out])

    def act(self, out, in_, func, scale=None, bias=None, accum=None, junk=False):
        kw = {}
        rd = [in_]
        if scale is not None:
            kw["scale"] = scale
            if not isinstance(scale, (int, float)):
                rd.append(scale)
        if bias is not None:
            kw["bias"] = bias
            if not isinstance(bias, (int, float)):
                rd.append(bias)
        wr = [out]
        if accum is not None:
            kw["accum_out"] = accum
            wr.append(accum)
        self.op("act", lambda e: e.activation(out=out, in_=in_, func=func, **kw), rd, wr)

    def tt(self, eng, out, in0, in1, op):
        self.op(eng, lambda e: e.tensor_tensor(out=out, in0=in0, in1=in1, op=op), [in0, in1], [out])

    def ts(self, eng, out, in0, s1, op0, s2=None, op1=None):
        rd = [in0] + [s for s in (s1, s2) if s is not None and not isinstance(s, (int, float))]
        if op1 is None:
            self.op(eng, lambda e: e.tensor_scalar(out=out, in0=in0, scalar1=s1, scalar2=None, op0=op0), rd, [out])
        else:
            self.op(eng, lambda e: e.tensor_scalar(out=out, in0=in0, scalar1=s1, scalar2=s2, op0=op0, op1=op1), rd, [out])

    def stt(self, out, in0, sc, in1, op0, op1):
        rd = [in0, in1] + ([] if isinstance(sc, (int, float)) else [sc])
        self.op("dve", lambda e: e.scalar_tensor_tensor(out=out, in0=in0, scalar=sc, in1=in1, op0=op0, op1=op1), rd, [out])

    def copy(self, eng, out, in_):
        if eng == "act":
            self.op("act", lambda e: e.copy(out=out, in_=in_), [in_], [out])
        else:
            self.op(eng, lambda e: e.tensor_copy(out=out, in_=in_), [in_], [out])

    def recip(self, out, in_):
        self.op("dve", lambda e: e.reciprocal(out=out, in_=in_), [in_], [out])

    def memset(self, eng, ap, val):
        self.op(eng, lambda e: e.memset(ap, val), [], [ap])

    def dma(self, out, in_, key=None, q="sp"):
        if self.dry:
            return
        bo, bi = self.box(out), self.box(in_)
        if bo is not None:
            key = ("i", bo[0], bo[3])
        else:
            key = ("o", bi[0], bi[3])
        self.op(q, lambda e: e.dma_start(out=out, in_=in_), [in_], [out], dkey=key)


def _consts():
    c = {}
    i128 = np.arange(128)
    c["ident"] = np.eye(128, dtype=np.float32)
    c["ones"] = np.ones((128, 128), np.float32)
    gam = 1.0 - 2.0 ** (-5.0 - np.arange(4))
    lg = np.log(gam)
    s = i128[:, None].astype(np.float64)
    t = i128[None, :].astype(np.float64)
    mk = np.zeros((128, 4, 128), np.float64)
    for h in range(4):
        mk[:, h, :] = np.where(s <= t, np.exp(lg[h] * (-(s + 1.0))), 0.0) / 16.0
    c["maskY_p"] = mk.astype(np.float32)
    rd = np.zeros((128, 16), np.float64)
    for h in range(4):
        rd[:, h] = np.exp(lg[h] * (i128 + 1.0))
        rd[:, 4 + h] = np.exp(lg[h] * (127.0 - i128)) / 16.0
    c["rdec_p"] = rd.astype(np.float32)
    c["cdec_p"] = [float(np.exp(lg[h] * 128.0)) for h in range(4)]
    i64 = np.arange(64)
    sq = i64 // 32
    sl = (i64 % 32).astype(np.float64)
    same = sq[:, None] == sq[None, :]
    mk = np.zeros((128, 4, 128), np.float64)
    for h in range(4):
        mk[:64, h, :64] = np.where(same & (sl[:, None] <= sl[None, :]), np.exp(lg[h] * (-(sl[:, None] + 1.0))), 0.0) / 16.0
    c["maskY_s"] = mk.astype(np.float32)
    rd = np.zeros((128, 16), np.float64)
    for h in range(4):
        rd[:64, h] = np.exp(lg[h] * (sl + 1.0))
        for i in range(2):
            rd[:64, 4 + 2 * h + i] = np.where(sq == i, np.exp(lg[h] * (31.0 - sl)) / 16.0, 0.0)
    c["rdec_s"] = rd.astype(np.float32)
    c["cdec_s"] = [float(np.exp(lg[h] * 32.0)) for h in range(4)]
    cm = np.zeros((128, 2, 64), np.float32)
    cm[:, 0, :32] = 1.0
    cm[:, 1, 32:] = 1.0
    c["colmask_s"] = cm

    def rope_tab(pos, half):
        inv = (np.float32(10000.0) ** (-np.arange(half, dtype=np.float32) / np.float32(half))).astype(np.float32)
        ang = (pos.astype(np.float32)[:, None] * inv[None, :]).astype(np.float32)
        return np.cos(ang.astype(np.float64)).astype(np.float32), np.sin(ang.astype(np.float64)).astype(np.float32)

    pos_p = np.arange(2048)
    pos_s = np.concatenate([2048 + np.arange(32), 2048 + np.arange(32)])
    cs, sn = rope_tab(pos_p, 128)
    c["rc_p"] = np.ascontiguousarray(cs.T)
    c["rs_p"] = np.ascontiguousarray(sn.T)
    cs, sn = rope_tab(pos_s, 128)
    c["rc_s"] = np.ascontiguousarray(cs.T)
    c["rs_s"] = np.ascontiguousarray(sn.T)
    cs, sn = rope_tab(pos_p, 32)
    c["mc_p"] = cs
    c["ms_p"] = sn
    cs, sn = rope_tab(pos_s, 32)
    c["mc_s"] = cs
    c["ms_s"] = sn
    c["negtri"] = np.where(i128[:, None] <= i128[None, :], 0.0, NEG).astype(np.float32)
    c["negchunk"] = np.where((i128[:, None] // 64) <= (i128[None, :] // 64), 0.0, NEG).astype(np.float32)
    nc_ = np.zeros((128, 2, 64), np.float32)
    nc_[:, 0, 32:] = NEG
    nc_[:, 1, :32] = NEG
    c["negcol"] = nc_
    a = np.full((128, 128), 0.0, np.float32)
    a[:64, :64] = np.where(same & (sl[:, None] <= sl[None, :]), 0.0, NEG)
    c["fox_negnew"] = a
    a = np.full((128, 128), 0.0, np.float32)
    a[:64, :64] = np.where(same, 0.0, NEG)
    c["mla_negnew"] = a
    c["U"] = (i128[:, None] <= i128[None, :]).astype(np.float32)
    a = np.zeros((128, 128), np.float32)
    a[:64, :64] = (same & (sl[:, None] <= sl[None, :])).astype(np.float32)
    c["Ublk"] = a
    sel = np.zeros((128, 3, 128), np.float32)
    sel[127, 0, :] = 1.0
    sel[127, 1, :32] = 1.0
    sel[127, 2, 32:64] = 1.0
    c["SEL"] = sel
    return c


CONST_NAMES = ["ident", "ones", "maskY_p", "rdec_p", "maskY_s", "rdec_s", "colmask_s", "rc_p", "rs_p", "rc_s", "rs_s",
               "mc_p", "ms_p", "mc_s", "ms_s", "negtri", "negchunk", "negcol", "fox_negnew", "mla_negnew", "U", "Ublk", "SEL"]

IN_SHAPES = dict(
    x_prompt=(2, 2048, 1024), x_sample=(2, 32, 1024), state_ret=(2, 2, 4, 256, 512),
    cache_mla_latent=(1, 2, 2048, 256), cache_mla_krope=(1, 2, 2048, 64),
    cache_fox_k=(1, 2, 2048, 16, 64), cache_fox_v=(1, 2, 2048, 16, 64), cache_fox_logf=(1, 2, 2048, 16),
    p_prompt=(4, 2, 2048, 256), p_sample=(4, 2, 32, 256),
    norm_mix=(4, 1024), norm_ffn=(4, 1024), norm_pe=(4, 1024), norm_final=(1024,),
    ret_w_in=(2, 1024, 6144), ret_gn=(2, 2048), ret_w_out=(2, 2048, 1024),
    mla_w_in=(1, 1024, 832), mla_q_norm=(1, 512), mla_kv_norm=(1, 256), mla_w_qb=(1, 512, 3072),
    mla_w_kvb=(1, 256, 4096), mla_gq_nope=(1, 128), mla_gq_rope=(1, 64), mla_gk_nope=(1, 128), mla_gk_rope=(1, 64),
    mla_w_out=(1, 2048, 1024), fox_w_in=(1, 1024, 4112), fox_b_f=(1, 16), fox_gq=(1, 64), fox_gk=(1, 64),
    fox_w_out=(1, 1024, 1024), ffn_w_gate=(4, 1024, 2816), ffn_w_up=(4, 1024, 2816), ffn_w_down=(4, 2816, 1024),
    pe_w_proj=(4, 256, 1024), pe_w_gate=(4, 1024, 1024))
SHARDED = dict(x_prompt=0, x_sample=0, state_ret=1, cache_mla_latent=1, cache_mla_krope=1, cache_fox_k=1,
               cache_fox_v=1, cache_fox_logf=1, p_prompt=1, p_sample=1)
OUT_SHAPES = [
    ("y_prompt", (2, 2048, 1024), 0), ("y_sample", (2, 32, 1024), 0),
    ("ret_p", (2, 2, 4, 256, 512), 1), ("ret_s", (2, 2, 4, 256, 512), 1),
    ("lat_p", (1, 2, 2048, 256), 1), ("kr_p", (1, 2, 2048, 64), 1), ("lat_s", (1, 2, 32, 256), 1), ("kr_s", (1, 2, 32, 64), 1),
    ("fk_p", (1, 2, 2048, 16, 64), 1), ("fv_p", (1, 2, 2048, 16, 64), 1), ("flf_p", (1, 2, 2048, 16), 1),
    ("fk_s", (1, 2, 32, 16, 64), 1), ("fv_s", (1, 2, 32, 16, 64), 1), ("flf_s", (1, 2, 32, 16), 1)]


def emit(nc, dry, specs, CST):
    din = {}
    for n, shp in IN_SHAPES.items():
        din[n] = nc.dram_tensor(n, list(shp), F32, kind="ExternalInput").ap()
    for n in CONST_NAMES:
        din["c_" + n] = nc.dram_tensor("c_" + n, list(CST[n].shape), F32, kind="ExternalInput").ap()
    dout = {}
    for n, shp, _ in OUT_SHAPES:
        dout[n] = nc.dram_tensor(n, list(shp), F32, kind="ExternalOutput").ap()

    gs = ExitStack()
    with gs:
        P = Prog(nc, gs, dry)

        def sb(name, shape, dt, st=gs):
            return st.enter_context(nc.sbuf_tensor(name, list(shape), dt))

        WB = sb("WB", [128, NWB, 2048], BF)
        GB = sb("GB", [128, 1024], F32)
        TF = sb("TF", [128, 3, 512], F32)
        XS = sb("XS", [128, 3, 1024], BF)
        JUNK = sb("JUNK", [128, 1024], BF)
        ST = sb("ST", [128, 4, 8], F32)
        SS = sb("SS", [128, 16], F32)
        RS = sb("RS", [128, 16], F32)
        identf = sb("identf", [128, 128], F32)
        identb = sb("identb", [128, 128], BF)
        onesf = sb("onesf", [128, 128], F32)
        MASKY = sb("MASKY", [128, 4, 128], F32)
        RD = sb("RD", [128, 16], F32)
        NEGA = sb("NEGA", [128, 128], F32)
        NEGB = sb("NEGB", [128, 128], F32)
        NEGC = sb("NEGC", [128, 2, 64], F32)
        UT = sb("UT", [128, 128], F32)
        UBLK = sb("UBLK", [128, 128], F32)
        SEL = sb("SEL", [128, 3, 128], F32)
        CMK = sb("CMK", [128, 2, 64], BF)
        SMALL = sb("SMALL", [128, 512], F32)
        banks = [gs.enter_context(nc.psum_tensor("ps%d" % i, [128, 512], F32)) for i in range(8)]
        bankb = [b[:, :].bitcast(BF) for b in banks]
        psi = [0]
        PSN = [8]

        def ps():
            i = psi[0] % PSN[0]
            psi[0] += 1
            return i

        tfi = [0]
        TFS = [TF[:, i, :] for i in range(3)]
        XSS = [XS[:, i, :] for i in range(3)]
        STS = [ST[:, i, :] for i in range(4)]
        DEPTH = [2]

        def tf():
            tfi[0] += 1
            return TFS[tfi[0] % len(TFS)]

        xsi = [0]

        def xs():
            xsi[0] += 1
            return XSS[xsi[0] % len(XSS)]

        sti = [0]

        def stt_slot():
            sti[0] += 1
            return STS[sti[0] % len(STS)]

        class WStream:
            def __init__(self):
                self.i = 0
                self.issued = 0
                self.nst = 0
                self.rec = []
                self.la = LA

            def get(self, name, pre, r0, r1, c0, c1):
                kc = (r1 - r0) // 128
                n = c1 - c0
                assert kc * 128 == r1 - r0 and kc * n <= 2048, (name, r0, r1, c0, c1)
                i = self.i
                self.i += 1
                view = WB[:, i % NWB, 0:kc * n].rearrange("p (k n) -> p k n", n=n)
                spec = (name, pre, r0, r1, c0, c1)
                if dry:
                    self.rec.append(spec)
                    return view
                assert specs[i] == spec, (i, specs[i], spec)
                while self.issued < min(len(specs), i + 1 + self.la):
                    self._issue(self.issued)
                    self.issued += 1
                return view

            def _issue(self, j):
                name, pre, r0, r1, c0, c1 = specs[j]
                kc = (r1 - r0) // 128
                n = c1 - c0
                Wd = din[name]
                for ix in pre:
                    Wd = Wd[ix]
                src = Wd[r0:r1, c0:c1].rearrange("(k p) n -> p k n", p=128)
                dst = WB[:, j % NWB, 0:kc * n].rearrange("p (k n) -> p k n", n=n)
                P.dma(dst, src, q="pool")

        W = WStream()

        def load_const(dst, name, via_bf=False, shape=None):
            src = din["c_" + name]
            if via_bf:
                t = tf()
                v = t[:, 0:int(np.prod(src.shape[1:]))]
                if len(src.shape) == 3:
                    v = v.rearrange("p (a b) -> p a b", b=src.shape[2])
                P.dma(v, src, key="cst")
                P.copy("dve", dst, v)
            else:
                P.dma(dst, src, key="cst")

        load_const(identf[:, :], "ident")
        load_const(identb[:, :], "ident", via_bf=True)
        load_const(onesf[:, :], "ones")
        load_const(UT[:, :], "U")
        load_const(UBLK[:, :], "Ublk")
        load_const(SEL[:, :, :], "SEL")
        load_const(NEGC[:, :, :], "negcol")
        load_const(CMK[:, :, :], "colmask_s", via_bf=True)

        def bcast_row(dst, row_ap, key="bc"):
            P.dma(dst, row_ap.partition_broadcast(128), key=key)

        def run_kind(kind):
            ks = ExitStack()
            with ks:
                if kind == "p":
                    T, NT, R, NKT = 2048, 16, 128, 16
                    blocks = [(b * 512, 512) for b in range(4)]
                else:
                    T, NT, R, NKT = 64, 1, 64, 33
                    blocks = [(0, 64)]
                TK = NKT * 128
                nseq = 1 if kind == "p" else 2

                def kb(name, shape, dt):
                    return sb(name + kind, shape, dt, ks)

                H = kb("H", [128, NT, 1024], F32)
                XT = kb("XT", [128, 8, T], BF)
                if kind == "p":
                    A = kb("A", [128, 8192], BF)
                    B = kb("B", [128, 8192], BF)
                    C8 = kb("C8", [128, 4096], BF)
                    D8 = kb("D8", [128, 4096], BF)
                    E8 = kb("E8", [128, 4096], BF)
                    F8 = kb("F8", [128, 4096], BF)
                else:
                    A = kb("A", [128, 8704], BF)
                    B = kb("B", [128, 8704], BF)
                    C8 = kb("C8", [128, 4096], BF)
                    D8 = kb("D8", [128, 4352], BF)
                    E8 = kb("E8", [128, 4096], BF)
                    F8 = kb("F8", [128, 4096], BF)
                AT = {}
                if kind == "s":
                    AT["FALL"] = kb("FALL", [128, NKT, 16], F32)
                    AT["NEGF"] = kb("NEGF", [128, NKT, 16], F32)
                    AT["LF"] = kb("LF", [128, NKT, 16], F32)
                    AT["MCS"] = kb("MCS", [128, NT, 2, 32], F32)
                    AT["PTB"] = kb("PTB", [128, 2, 512], BF)
                    AT["ZB"] = kb("ZB", [128, 2, 512], F32)
                    AT["FQ"] = kb("FQ", [128, 2, 512], F32)
                    AT["TFX"] = kb("TFX", [128, 4, 512], F32)
                    AT["XSX"] = kb("XSX", [128, 4, 1024], BF)
                    AT["STX"] = kb("STX", [128, 16, 8], F32)
                    AT["QZ"] = kb("QZ", [128, 2, 512], BF)
                pti = [0]

                load_const(MASKY[:, :, :], "maskY_" + kind)
                load_const(RD[:, :], "rdec_" + kind)
                load_const(NEGA[:, :], "negtri" if kind == "p" else "fox_negnew")
                load_const(NEGB[:, :], "negchunk" if kind == "p" else "mla_negnew")
                if kind == "s":
                    P.dma(AT["MCS"][0:64, 0, 0, :], din["c_mc_s"], key="cst")
                    P.dma(AT["MCS"][0:64, 0, 1, :], din["c_ms_s"], key="cst")
                cdec = CST["cdec_" + kind]
                rc_d, rs_d = din["c_rc_" + kind], din["c_rs_" + kind]

                def run_pass(s):
                    def tokv(ap3):
                        if kind == "p":
                            return ap3[s]
                        return ap3.rearrange("b t f -> (b t) f")

                    def tcols(ti):
                        return slice(ti * 128, ti * 128 + R)

                    if kind == "p":
                        for q in range(4):
                            P.dma(H[:, 4 * q:4 * q + 4, :],
                                  din["x_prompt"][s, 512 * q:512 * (q + 1), :].rearrange("(t p) d -> p t d", p=128), key=("h", q))
                    else:
                        P.dma(H[0:64, 0, :], din["x_sample"].rearrange("b t d -> (b t) d"), key=("h", 0))

                    def norm_stats(gain_row):
                        bcast_row(GB[:, :], gain_row, key="gb")
                        for ti in range(NT):
                            P.act(JUNK[:R, :], H[:R, ti, :], AF.Square, accum=SS[:R, ti:ti + 1], junk=True)
                        P.act(RS[:R, :NT], SS[:R, :NT], AF.Sqrt, scale=1.0 / D, bias=EPS)
                        P.recip(RS[:R, :NT], RS[:R, :NT])

                    def norm_to_xt(gain_row):
                        norm_stats(gain_row)
                        dq = Defer(2)
                        for ti in range(NT):
                            x = xs()
                            P.stt(x[:R, :], H[:R, ti, :], RS[:R, ti:ti + 1], GB[:R, :], ALU.mult, ALU.mult)

                            def pe_part(ti=ti, x=x):
                                b = ps()
                                pv = bankb[b][:, 0:1024].rearrange("p (c r) -> p c r", r=128)
                                for c in range(8):
                                    P.tr(pv[:, c, :R], x[:R, c * 128:(c + 1) * 128], identb[:R, :R])
                                P.copy("act", XT[:, :, tcols(ti)], pv[:, :, :R])

                            dq.push(pe_part)
                        dq.flush()

                    def lin_tm(src, wviews, ncols, evac, tiles=None):
                        rh = []
                        for v in wviews:
                            for k in range(v.shape[1]):
                                rh.append(v[:, k, :])
                        dq = Defer(DEPTH[0])
                        for ti in (range(NT) if tiles is None else tiles):
                            b = ps()
                            out = banks[b][:R, :ncols]
                            for i, r_ in enumerate(rh):
                                P.mm(out, src(i, ti), r_, i == 0, i == len(rh) - 1)
                            dq.push(evac(ti, out))
                        dq.flush()

                    def xt_src(i, ti):
                        return XT[:, i, tcols(ti)]

                    def add_to_h(ti, c0, n, psum):
                        P.tt("dve", H[:R, ti, c0:c0 + n], psum, H[:R, ti, c0:c0 + n], ALU.add)

                    def transpose_to(dst_fn, src_tile, nchunks, ti, eng="act"):
                        b = ps()
                        pv = bankb[b][:, 0:1024].rearrange("p (c r) -> p c r", r=128)
                        for c in range(nchunks):
                            P.tr(pv[:, c, :R], src_tile[:R, c * 128:(c + 1) * 128], identb[:R, :R])
                        for c in range(nchunks):
                            P.copy(eng, dst_fn(c), pv[:, c, :R])

                    def retention(j):
                        QT = C8[:, 0:4096].rearrange("p (m t) -> p m t", m=2)
                        KT = D8[:, 0:4096].rearrange("p (m t) -> p m t", m=2)
                        V = A[:, 0:8192].rearrange("p (c v) -> p c v", v=512)
                        YT = A[:, 0:8192].rearrange("p (k t) -> p k t", k=4)
                        Y = B[:, 0:8192].rearrange("p (c v) -> p c v", v=512)
                        RT = B[:, 0:8192].bitcast(F32).rearrange("p (a n) -> p a n", n=512)
                        if nseq == 1:
                            Sf = E8[:, 0:2048].bitcast(F32).rearrange("p (i m v) -> p i m v", i=1, m=2)
                            Sb = E8[:, 2048:3072].rearrange("p (i m v) -> p i m v", i=1, m=2)
                            Sb2 = [Sb, E8[:, 3072:4096].rearrange("p (i m v) -> p i m v", i=1, m=2)]
                            TAB = F8[:, 0:4096].bitcast(F32).rearrange("p (s a n) -> p s a n", s=2, a=2)
                        else:
                            Sf = E8[:, 0:4096].bitcast(F32).rearrange("p (i m v) -> p i m v", i=2, m=2)
                            Sb = F8[:, 0:2048].rearrange("p (i m v) -> p i m v", i=2, m=2)
                            TAB = F8[:, 2048:2560].bitcast(F32).rearrange("p (s a n) -> p s a n", s=1, a=2)
                            QM = F8[:, 2560:2816].rearrange("p (i m t) -> p i m t", i=2, m=2)
                        for h in range(4):
                            wq = W.get("ret_w_in", (j,), 0, 1024, h * 256, h * 256 + 256)
                            wk = W.get("ret_w_in", (j,), 0, 1024, 1024 + h * 256, 1024 + h * 256 + 256)
                            rti = 0
                            for bi, (b0, bn) in enumerate(blocks):
                                sl = bi % TAB.shape[1]
                                P.dma(TAB[:, sl, 0, :bn], rc_d[:, b0:b0 + bn], key=("tab", sl))
                                P.dma(TAB[:, sl, 1, :bn], rs_d[:, b0:b0 + bn], key=("tab", sl))
                                cos, sin = TAB[:, sl, 0, :bn], TAB[:, sl, 1, :bn]
                                for wv, dst in ((wq, QT), (wk, KT)):
                                    bk = []
                                    for m in range(2):
                                        b = ps()
                                        bk.append(banks[b][:, :bn])
                                        for k in range(8):
                                            P.mm(bk[m], wv[:, k, m * 128:(m + 1) * 128], XT[:, k, b0:b0 + bn], k == 0, k == 7)
                                    t = [RT[:, (rti % 2) * 4 + a, :bn] for a in range(4)]
                                    rti += 1
                                    P.tt("dve", t[0], bk[0], cos, ALU.mult)
                                    P.tt("dve", t[1], bk[1], sin, ALU.mult)
                                    P.tt("dve", t[2], bk[1], cos, ALU.mult)
                                    P.tt("dve", t[3], bk[0], sin, ALU.mult)
                                    P.tt("dve", dst[:, 0, b0:b0 + bn], t[0], t[1], ALU.subtract)
                                    P.tt("pool", dst[:, 1, b0:b0 + bn], t[2], t[3], ALU.add)
                            if nseq == 2:
                                for i in range(2):
                                    for m in range(2):
                                        P.tt("pool", QM[:, i, m, :], QT[:, m, 0:64], CMK[:, i, :], ALU.mult)
                            wv0 = W.get("ret_w_in", (j,), 0, 512, 2048 + h * 512, 2048 + h * 512 + 512)
                            wv1 = W.get("ret_w_in", (j,), 512, 1024, 2048 + h * 512, 2048 + h * 512 + 512)
                            lin_tm(xt_src, [wv0, wv1], 512, lambda ti, o: P.copy("act", V[:R, ti, :], o))
                            bcast_row(GB[:, 0:512], din["ret_gn"][j, h * 512:(h + 1) * 512], key="gb")
                            if kind == "p":
                                P.memset("pool", Sf[:, 0, :, :], 0.0)
                                P.memset("pool", Sb[:, 0, :, :], 0.0)
                            else:
                                for i in range(2):
                                    P.dma(Sf[:, i, :, :], din["state_ret"][j, i, h].rearrange("(m p) v -> p m v", p=128), key=("sin", i))
                                    P.copy("pool", Sb[:, i, :, :], Sf[:, i, :, :])
                            PSN[0] = 4
                            for c in range(NT):
                                cols = tcols(c)
                                sb_r = Sb if nseq == 2 else Sb2[c % 2]
                                sb_w = Sb if nseq == 2 else Sb2[(c + 1) % 2]
                                bT = ps()
                                ktv = bankb[bT][:, 0:256]
                                for m in range(2):
                                    P.tr(ktv[:R, m * 128:(m + 1) * 128], KT[:, m, cols], identb[:, :])
                                kds = []
                                for i in range(nseq):
                                    kd = xs()
                                    kcol = (4 + h) if nseq == 1 else (4 + 2 * h + i)
                                    P.act(kd[:R, 0:256], ktv[:R, 0:256], AF.Copy, scale=RD[:R, kcol:kcol + 1])
                                    kds.append(kd)
                                bA = ps()
                                att = banks[bA][:R, :R]
                                for m in range(2):
                                    P.mm(att, KT[:, m, cols], QT[:, m, cols], m == 0, m == 1)
                                at = xs()
                                P.tt("dve", at[:R, :R], att, MASKY[:R, h, :R], ALU.mult)
                                bO = ps()
                                o = banks[bO][:R, :512]
                                P.mm(o, at[:R, :R], V[:R, c, :], True, False)
                                if nseq == 1:
                                    for m in range(2):
                                        P.mm(o, QT[:, m, cols], sb_r[:, 0, m, :], False, m == 1)
                                else:
                                    for i in range(2):
                                        for m in range(2):
                                            P.mm(o, QM[:, i, m, :], sb_r[:, i, m, :], False, i == 1 and m == 1)
                                osb = tf()
                                st = stt_slot()
                                P.act(osb[:R, :], o, AF.Copy, scale=RD[:R, h:h + 1], accum=st[:R, 0:1])
                                for i in range(nseq):
                                    for m in range(2):
                                        sp_ = banks[4 + ((2 * i + m) % 4)][:, :512]
                                        P.mm(sp_, kds[i][:R, m * 128:(m + 1) * 128], V[:R, c, :], True, True)
                                        P.stt(Sf[:, i, m, :], Sf[:, i, m, :], cdec[h], sp_, ALU.mult, ALU.add)
                                for i in range(nseq):
                                    P.copy("act", sb_w[:, i, :, :], Sf[:, i, :, :])
                                P.act(JUNK[:R, 0:512], osb[:R, :], AF.Square, accum=st[:R, 1:2], junk=True)
                                P.ts("dve", st[:R, 2:3], st[:R, 0:1], 1.0 / 512, ALU.mult)
                                P.tt("dve", st[:R, 3:4], st[:R, 2:3], st[:R, 2:3], ALU.mult)
                                P.stt(st[:R, 4:5], st[:R, 1:2], 1.0 / 512, st[:R, 3:4], ALU.mult, ALU.subtract)
                                P.act(st[:R, 5:6], st[:R, 4:5], AF.Sqrt, bias=EPS)
                                P.recip(st[:R, 5:6], st[:R, 5:6])
                                P.stt(st[:R, 6:7], st[:R, 2:3], -1.0, st[:R, 5:6], ALU.mult, ALU.mult)
                                o2 = tf()
                                P.act(o2[:R, :], osb[:R, :], AF.Identity, scale=st[:R, 5:6], bias=st[:R, 6:7])
                                P.tt("dve", Y[:R, c, :], o2[:R, :], GB[:R, 0:512], ALU.mult)
                            PSN[0] = 8
                            od = dout["ret_p"] if kind == "p" else dout["ret_s"]
                            for i in range(nseq):
                                sq_ = s if kind == "p" else i
                                P.dma(od[j, sq_, h].rearrange("(m p) v -> p m v", p=128), Sf[:, i, :, :], key=("sout", i))
                            wg0 = W.get("ret_w_in", (j,), 0, 512, 4096 + h * 512, 4096 + h * 512 + 512)
                            wg1 = W.get("ret_w_in", (j,), 512, 1024, 4096 + h * 512, 4096 + h * 512 + 512)

                            def g_evac(ti, o):
                                g = xs()
                                P.act(g[:R, 0:512], o, AF.Silu)
                                P.tt("dve", Y[:R, ti, :], Y[:R, ti, :], g[:R, 0:512], ALU.mult)
                                return lambda: transpose_to(lambda cc: YT[:, cc, tcols(ti)], Y[:, ti, :], 4, ti, eng="dve")

                            lin_tm(xt_src, [wg0, wg1], 512, g_evac)
                            wo = [W.get("ret_w_out", (j,), h * 512, h * 512 + 512, hf * 512, hf * 512 + 512) for hf in range(2)]
                            for hf in range(2):
                                lin_tm(lambda i, ti: YT[:, i, tcols(ti)], [wo[hf]], 512,
                                       lambda ti, o, hf=hf: add_to_h(ti, hf * 512, 512, o))

                    def ffn(li):
                        norm_to_xt(din["norm_ffn"][li])
                        HT = A[:, 0:8192].rearrange("p (m t) -> p m t", m=4)
                        for g0 in range(0, 2816, 512):
                            g1 = min(2816, g0 + 512)
                            nch = (g1 - g0) // 128
                            for half in range(0, nch, 2):
                                c0 = g0 + half * 128
                                wg = W.get("ffn_w_gate", (li,), 0, 1024, c0, c0 + 256)
                                wu = W.get("ffn_w_up", (li,), 0, 1024, c0, c0 + 256)
                                for mm_ in range(2):
                                    m = half + mm_
                                    for (b0, bn) in blocks:
                                        bg, bu = ps(), ps()
                                        pg, pu = banks[bg][:, :bn], banks[bu][:, :bn]
                                        for k in range(8):
                                            P.mm(pg, wg[:, k, mm_ * 128:(mm_ + 1) * 128], XT[:, k, b0:b0 + bn], k == 0, k == 7)
                                        for k in range(8):
                                            P.mm(pu, wu[:, k, mm_ * 128:(mm_ + 1) * 128], XT[:, k, b0:b0 + bn], k == 0, k == 7)
                                        sg = tf()
                                        P.act(sg[:, :bn], pg, AF.Silu)
                                        P.tt("dve", HT[:, m, b0:b0 + bn], pu, sg[:, :bn], ALU.mult)
                            wd = [W.get("ffn_w_down", (li,), g0, g1, hf * 512, hf * 512 + 512) for hf in range(2)]
                            for hf in range(2):
                                lin_tm(lambda i, ti: HT[:, i, tcols(ti)], [wd[hf]], 512,
                                       lambda ti, o, hf=hf: add_to_h(ti, hf * 512, 512, o))

                    def pegate(li):
                        norm_to_xt(din["norm_pe"][li])
                        PT_ = F8[:, 0:4096].rearrange("p (m t) -> p m t", m=2)
                        pd = tokv(din["p_prompt"][li] if kind == "p" else din["p_sample"][li])
                        dq = Defer(2)
                        for ti in range(NT):
                            st_ = tf()
                            P.dma(st_[:R, 0:256], pd[ti * 128:ti * 128 + R, :], key=("pin", tfi[0] % 3))
                            xb = xs()
                            P.copy("pool", xb[:R, 0:256], st_[:R, 0:256])
                            dq.push(lambda ti=ti, xb=xb: transpose_to(lambda cc: PT_[:, cc, tcols(ti)], xb, 2, ti))
                        dq.flush()
                        for q4 in range(4):
                            wga = W.get("pe_w_gate", (li,), 0, 1024, q4 * 256, q4 * 256 + 256)
                            wp = W.get("pe_w_proj", (li,), 0, 256, q4 * 256, q4 * 256 + 256)
                            for ti in range(NT):
                                bg, bp = ps(), ps()
                                pg, pp = banks[bg][:R, :256], banks[bp][:R, :256]
                                for k in range(8):
                                    P.mm(pg, XT[:, k, tcols(ti)], wga[:, k, :], k == 0, k == 7)
                                for k in range(2):
                                    P.mm(pp, PT_[:, k, tcols(ti)], wp[:, k, :], k == 0, k == 1)
                                sg = tf()
                                P.act(sg[:R, 0:256], pg, AF.Sigmoid)
                                P.tt("dve", sg[:R, 256:512], pp, sg[:R, 0:256], ALU.mult)
                                P.tt("pool", H[:R, ti, q4 * 256:(q4 + 1) * 256], H[:R, ti, q4 * 256:(q4 + 1) * 256], sg[:R, 256:512], ALU.add)

                    def attn(h_q, st_mm, vext, dv, kind_mask, negf_col, fq_build, scale, evac):
                        def blk_post(stp, kt, krows, q0, n, tot_q0, mask, fqs):
                            pt = AT["PTB"][:, pti[0] % 2, :]
                            pti[0] += 1
                            c0 = q0 - tot_q0
                            bias = negf_col(kt, krows)
                            if fqs is not None:
                                if mask is not None:
                                    mw = mask.shape[1]
                                    tm = tf()
                                    P.tt("pool", tm[:krows, :mw], fqs[:krows, c0:c0 + mw], mask, ALU.add)
                                    P.tt("dve", stp[:, 0:mw], stp[:, 0:mw], tm[:krows, :mw], ALU.add)
                                    if n > mw:
                                        P.tt("dve", stp[:, mw:n], stp[:, mw:n], fqs[:krows, c0 + mw:c0 + n], ALU.add)
                                else:
                                    P.tt("dve", stp, stp, fqs[:krows, c0:c0 + n], ALU.add)
                                P.act(pt[:krows, c0:c0 + n], stp, AF.Exp, scale=scale, bias=bias)
                            else:
                                if mask is not None:
                                    mw = mask.shape[1]
                                    P.tt("dve", stp[:, 0:mw], stp[:, 0:mw], mask, ALU.add)
                                P.act(pt[:krows, c0:c0 + n], stp, AF.Exp, scale=scale)
                            return pt, c0

                        def blk_pv(ptc, kt, krows, accs, first, lastmap):
                            pt, c0 = ptc
                            for (jq, acc, qr) in accs:
                                if jq * 128 < c0:
                                    continue
                                P.mm(acc, pt[:krows, jq * 128:jq * 128 + qr], vext(kt, krows), first, lastmap[jq] == kt)

                        def run_blocks(descs, tot_q0, accs, lastmap, fqs, hook=None):
                            LOOK = 2
                            nb = len(descs)
                            sts = {}
                            for i in range(min(LOOK, nb)):
                                d = descs[i]
                                sts[i] = st_mm(d[0], d[1], d[2], d[3])
                            d = descs[0]
                            posts = {0: blk_post(sts.pop(0), d[0], d[1], d[2], d[3], tot_q0, d[4], fqs)}
                            for i in range(nb):
                                if i + LOOK < nb:
                                    d = descs[i + LOOK]
                                    sts[i + LOOK] = st_mm(d[0], d[1], d[2], d[3])
                                if i + 1 < nb:
                                    d = descs[i + 1]
                                    posts[i + 1] = blk_post(sts.pop(i + 1), d[0], d[1], d[2], d[3], tot_q0, d[4], fqs)
                                d = descs[i]
                                blk_pv(posts.pop(i), d[0], d[1], accs, d[5], lastmap)
                                if i == 1 and hook is not None:
                                    hook()

                        PSN[0] = 4
                        if kind == "p":
                            nxt = [fq_build(0) if fq_build else None]
                            for qb in range(4):
                                fqs = nxt[0]

                                def hook(qb=qb):
                                    if fq_build and qb + 1 < 4:
                                        nxt[0] = fq_build(qb + 1)

                                accs = [(jq, banks[4 + jq][:128, :dv + 1], 128) for jq in range(4)]
                                lastmap = {jq: 4 * qb + jq for jq in range(4)}
                                descs = []
                                for kt in range(4 * qb + 4):
                                    jmin = max(0, kt - 4 * qb)
                                    q0 = qb * 512 + jmin * 128
                                    mask = kind_mask if kt >= 4 * qb else None
                                    descs.append((kt, 128, q0, 512 - jmin * 128, mask, kt == 0))
                                run_blocks(descs, qb * 512, accs, lastmap, fqs, hook)
                                dq = Defer(2)
                                for jq in range(4):
                                    dq.push(evac(4 * qb + jq, accs[jq][1]))
                                dq.flush()
                        else:
                            fqs = fq_build(0) if fq_build else None
                            accs = [(0, banks[4][:64, :dv + 1], 64)]
                            lastmap = {0: 32}
                            descs = []
                            for kt in range(33):
                                if kt < 32:
                                    descs.append((kt, 128, 0, 64, NEGC[:, kt // 16, :], kt == 0))
                                else:
                                    descs.append((kt, 64, 0, 64, kind_mask[:64, :64], False))
                            run_blocks(descs, 0, accs, lastmap, fqs)
                            p_ = evac(0, accs[0][1])
                            if p_ is not None:
                                p_()
                        PSN[0] = 8

                    def mla(li):
                        norm_to_xt(din["norm_mix"][li])
                        if kind == "p":
                            AT["MCS"] = E8[:, 0:2048].bitcast(F32).rearrange("p (t a f) -> p t a f", a=2, f=32)
                            AT["PTB"] = E8[:, 2048:3072].rearrange("p (s n) -> p s n", s=2)
                            AT["ZB"] = F8[:, 0:2048].bitcast(F32).rearrange("p (s n) -> p s n", s=2)
                            AT["FQ"] = F8[:, 2048:4096].bitcast(F32).rearrange("p (s n) -> p s n", s=2)
                            P.dma(AT["MCS"][:, :, 0, :], din["c_mc_p"].rearrange("(t p) f -> p t f", p=128))
                            P.dma(AT["MCS"][:, :, 1, :], din["c_ms_p"].rearrange("(t p) f -> p t f", p=128))
                        MCS = AT["MCS"]
                        if kind == "p":
                            ex_tf = [XT[:, 4 + i // 2, (i % 2) * 1024:(i % 2) * 1024 + 1024].bitcast(F32) for i in range(3)]
                            ex_st_base = XT[:, 5, 1024:2048].bitcast(F32).rearrange("p (s e) -> p s e", e=8)
                            ex_st = [ex_st_base[:, i, :] for i in range(16)]
                            ex_xs = [XT[:, 6 + i // 2, (i % 2) * 1024:(i % 2) * 1024 + 1024] for i in range(4)]
                        else:
                            ex_tf = [AT["TFX"][:, i, :] for i in range(4)]
                            ex_st = [AT["STX"][:, i, :] for i in range(16)]
                            ex_xs = [AT["XSX"][:, i, :] for i in range(4)]
                        n_tf0, n_xs0, n_st0 = len(TFS), len(XSS), len(STS)
                        if kind == "p":
                            CQ = A[:, 0:8192].rearrange("p (k t) -> p k t", k=4)
                            LATT = B[:, 0:4096].rearrange("p (k t) -> p k t", k=2)
                            KRT = B[:, 4096:6144]
                            KNT = B[:, 6144:8192]
                            QNT = C8[:, 0:2048]
                            QRT = C8[:, 2048:4096]
                            VE = D8[:, 0:16 * 129].rearrange("p (t v) -> p t v", v=129)
                            OTG = XT[:, 0:4, :]
                        else:
                            CQ = C8[:, 0:256].rearrange("p (k t) -> p k t", k=4)
                            LATT = A[:, 0:2 * TK].rearrange("p (k t) -> p k t", k=2)
                            KRT = B[:, 0:TK]
                            KNT = B[:, TK:2 * TK]
                            QNT = C8[:, 256:320]
                            QRT = C8[:, 320:384]
                            VE = D8[:, 0:33 * 129].rearrange("p (t v) -> p t v", v=129)
                            OTG = C8[:, 512:768].rearrange("p (k t) -> p k t", k=4)
                        bcast_row(SMALL[:, 0:128], din["mla_gq_nope"][0], key="sm")
                        bcast_row(SMALL[:, 128:192], din["mla_gq_rope"][0], key="sm")
                        bcast_row(SMALL[:, 192:320], din["mla_gk_nope"][0], key="sm")
                        bcast_row(SMALL[:, 320:384], din["mla_gk_rope"][0], key="sm")
                        first = [True]

                        def rope_tm(dst_f32, src_f32, ti, t):
                            cs, sn = MCS[:R, ti, 0, :], MCS[:R, ti, 1, :]
                            P.tt("dve", t[:R, 0:32], src_f32[:, 0:32], cs, ALU.mult)
                            P.tt("dve", t[:R, 32:64], src_f32[:, 32:64], sn, ALU.mult)
                            P.tt("dve", t[:R, 64:96], src_f32[:, 32:64], cs, ALU.mult)
                            P.tt("dve", t[:R, 96:128], src_f32[:, 0:32], sn, ALU.mult)
                            P.tt("dve", dst_f32[:, 0:32], t[:R, 0:32], t[:R, 32:64], ALU.subtract)
                            P.tt("dve", dst_f32[:, 32:64], t[:R, 64:96], t[:R, 96:128], ALU.add)

                        def rms_tm(psum, n, gain, out, st, col):
                            P.act(JUNK[:psum.shape[0], 0:n], psum, AF.Square, accum=st[:psum.shape[0], col:col + 1], junk=True)
                            P.act(st[:psum.shape[0], col + 1:col + 2], st[:psum.shape[0], col:col + 1], AF.Sqrt, scale=1.0 / n, bias=EPS)
                            P.recip(st[:psum.shape[0], col + 1:col + 2], st[:psum.shape[0], col + 1:col + 2])
                            P.stt(out, psum, st[:psum.shape[0], col + 1:col + 2], gain, ALU.mult, ALU.mult)

                        key_new0 = (NKT - 1) * 128 if kind == "s" else 0

                        def kcols_new(ti):
                            return slice(key_new0 + ti * 128, key_new0 + ti * 128 + R)

                        bcast_row(GB[:, 0:512], din["mla_q_norm"][0], key="gb")
                        bcast_row(GB[:, 512:768], din["mla_kv_norm"][0], key="gb")

                        def cq_evac(ti, o):
                            st = stt_slot()
                            x = xs()
                            rms_tm(o, 512, GB[:R, 0:512], x[:R, 0:512], st, 0)
                            return lambda: transpose_to(lambda cc: CQ[:, cc, tcols(ti)], x, 4, ti)

                        w0 = W.get("mla_w_in", (0,), 0, 512, 0, 512)
                        w1 = W.get("mla_w_in", (0,), 512, 1024, 0, 512)
                        lin_tm(xt_src, [w0, w1], 512, cq_evac)

                        def kv_evac(ti, o):
                            st = stt_slot()
                            lat = tf()
                            rms_tm(o[:, 0:256], 256, GB[:R, 512:768], lat[:R, 0:256], st, 0)
                            rms_tm(o[:, 256:320], 64, SMALL[:R, 320:384], lat[:R, 320:384], st, 2)
                            rope_tm(lat[:R, 256:320], lat[:R, 320:384], ti, lat[:, 384:512])
                            ld = tokv(dout["lat_p"][0] if kind == "p" else dout["lat_s"][0])
                            kd_ = tokv(dout["kr_p"][0] if kind == "p" else dout["kr_s"][0])
                            P.dma(ld[ti * 128:ti * 128 + R, :], lat[:R, 0:256], key=("mo", tfi[0] % 3))
                            P.dma(kd_[ti * 128:ti * 128 + R, :], lat[:R, 256:320], key=("mo", tfi[0] % 3))
                            x = xs()
                            P.copy("pool", x[:R, 0:320], lat[:R, 0:320])

                            def pe_part():
                                b = ps()
                                pv = bankb[b]
                                for c in range(2):
                                    P.tr(pv[:, c * 128:c * 128 + R], x[:R, c * 128:(c + 1) * 128], identb[:R, :R])
                                P.tr(pv[0:64, 256:256 + R], x[:R, 256:320], identb[:R, :R])
                                for c in range(2):
                                    P.copy("act", LATT[:, c, kcols_new(ti)], pv[:, c * 128:c * 128 + R])
                                P.copy("act", KRT[0:64, kcols_new(ti)], pv[0:64, 256:256 + R])

                            return pe_part

                        w2 = W.get("mla_w_in", (0,), 0, 512, 512, 832)
                        w3 = W.get("mla_w_in", (0,), 512, 1024, 512, 832)
                        lin_tm(xt_src, [w2, w3], 320, kv_evac)
                        if kind == "s":
                            for i in range(2):
                                for q in range(16):
                                    kt = 16 * i + q
                                    st_ = tf()
                                    P.dma(st_[:, 0:256], din["cache_mla_latent"][0, i, q * 128:(q + 1) * 128, :], key=("pin", tfi[0] % 3))
                                    P.dma(st_[:, 256:320], din["cache_mla_krope"][0, i, q * 128:(q + 1) * 128, :], key=("pin", tfi[0] % 3))
                                    x = xs()
                                    P.copy("pool", x[:, 0:320], st_[:, 0:320])
                                    b = ps()
                                    pv = bankb[b]
                                    for c in range(2):
                                        P.tr(pv[:, c * 128:(c + 1) * 128], x[:, c * 128:(c + 1) * 128], identb[:, :])
                                    P.tr(pv[0:64, 256:384], x[:, 256:320], identb[:, :])
                                    for c in range(2):
                                        P.copy("act", LATT[:, c, kt * 128:(kt + 1) * 128], pv[:, c * 128:(c + 1) * 128])
                                    P.copy("act", KRT[0:64, kt * 128:(kt + 1) * 128], pv[0:64, 256:384])
                        P.memset("pool", VE[:, :, 128:129], 1.0)
                        P.memset("pool", KRT[64:128, :], 0.0)
                        P.memset("pool", QRT[64:128, :], 0.0)
                        scale = 192.0 ** -0.5
                        TFS.extend(ex_tf)
                        XSS.extend(ex_xs)
                        STS.extend(ex_st)
                        DEPTH[0] = 4
                        for hg in range(4):
                            W.la = 2
                            wkv = W.get("mla_w_kvb", (0,), 0, 256, hg * 1024, hg * 1024 + 1024)
                            for hh in range(4):
                                h = hg * 4 + hh
                                if hh % 2 == 0:
                                    wq2 = W.get("mla_w_qb", (0,), 0, 512, (hg * 4 + hh) * 192, (hg * 4 + hh) * 192 + 384)
                                wq = wq2[:, :, (hh % 2) * 192:(hh % 2) * 192 + 192]

                                def q_evac(ti, o):
                                    st = stt_slot()
                                    x = xs()
                                    rms_tm(o[:, 0:128], 128, SMALL[:R, 0:128], x[:R, 0:128], st, 0)
                                    qr = tf()
                                    rms_tm(o[:, 128:192], 64, SMALL[:R, 128:192], qr[:R, 128:192], st, 2)
                                    rope_tm(x[:R, 128:192], qr[:R, 128:192], ti, qr[:, 256:384])

                                    def pe_part():
                                        b = ps()
                                        pv = bankb[b]
                                        P.tr(pv[:, 0:R], x[:R, 0:128], identb[:R, :R])
                                        P.tr(pv[0:64, 128:128 + R], x[:R, 128:192], identb[:R, :R])
                                        P.copy("act", QNT[:, tcols(ti)], pv[:, 0:R])
                                        P.copy("act", QRT[0:64, tcols(ti)], pv[0:64, 128:128 + R])

                                    return pe_part

                                lin_tm(lambda i, ti: CQ[:, i, tcols(ti)], [wq], 192, q_evac)
                                dq = Defer(DEPTH[0])
                                for kt in range(NKT):
                                    kr_ = R if (kind == "s" and kt == 32) else 128
                                    b = ps()
                                    o = banks[b][:kr_, :256]
                                    for k in range(2):
                                        P.mm(o, LATT[:, k, kt * 128:kt * 128 + kr_], wkv[:, k, hh * 256:(hh + 1) * 256], k == 0, k == 1)
                                    st = stt_slot()
                                    x = xs()
                                    rms_tm(o[:, 0:128], 128, SMALL[:kr_, 192:320], x[:kr_, 0:128], st, 0)
                                    P.copy("act", VE[:kr_, kt, 0:128], o[:, 128:256])

                                    def pe_part(kt=kt, kr_=kr_, x=x):
                                        b2 = ps()
                                        pv = bankb[b2]
                                        P.tr(pv[:, 0:kr_], x[:kr_, 0:128], identb[:kr_, :kr_])
                                        P.copy("dve", KNT[:, kt * 128:kt * 128 + kr_], pv[:, 0:kr_])

                                    dq.push(pe_part)
                                dq.flush()

                                def st_mm(kt, krows, q0, n):
                                    b = ps()
                                    o = banks[b][:krows, :n]
                                    P.mm(o, KNT[:, kt * 128:kt * 128 + krows], QNT[:, q0:q0 + n], True, False)
                                    P.mm(o, KRT[:, kt * 128:kt * 128 + krows], QRT[:, q0:q0 + n], False, True)
                                    return o

                                def o_evac(ti, acc):
                                    st = stt_slot()
                                    rr = acc.shape[0]
                                    P.recip(st[:rr, 0:1], acc[:, 128:129])
                                    x = xs()
                                    P.ts("dve", x[:rr, 0:128], acc[:, 0:128], st[:rr, 0:1], ALU.mult)

                                    def pe_part():
                                        b = ps()
                                        pv = bankb[b]
                                        P.tr(pv[:, 0:rr], x[:rr, 0:128], identb[:rr, :rr])
                                        P.copy("act", OTG[:, hh, tcols(ti)], pv[:, 0:rr])

                                    return pe_part

                                attn(h, st_mm, lambda kt, kr: VE[:kr, kt, :], 128, NEGB[:, :], lambda kt, kr: None, None, scale, o_evac)
                            wo = [W.get("mla_w_out", (0,), hg * 512, hg * 512 + 512, hf * 512, hf * 512 + 512) for hf in range(2)]
                            W.la = LA
                            for hf in range(2):
                                lin_tm(lambda i, ti: OTG[:, i, tcols(ti)], [wo[hf]], 512,
                                       lambda ti, o, hf=hf: add_to_h(ti, hf * 512, 512, o))
                        del TFS[n_tf0:]
                        del XSS[n_xs0:]
                        del STS[n_st0:]
                        DEPTH[0] = 2

                    def fox(li):
                        norm_to_xt(din["norm_mix"][li])
                        key_new0 = (NKT - 1) * 128 if kind == "s" else 0
                        knew = NKT - 1 if kind == "s" else None
                        if kind == "p":
                            AT["FALL"] = A[:, 4224:4736].bitcast(F32).rearrange("p (t h) -> p t h", h=16)
                            AT["NEGF"] = A[:, 4736:5248].bitcast(F32).rearrange("p (t h) -> p t h", h=16)
                            AT["LF"] = A[:, 5248:5760].bitcast(F32).rearrange("p (t h) -> p t h", h=16)
                            AT["PTB"] = A[:, 6144:7168].rearrange("p (s n) -> p s n", s=2)
                            AT["ZB"] = F8[:, 0:2048].bitcast(F32).rearrange("p (s n) -> p s n", s=2)
                            AT["FQ"] = F8[:, 2048:4096].bitcast(F32).rearrange("p (s n) -> p s n", s=2)
                        if kind == "p":
                            AT["QZ"] = A[:, 7168:8192].rearrange("p (s n) -> p s n", s=2)
                        FALL, NEGF, LF, FQ = AT["FALL"], AT["NEGF"], AT["LF"], AT["FQ"]
                        QZ = AT["QZ"]

                        if kind == "p":
                            VE = A[:, 0:16 * 260].rearrange("p (t h v) -> p t h v", h=4, v=65)
                            KT = B[:, 0:4096].rearrange("p (m t) -> p m t", m=2)
                            OTOK = B[:, 4096:8192].rearrange("p (t v) -> p t v", v=256)
                            QT = C8[:, 0:4096].rearrange("p (m t) -> p m t", m=2)
                            G = D8[:, 0:4096].rearrange("p (t v) -> p t v", v=256)
                            OT = E8[:, 0:4096].rearrange("p (m t) -> p m t", m=2)
                        else:
                            VE = A[:, 0:33 * 260].rearrange("p (t h v) -> p t h v", h=4, v=65)
                            KT = B[:, 0:2 * TK].rearrange("p (m t) -> p m t", m=2)
                            OTOK = C8[:, 0:256].rearrange("p (t v) -> p t v", v=256)
                            QT = C8[:, 256:384].rearrange("p (m t) -> p m t", m=2)
                            G = C8[:, 512:768].rearrange("p (t v) -> p t v", v=256)
                            OT = C8[:, 1024:1152].rearrange("p (m t) -> p m t", m=2)
                        bcast_row(SMALL[:, 0:64], din["fox_gq"][0], key="sm")
                        bcast_row(SMALL[:, 64:128], din["fox_gk"][0], key="sm")
                        bcast_row(SMALL[:, 128:144], din["fox_b_f"][0], key="sm")
                        for a in range(4):
                            P.copy("pool", SMALL[:, 256 + a * 64:256 + (a + 1) * 64], SMALL[:, 0:64])
                        for a in range(4):
                            P.copy("pool", GB[:, a * 64:(a + 1) * 64], SMALL[:, 64:128])
                        GQ4 = SMALL[:, 256:512]
                        GK4 = GB[:, 0:256]
                        wf = W.get("fox_w_in", (0,), 0, 1024, 4096, 4112)
                        lq = NKT - 1 if kind == "s" else 0

                        def f_evac(ti, o):
                            t = tf()
                            P.tt("dve", t[:R, 0:16], o, SMALL[:R, 128:144], ALU.add)
                            P.act(t[:R, 16:32], t[:R, 0:16], AF.Exp, scale=-1.0)
                            P.act(t[:R, 32:48], t[:R, 16:32], AF.Ln, bias=1.0)
                            P.ts("dve", LF[:R, lq + ti, :], t[:R, 32:48], -1.0, ALU.mult)
                            fd = tokv(dout["flf_p"][0] if kind == "p" else dout["flf_s"][0])
                            P.dma(fd[ti * 128:ti * 128 + R, :], LF[:R, lq + ti, :], key="lfo")

                        lin_tm(xt_src, [wf], 16, f_evac)
                        if kind == "s":
                            for i in range(2):
                                P.dma(LF[:, 16 * i:16 * i + 16, :], din["cache_fox_logf"][0, i].rearrange("(t p) h -> p t h", p=128), key=("lfi", i))
                        for kt in range(NKT):
                            b = ps()
                            if kind == "s" and kt == 32:
                                o = banks[b][:64, :16]
                                P.mm(o, UBLK[:64, :64], LF[:64, kt, :], True, False)
                                P.mm(o, SEL[:, 1, 0:64], FALL[:, 15, :], False, False)
                                P.mm(o, SEL[:, 2, 0:64], FALL[:, 31, :], False, True)
                                rr = 64
                            else:
                                o = banks[b][:128, :16]
                                chain = (kt % 16 != 0) if kind == "s" else (kt != 0)
                                P.mm(o, UT[:, :], LF[:, kt, :], True, not chain)
                                if chain:
                                    P.mm(o, SEL[:, 0, :], FALL[:, kt - 1, :], False, True)
                                rr = 128
                            P.copy("dve", FALL[:rr, kt, :], o)
                            P.ts("dve", NEGF[:rr, kt, :], o, -1.0, ALU.mult)
                        for g in range(4):

                            def qk_norm(o, gain4, outf, extra_scale):
                                st = stt_slot()
                                sq = tf()
                                P.act(sq[:R, 0:256], o, AF.Square)
                                P.op("dve", lambda e: e.tensor_reduce(out=st[:R, 0:4], in_=sq[:R, 0:256].rearrange("p (h d) -> p h d", d=64),
                                                                     axis=AX.X, op=ALU.add),
                                     [sq[:R, 0:256]], [st[:R, 0:4]])
                                P.act(st[:R, 4:8], st[:R, 0:4], AF.Sqrt, scale=1.0 / 64, bias=EPS)
                                P.recip(st[:R, 4:8], st[:R, 4:8])
                                if extra_scale != 1.0:
                                    P.ts("dve", st[:R, 4:8], st[:R, 4:8], extra_scale, ALU.mult)
                                for a in range(4):
                                    P.stt(outf[:, a * 64:(a + 1) * 64], o[:, a * 64:(a + 1) * 64], st[:R, 4 + a:5 + a],
                                          gain4[:R, a * 64:(a + 1) * 64], ALU.mult, ALU.mult)

                            def q_evac(ti, o):
                                x = xs()
                                qk_norm(o, GQ4, x[:R, 0:256], 0.125)
                                return lambda: transpose_to(lambda cc: QT[:, cc, tcols(ti)], x, 2, ti)

                            wq = W.get("fox_w_in", (0,), 0, 1024, g * 256, g * 256 + 256)
                            lin_tm(xt_src, [wq], 256, q_evac)

                            def k_evac(ti, o):
                                kf = tf()
                                qk_norm(o, GK4, kf[:R, 0:256], 1.0)
                                kd_ = tokv((dout["fk_p"][0] if kind == "p" else dout["fk_s"][0]).rearrange("b t h d -> b t (h d)"))
                                P.dma(kd_[ti * 128:ti * 128 + R, g * 256:(g + 1) * 256], kf[:R, 0:256], key=("ko", tfi[0] % 3))
                                x = xs()
                                P.copy("pool", x[:R, 0:256], kf[:R, 0:256])
                                return lambda: transpose_to(lambda cc: KT[:, cc, key_new0 + ti * 128:key_new0 + ti * 128 + R], x, 2, ti)

                            wk = W.get("fox_w_in", (0,), 0, 1024, 1024 + g * 256, 1024 + g * 256 + 256)
                            lin_tm(xt_src, [wk], 256, k_evac)
                            P.memset("pool", VE[:, :, :, 64:65], 1.0)

                            def v_evac(ti, o):
                                vf = tf()
                                P.copy("act", vf[:R, 0:256], o)
                                vd = tokv((dout["fv_p"][0] if kind == "p" else dout["fv_s"][0]).rearrange("b t h d -> b t (h d)"))
                                P.dma(vd[ti * 128:ti * 128 + R, g * 256:(g + 1) * 256], vf[:R, 0:256], key=("vo", tfi[0] % 3))
                                P.copy("pool", VE[:R, lq + ti, :, 0:64], vf[:R, 0:256].rearrange("p (h d) -> p h d", d=64))

                            wv = W.get("fox_w_in", (0,), 0, 1024, 2048 + g * 256, 2048 + g * 256 + 256)
                            lin_tm(xt_src, [wv], 256, v_evac)
                            if kind == "s":
                                for i in range(2):
                                    for q in range(16):
                                        kt = 16 * i + q
                                        st_ = tf()
                                        P.dma(st_[:, 0:256], din["cache_fox_k"][0, i, q * 128:(q + 1) * 128, 4 * g:4 * g + 4, :].rearrange("p h d -> p (h d)"),
                                              key=("pin", tfi[0] % 3))
                                        P.dma(st_[:, 256:512], din["cache_fox_v"][0, i, q * 128:(q + 1) * 128, 4 * g:4 * g + 4, :].rearrange("p h d -> p (h d)"),
                                              key=("pin", tfi[0] % 3))
                                        x = xs()
                                        P.copy("pool", x[:, 0:256], st_[:, 0:256])
                                        P.copy("pool", VE[:, kt, :, 0:64], st_[:, 256:512].rearrange("p (h d) -> p h d", d=64))
                                        b = ps()
                                        pv = bankb[b]
                                        for c in range(2):
                                            P.tr(pv[:, c * 128:(c + 1) * 128], x[:, c * 128:(c + 1) * 128], identb[:, :])
                                        for c in range(2):
                                            P.copy("act", KT[:, c, kt * 128:(kt + 1) * 128], pv[:, c * 128:(c + 1) * 128])
                            wg_ = W.get("fox_w_in", (0,), 0, 1024, 3072 + g * 256, 3072 + g * 256 + 256)
                            lin_tm(xt_src, [wg_], 256, lambda ti, o: P.act(G[:R, ti, :], o, AF.Sigmoid))
                            for hh in range(4):
                                h = g * 4 + hh
                                pr = slice((hh % 2) * 64, (hh % 2) * 64 + 64)
                                mc = hh // 2

                                orow = slice(64, 128) if hh % 2 == 0 else slice(0, 64)
                                P.memset("pool", QZ[orow, :, :], 0.0)

                                def fq_build(qb):
                                    qz = QZ[:, qb % 2, :]
                                    wq_ = 512 if kind == "p" else 64
                                    P.copy("act", qz[pr, 0:wq_], QT[pr, mc, qb * 512:qb * 512 + wq_])
                                    fq = FQ[:, qb % 2, :]
                                    b = ps()
                                    for jq in range(len(blocks[0:1]) * (4 if kind == "p" else 1)):
                                        qt = (4 * qb + jq) if kind == "p" else 32
                                        bm = tf()
                                        P.ts("dve", bm[:R, 0:R], identf[:R, :R], FALL[:R, qt, h:h + 1], ALU.mult)
                                        P.mm(banks[b][:, jq * 128:jq * 128 + R], onesf[:R, :], bm[:R, 0:R], True, True)
                                    wdt = 512 if kind == "p" else 64
                                    P.copy("act", fq[:, 0:wdt], banks[b][:, 0:wdt])
                                    return fq

                                def st_mm(kt, krows, q0, n):
                                    b = ps()
                                    o = banks[b][:krows, :n]
                                    P.mm(o, KT[:, mc, kt * 128:kt * 128 + krows], QZ[:, (q0 // 512) % 2, (q0 % 512):(q0 % 512) + n], True, True)
                                    return o

                                def o_evac(ti, acc):
                                    st = stt_slot()
                                    rr = acc.shape[0]
                                    P.recip(st[:rr, 0:1], acc[:, 64:65])
                                    P.stt(OTOK[:rr, ti, hh * 64:(hh + 1) * 64], acc[:, 0:64], st[:rr, 0:1], G[:rr, ti, hh * 64:(hh + 1) * 64],
                                          ALU.mult, ALU.mult)

                                attn(h, st_mm, lambda kt, kr: VE[:kr, kt, hh, :], 64, NEGA[:, :],
                                     lambda kt, kr: NEGF[:kr, kt, h:h + 1], fq_build, 1.0, o_evac)
                            for ti in range(NT):
                                transpose_to(lambda cc: OT[:, cc, tcols(ti)], OTOK[:, ti, :], 2, ti)
                            wo = W.get("fox_w_out", (0,), g * 256, g * 256 + 256, 0, 1024)
                            for hf in range(2):
                                lin_tm(lambda i, ti: OT[:, i, tcols(ti)], [wo[:, :, hf * 512:(hf + 1) * 512]], 512,
                                       lambda ti, o, hf=hf: add_to_h(ti, hf * 512, 512, o))

                    for li in range(DBG_LAYERS):
                        kind_m = li % 3
                        if kind_m == 0:
                            norm_to_xt(din["norm_mix"][li])
                            retention(li // 3)
                        elif kind_m == 1:
                            mla(li)
                        else:
                            fox(li)
                        ffn(li)
                        pegate(li)
                    norm_stats(din["norm_final"])
                    yout = dout["y_prompt"][s] if kind == "p" else dout["y_sample"].rearrange("b t d -> (b t) d")
                    for ti in range(NT):
                        for hf in range(2):
                            t = tf()
                            P.stt(t[:R, :], H[:R, ti, hf * 512:(hf + 1) * 512], RS[:R, ti:ti + 1], GB[:R, hf * 512:(hf + 1) * 512], ALU.mult, ALU.mult)
                            P.dma(yout[ti * 128:ti * 128 + R, hf * 512:(hf + 1) * 512], t[:R, :], key=("yo", tfi[0] % 3))

                if kind == "p":
                    for s in range(int(os.environ.get("MK_PSEQ", "2"))):
                        run_pass(s)
                else:
                    run_pass(0)
                P.barrier()

        if "p" in DBG_PASSES:
            run_kind("p")
        if "s" in DBG_PASSES:
            run_kind("s")
        P.barrier()
        return W.rec, P.nops


_CACHE = {}


def _build():
    if "nc" in _CACHE:
        return _CACHE["nc"], _CACHE["cst"]
    CST = _consts()
    nc0 = bass.Bass("TRN2", target_bir_lowering=False)
    specs, _ = emit(nc0, True, None, CST)
    nc = bass.Bass("TRN2", target_bir_lowering=False)
    _, nops = emit(nc, False, specs, CST)
    _CACHE["nc"] = nc
    _CACHE["cst"] = CST
    _CACHE["nops"] = nops
    return nc, CST


def kernel(**inputs):
    nc, CST = _build()
    in_maps = []
    for c in range(8):
        m = {}
        for n in IN_SHAPES:
            a = np.asarray(inputs[n], dtype=np.float32)
            if n in SHARDED:
                ax = SHARDED[n]
                sl = [slice(None)] * a.ndim
                sl[ax] = slice(2 * c, 2 * c + 2)
                a = a[tuple(sl)]
            m[n] = np.ascontiguousarray(a)
        for n in CONST_NAMES:
            m["c_" + n] = np.ascontiguousarray(CST[n], dtype=np.float32)
        in_maps.append(m)
    res = run_bass_kernel_spmd(nc, in_maps, core_ids=list(range(8)))
    outs = []
    for n, shp, ax in OUT_SHAPES:
        outs.append(np.concatenate([np.asarray(res.results[c][n], dtype=np.float32) for c in range(8)], axis=ax))
    return tuple(outs)
```
